# Optimizing a Trainium2 kernel written in Bass

```python
import math
import jax, jax.numpy as jnp
from jax import lax
import numpy as np

D_MODEL = 1024
BATCH = 4
SEQ = 8192
DEPTH = 2

D_MIX = D_MODEL
N_MIXERS = 4
D_BRANCH = D_MIX // N_MIXERS
HEAD_DIM = 64
N_HEADS_BRANCH = D_BRANCH // HEAD_DIM

SSM_STATE = 64
SSM_GROUPS = 2
SSM_CONV = 4
SSM_CHUNK = 128
SSM_XBC = D_BRANCH + 2 * SSM_GROUPS * SSM_STATE
HGRN_CHUNK = 64
RET_CHUNK = 128
MOBA_BLOCK = 256
MOBA_TOPK = 3
MOBA_QBLOCK = 128

LN_EPS = 1e-5
RMS_EPS = 1e-6
DEEPNORM_ALPHA = (2.0 * DEPTH) ** 0.25
DEEPNORM_BETA = (8.0 * DEPTH) ** -0.25

SPLIT_SIZES = (D_BRANCH, SSM_XBC, N_HEADS_BRANCH,
               D_BRANCH, D_BRANCH, D_BRANCH, D_BRANCH,
               D_BRANCH, D_BRANCH, D_BRANCH, D_BRANCH,
               D_BRANCH, D_BRANCH, D_BRANCH, D_BRANCH)
N_IN = sum(SPLIT_SIZES)

kernel_name = "hymba_ssd_hgrn2_retnet_moba_deepnorm"

F32 = jnp.float32


def _layernorm(x, g, b):
    xf = x.astype(F32)
    mu = jnp.mean(xf, -1, keepdims=True)
    var = jnp.mean(jnp.square(xf - mu), -1, keepdims=True)
    return ((xf - mu) * lax.rsqrt(var + LN_EPS) * g + b).astype(x.dtype)


def _rms(x):
    xf = x.astype(F32)
    return xf * lax.rsqrt(jnp.mean(xf * xf, -1, keepdims=True) + RMS_EPS)


def _groupnorm(x):
    xf = x.astype(F32)
    mu = jnp.mean(xf, -1, keepdims=True)
    var = jnp.mean(jnp.square(xf - mu), -1, keepdims=True)
    return (xf - mu) * lax.rsqrt(var + LN_EPS)


def _segsum(a):
    T = a.shape[-1]
    rep = jnp.broadcast_to(a[..., :, None], a.shape + (T,))
    idx = jnp.arange(T)
    rep = jnp.where(idx[:, None] > idx[None, :], rep, 0.0)
    cs = jnp.cumsum(rep, axis=-2)
    return jnp.where(idx[:, None] >= idx[None, :], cs, -jnp.inf)


def _ssd(z, xbc_raw, dt_raw, conv_w, conv_b, dt_bias, a_log, d_skip, norm_w):
    Bsz, L, ch = xbc_raw.shape
    H, P, G, N, C = N_HEADS_BRANCH, HEAD_DIM, SSM_GROUPS, SSM_STATE, SSM_CHUNK
    nc = L // C
    xbc = lax.conv_general_dilated(xbc_raw, conv_w[:, None, :], window_strides=(1,),
                                   padding=((SSM_CONV - 1, 0),),
                                   dimension_numbers=('NWC', 'WIO', 'NWC'),
                                   feature_group_count=ch) + conv_b
    xbc = jax.nn.silu(xbc)
    xs, Bm, Cm = jnp.split(xbc, [D_BRANCH, D_BRANCH + G * N], axis=-1)
    xs = xs.reshape(Bsz, nc, C, H, P)
    Bm = jnp.repeat(Bm.reshape(Bsz, nc, C, G, N), H // G, axis=3)
    Cm = jnp.repeat(Cm.reshape(Bsz, nc, C, G, N), H // G, axis=3)
    dt = jax.nn.softplus(dt_raw.astype(F32) + dt_bias)
    A = -jnp.exp(a_log.astype(F32))
    dt4 = dt.reshape(Bsz, nc, C, H)
    a = (dt4 * A).transpose(0, 3, 1, 2)
    a_cum = jnp.cumsum(a, axis=-1)
    xdt = xs * dt4[..., None]
    Lmat = jnp.exp(_segsum(a))
    scores = jnp.einsum('bclhn,bcshn->bhcls', Cm, Bm) * Lmat
    y_diag = jnp.einsum('bhcls,bcshp->bclhp', scores, xdt)
    decay_states = jnp.exp(a_cum[..., -1:] - a_cum)
    states = jnp.einsum('bclhn,bhcl,bclhp->bchpn', Bm, decay_states, xdt)
    states = jnp.concatenate([jnp.zeros_like(states[:, :1]), states], axis=1)
    chunk_decay = jnp.exp(_segsum(jnp.pad(a_cum[..., -1], ((0, 0), (0, 0), (1, 0)))))
    new_states = jnp.einsum('bhzc,bchpn->bzhpn', chunk_decay, states)
    prev_states = new_states[:, :-1]
    y_off = jnp.einsum('bclhn,bchpn,bhcl->bclhp', Cm, prev_states, jnp.exp(a_cum))
    y = (y_diag + y_off + xs * d_skip[:, None]).reshape(Bsz, L, D_BRANCH)
    g = (y * jax.nn.silu(z.astype(F32))).reshape(Bsz, L, G, D_BRANCH // G)
    return _rms(g).reshape(Bsz, L, D_BRANCH) * norm_w


def _hgrn2(q, f_raw, i, gate, lb, norm_w):
    Bsz, L, _ = q.shape
    H, dk, C = N_HEADS_BRANCH, HEAD_DIM, HGRN_CHUNK
    nc = L // C
    lbf = lb.astype(F32)
    f = lbf + (1.0 - lbf) * jax.nn.sigmoid(f_raw.astype(F32))
    logf = jnp.log(f)
    k = 1.0 - f

    def chunks(t):
        return t.astype(F32).reshape(Bsz, nc, C, H, dk).transpose(1, 0, 3, 2, 4)

    qs, ks, vs, ls = chunks(q), chunks(k), chunks(i), chunks(logf)
    causal = jnp.tril(jnp.ones((C, C), dtype=bool))

    def step(S, inp):
        qc, kc, vc, lc = inp
        b = jnp.cumsum(lc, axis=2)
        o_inter = jnp.einsum('bhtd,bhdv->bhtv', qc * jnp.exp(b), S)
        diff = b[:, :, :, None, :] - b[:, :, None, :, :]
        diff = jnp.where(causal[None, None, :, :, None], diff, -jnp.inf)
        att = jnp.einsum('bhtd,bhsd,bhtsd->bhts', qc, kc, jnp.exp(diff))
        o_intra = jnp.einsum('bhts,bhsv->bhtv', att, vc)
        b_last = b[:, :, -1]
        S_new = jnp.exp(b_last)[..., None] * S + jnp.einsum(
            'bhsd,bhsv->bhdv', kc * jnp.exp(b_last[:, :, None] - b), vc)
        return S_new, o_inter + o_intra

    S0 = jnp.zeros((Bsz, H, dk, dk), F32)
    _, o = lax.scan(step, S0, (qs, ks, vs, ls))
    o = o.transpose(1, 0, 3, 2, 4).reshape(Bsz, L, H, dk)
    o = (_rms(o) * norm_w.reshape(H, dk)).reshape(Bsz, L, D_BRANCH)
    return o * jax.nn.silu(gate.astype(F32))


def _retention(q, k, v, gate):
    Bsz, L, _ = q.shape
    H, d, C = N_HEADS_BRANCH, HEAD_DIM, RET_CHUNK
    nc = L // C
    log_g = jnp.log(1.0 - 2.0 ** (-5.0 - jnp.arange(H, dtype=F32)))
    qh = q.astype(F32).reshape(Bsz, nc, C, H, d)
    kh = k.astype(F32).reshape(Bsz, nc, C, H, d) * (d ** -0.5)
    vh = v.astype(F32).reshape(Bsz, nc, C, H, d)
    pos = jnp.arange(C, dtype=F32)
    dist = pos[:, None] - pos[None, :]
    intra_decay = jnp.where(dist >= 0, jnp.exp(log_g[:, None, None] * jnp.maximum(dist, 0.0)), 0.0)
    scores = jnp.einsum('bclhd,bcshd->bchls', qh, kh) * intra_decay
    y_intra = jnp.einsum('bchls,bcshv->bclhv', scores, vh)
    k_decay = jnp.exp(log_g[:, None] * (C - 1.0 - pos)[None, :])
    U = jnp.einsum('bcshd,hs,bcshv->bchdv', kh, k_decay, vh)
    chunk_g = jnp.exp(log_g * C)[None, :, None, None]

    def step(R, u):
        return chunk_g * R + u, R

    _, R_prev = lax.scan(step, jnp.zeros((Bsz, H, d, d), F32), U.transpose(1, 0, 2, 3, 4))
    R_prev = R_prev.transpose(1, 0, 2, 3, 4)
    q_decay = jnp.exp(log_g[:, None] * (pos + 1.0)[None, :])
    y_inter = jnp.einsum('bclhd,hl,bchdv->bclhv', qh, q_decay, R_prev)
    y = _groupnorm((y_intra + y_inter).reshape(Bsz, L, H, d)).reshape(Bsz, L, D_BRANCH)
    return y * jax.nn.silu(gate.astype(F32))


def _moba(q, k, v, gate):
    Bsz, L, _ = q.shape
    H, d, S, Q = N_HEADS_BRANCH, HEAD_DIM, MOBA_BLOCK, MOBA_QBLOCK
    nb = -(-L // S)
    pad = nb * S - L

    def heads(t):
        t = jnp.pad(t, ((0, 0), (0, pad), (0, 0)))
        return t.reshape(Bsz, nb * S, H, d).transpose(0, 2, 1, 3)

    qh = heads(q) * (d ** -0.5)
    kh, vh = heads(k), heads(v)
    k_blocks = kh.reshape(Bsz, H, nb, S, d)
    v_blocks = vh.reshape(Bsz, H, nb, S, d)
    k_mean = jnp.mean(k_blocks, axis=3)
    n_sel = min(MOBA_TOPK, nb)
    slopes = 2.0 ** (-8.0 * jnp.arange(1, H + 1, dtype=F32) / H)
    bi = jnp.arange(Bsz)[:, None, None, None]
    hi = jnp.arange(H)[None, :, None, None]
    qb_per_kb = S // Q

    def one_block(qb):
        q0 = qb * Q
        own = qb // qb_per_kb
        qc = lax.dynamic_slice_in_dim(qh, q0, Q, axis=2)
        t_pos = q0 + jnp.arange(Q)
        gsc = jnp.einsum('bhqd,bhnd->bhqn', qc, k_mean).astype(F32)
        gsc = jnp.where(jnp.arange(nb) < own, gsc, -jnp.inf)
        _, sel = lax.top_k(gsc, n_sel)
        valid = jnp.arange(n_sel) < own
        ks = k_blocks[bi, hi, sel]
        vs = v_blocks[bi, hi, sel]
        s_sel = jnp.einsum('bhqd,bhqnsd->bhqns', qc, ks).astype(F32)
        s_pos = sel[..., None] * S + jnp.arange(S)
        rel_sel = (t_pos[:, None, None] - s_pos).astype(F32)
        s_sel = s_sel - slopes[:, None, None, None] * rel_sel
        s_sel = jnp.where(valid[:, None], s_sel, -jnp.inf)
        k_own = lax.dynamic_slice_in_dim(kh, own * S, S, axis=2)
        v_own = lax.dynamic_slice_in_dim(vh, own * S, S, axis=2)
        rel = (t_pos[:, None] - (own * S + jnp.arange(S))[None, :]).astype(F32)
        s_own = jnp.einsum('bhqd,bhsd->bhqs', qc, k_own).astype(F32)
        s_own = jnp.where(rel >= 0, s_own - slopes[:, None, None] * rel, -jnp.inf)
        scores = jnp.concatenate([s_sel.reshape(Bsz, H, Q, n_sel * S), s_own], axis=-1)
        p = jax.nn.softmax(scores, axis=-1)
        p_sel = p[..., :n_sel * S].reshape(Bsz, H, Q, n_sel, S)
        p_own = p[..., n_sel * S:]
        return (jnp.einsum('bhqns,bhqnsd->bhqd', p_sel, vs)
                + jnp.einsum('bhqs,bhsd->bhqd', p_own, v_own))

    outs = lax.map(one_block, jnp.arange(L // Q))
    o = outs.transpose(1, 0, 3, 2, 4).reshape(Bsz, L, D_BRANCH)
    return o * jax.nn.silu(gate.astype(F32))


def _layer(x, w_in, conv_w, conv_b, dt_bias, a_log, d_skip, ssd_norm_w, lb, hgrn_norm_w,
           w_out, ln_g, ln_b):
    proj = jnp.einsum('bld,dn->bln', x, w_in)
    idx = [int(v) for v in np.cumsum(SPLIT_SIZES)[:-1]]
    (s_z, s_xbc, s_dt, h_q, h_f, h_i, h_g, r_q, r_k, r_v, r_g,
     m_q, m_k, m_v, m_g) = jnp.split(proj, idx, axis=-1)
    y_ssd = _ssd(s_z, s_xbc, s_dt, conv_w, conv_b, dt_bias, a_log, d_skip, ssd_norm_w)
    y_hgrn = _hgrn2(h_q, h_f, h_i, h_g, lb, hgrn_norm_w)
    y_ret = _retention(r_q, r_k, r_v, r_g)
    y_moba = _moba(m_q, m_k, m_v, m_g)
    y_cat = jnp.concatenate([y_ssd, y_hgrn, y_ret, y_moba], axis=-1).astype(x.dtype)
    y = jnp.einsum('blm,md->bld', y_cat, w_out)
    return _layernorm(DEEPNORM_ALPHA * x + y, ln_g, ln_b)


def setup_inputs(seed: int = 0) -> dict:
    key = jax.random.key(seed)
    ks = jax.random.split(key, 16)
    offs = np.cumsum((0,) + SPLIT_SIZES)
    col_scale = np.ones((N_IN,), np.float32)
    col_scale[offs[1]:offs[1] + D_BRANCH] = DEEPNORM_BETA
    col_scale[offs[5]:offs[6]] = DEEPNORM_BETA
    col_scale[offs[9]:offs[10]] = DEEPNORM_BETA
    col_scale[offs[13]:offs[14]] = DEEPNORM_BETA
    w_in = (jax.random.normal(ks[3], (DEPTH, D_MODEL, N_IN), F32) * (D_MODEL ** -0.5)
            * jnp.asarray(col_scale))
    dt0 = jnp.exp(jax.random.uniform(ks[6], (DEPTH, N_HEADS_BRANCH), F32,
                                     math.log(1e-3), math.log(1e-1)))
    dt_bias = dt0 + jnp.log(-jnp.expm1(-dt0))
    a_log = jnp.log(jax.random.uniform(ks[7], (DEPTH, N_HEADS_BRANCH), F32, 1.0, 16.0))
    return {
        "x": jax.random.normal(ks[0], (BATCH, SEQ, D_MODEL), F32),
        "emb_ln_g": 1.0 + 0.02 * jax.random.normal(ks[1], (D_MODEL,), F32),
        "emb_ln_b": 0.02 * jax.random.normal(ks[2], (D_MODEL,), F32),
        "w_in": w_in,
        "ssd_conv_w": jax.random.normal(ks[4], (DEPTH, SSM_CONV, SSM_XBC), F32) * (SSM_CONV ** -0.5),
        "ssd_conv_b": 0.02 * jax.random.normal(ks[5], (DEPTH, SSM_XBC), F32),
        "ssd_dt_bias": dt_bias,
        "ssd_a_log": a_log,
        "ssd_d": 1.0 + 0.02 * jax.random.normal(ks[8], (DEPTH, N_HEADS_BRANCH), F32),
        "ssd_norm_w": 1.0 + 0.02 * jax.random.normal(ks[9], (DEPTH, D_BRANCH), F32),
        "hgrn_lb_logits": 0.1 * jax.random.normal(ks[10], (DEPTH, D_BRANCH), F32),
        "hgrn_norm_w": 1.0 + 0.02 * jax.random.normal(ks[11], (DEPTH, D_BRANCH), F32),
        "w_out": jax.random.normal(ks[12], (DEPTH, D_MIX, D_MODEL), F32) * (D_MIX ** -0.5) * DEEPNORM_BETA,
        "ln_g": 1.0 + 0.02 * jax.random.normal(ks[13], (DEPTH, D_MODEL), F32),
        "ln_b": 0.02 * jax.random.normal(ks[14], (DEPTH, D_MODEL), F32),
    }


def reference(x, emb_ln_g, emb_ln_b, w_in, ssd_conv_w, ssd_conv_b, ssd_dt_bias, ssd_a_log,
              ssd_d, ssd_norm_w, hgrn_lb_logits, hgrn_norm_w, w_out, ln_g, ln_b):
    lbs = jnp.cumsum(jax.nn.softmax(hgrn_lb_logits.astype(F32), axis=0), axis=0)
    lbs = lbs - lbs[0]
    h = _layernorm(x, emb_ln_g, emb_ln_b)
    for l in range(DEPTH):
        h = _layer(h, w_in[l], ssd_conv_w[l], ssd_conv_b[l], ssd_dt_bias[l], ssd_a_log[l],
                   ssd_d[l], ssd_norm_w[l], lbs[l], hgrn_norm_w[l], w_out[l], ln_g[l], ln_b[l])
    return h
```

```python
import contextlib
import math
import numpy as np
import ml_dtypes
import concourse.bass as bass
import concourse.mybir as mybir
from concourse.bass_utils import run_bass_kernel_spmd

F32 = mybir.dt.float32
BF16 = mybir.dt.bfloat16
I32 = mybir.dt.int32
F32R = mybir.dt.float32r
AF = mybir.ActivationFunctionType
ALU = mybir.AluOpType
AX = mybir.AxisListType

ENGS = ("pe", "act", "dve", "pool", "sp")

D_MODEL = 1024
DEPTH = 2
NF = 6 * 128
NR = 256
NT = 1026
LN_EPS = 1e-5
RMS_EPS = 1e-6
ALPHA = (2.0 * DEPTH) ** 0.25
BIG = 30000.0
STAGES = set('abcdefgh')
BLIM = 99
NEG = -1.0e30

P_LNG, P_LNB = 0, 1024
P_CWX, P_CWB, P_CBX, P_CBB = 2048, 2052, 2056, 2057
P_DTB, P_ALOG, P_DSK = 2058, 2060, 2062
P_SNW, P_HNW = 2064, 2192
P_LBL = 2320
P_RLGP, P_RLGB, P_SLOPE = 2322, 2323, 2325
NPRM = 2328


class Buf:
    __slots__ = ("name", "t", "writer", "readers", "lock")

    def __init__(self, name, t=None, lock=None):
        self.name = name
        self.t = t
        self.writer = None
        self.readers = []
        self.lock = lock

    def __getitem__(self, idx):
        return self.t[idx]


class Sched:
    def __init__(self, nc, stack):
        self.nc = nc
        self.stack = stack
        self.ops = {e: [] for e in ENGS}
        self.count = {e: 0 for e in ENGS}
        self.waited = {e: {} for e in ENGS}
        self.sems = {}
        self.dma_count = {}
        self.semstack = stack
        self.prefix = ""
        for e in ENGS:
            if e != "sp":
                self.sems[e] = stack.enter_context(nc.semaphore("s_" + e))

    def sb(self, name, shape, dtype=F32):
        name = self.prefix + name
        return Buf(name, self.stack.enter_context(self.nc.sbuf_tensor("sb_" + name, list(shape), dtype)))

    def ps(self, name, shape, dtype=F32):
        name = self.prefix + name
        return Buf(name, self.stack.enter_context(self.nc.psum_tensor("ps_" + name, list(shape), dtype)), lock=[None])

    def barrier(self):
        toks = [(e, self.count[e]) for e in ENGS if e != "sp" and self.count[e] > 0]
        toks += [(k_, v) for k_, v in self.dma_count.items() if v > 0]
        for e in ENGS:
            waits = []
            for (k_, v) in toks:
                if self.waited[e].get(k_, 0) < v:
                    self.waited[e][k_] = v
                    waits.append((k_, v))
            self.ops[e].append((waits, None, None))

    def _deps(self, eng, reads, writes):
        toks = []
        for b in list(reads) + list(writes):
            if b.lock is not None and b.lock[0] is not None and b.lock[0][0] != eng:
                toks.append(b.lock[0])
        for b in reads:
            if b.writer is not None:
                toks.append(b.writer)
        for b in writes:
            if b.writer is not None:
                toks.append(b.writer)
            toks.extend(b.readers)
        need = {}
        for (k, v) in toks:
            if v > need.get(k, 0):
                need[k] = v
        w = self.waited[eng]
        out = []
        for k, v in need.items():
            if w.get(k, 0) >= v:
                continue
            w[k] = v
            out.append((k, v))
        return out

    def _commit(self, tok, reads, writes):
        for b in list(reads) + list(writes):
            if b.lock is not None:
                b.lock[0] = tok
        for b in writes:
            b.writer = tok
            b.readers = []
        for b in reads:
            if b not in writes:
                b.readers.append(tok)
                if len(b.readers) > 48:
                    m = {}
                    for (k, v) in b.readers:
                        if v > m.get(k, 0):
                            m[k] = v
                    b.readers = list(m.items())

    def op(self, eng, fn, reads=(), writes=()):
        reads = [b for b in reads if b is not None]
        writes = [b for b in writes if b is not None]
        waits = self._deps(eng, reads, writes)
        self.count[eng] += 1
        tok = (eng, self.count[eng])
        self.ops[eng].append((waits, fn, (eng, 1)))
        self._commit(tok, reads, writes)
        return tok

    def dma(self, fn, reads=(), writes=(), key=None, n=1, queue="sp", inc=16):
        reads = [b for b in reads if b is not None]
        writes = [b for b in writes if b is not None]
        semkey = key if isinstance(key, str) else "d_" + key.name
        if semkey not in self.sems:
            self.sems[semkey] = self.semstack.enter_context(self.nc.semaphore(semkey))
            self.dma_count[semkey] = 0
        waits = self._deps(queue, reads, writes)
        self.dma_count[semkey] += inc * n
        tok = (semkey, self.dma_count[semkey])
        self.ops[queue].append((waits, fn, (semkey, 0)))
        self._commit(tok, reads, writes)
        return tok

    def final_wait(self, queue="sp"):
        waits = []
        for e in ENGS:
            if e != "sp" and self.count[e] > 0:
                waits.append((e, self.count[e]))
        for k, v in self.dma_count.items():
            if v > 0:
                waits.append((k, v))
        self.ops[queue].append((waits, None, None))

    def emit(self):
        nc = self.nc
        sems = self.sems
        engmap = {"pe": "tensor", "act": "scalar", "dve": "vector", "pool": "gpsimd", "sp": "sync"}
        with nc.Block() as block:
            for e in ENGS:
                oplist = self.ops[e]

                def body(engine, oplist=oplist):
                    for (waits, fn, inc) in oplist:
                        for (k, v) in waits:
                            engine.wait_ge(sems[k], v)
                        if fn is None:
                            continue
                        if inc[1] == 0:
                            fn(engine, sems[inc[0]])
                        else:
                            fn(engine).then_inc(sems[inc[0]], 1)
                getattr(block, engmap[e])(body)


class K:
    def __init__(self, S):
        self.S = S

    def mm(self, out, lhsT, rhs, reads, writes, start=True, stop=True):
        return self.S.op("pe", lambda e: e.matmul(out, lhsT=lhsT, rhs=rhs, start=start, stop=stop), reads, writes)

    def mms(self, lst, reads, writes):
        def fn(e):
            ins = None
            for (out, lhsT, rhs, start, stop) in lst:
                ins = e.matmul(out, lhsT=lhsT, rhs=rhs, start=start, stop=stop)
            return ins
        return self.S.op("pe", fn, reads, writes)

    def trs(self, lst, reads, writes):
        def fn(e):
            ins = None
            for (out, in_, ident) in lst:
                ins = e.transpose(out=out, in_=in_, identity=ident)
            return ins
        return self.S.op("pe", fn, reads, writes)

    def act(self, out, in_, func, reads, writes, bias=None, scale=None, accum=None):
        kw = {}
        if bias is not None:
            kw["bias"] = bias
        if scale is not None:
            kw["scale"] = scale
        if accum is not None:
            kw["accum_out"] = accum
        return self.S.op("act", lambda e: e.activation(out=out, in_=in_, func=func, **kw), reads, writes)

    def tt(self, eng, out, in0, in1, op, reads, writes):
        return self.S.op(eng, lambda e: e.tensor_tensor(out=out, in0=in0, in1=in1, op=op), reads, writes)

    def ts(self, eng, out, in0, s1, s2, op0, op1, reads, writes):
        if op1 is None:
            return self.S.op(eng, lambda e: e.tensor_scalar(out=out, in0=in0, scalar1=s1, scalar2=None, op0=op0), reads, writes)
        return self.S.op(eng, lambda e: e.tensor_scalar(out=out, in0=in0, scalar1=s1, scalar2=s2, op0=op0, op1=op1), reads, writes)

    def stt(self, out, in0, scalar, in1, op0, op1, reads, writes):
        return self.S.op("dve", lambda e: e.scalar_tensor_tensor(out=out, in0=in0, scalar=scalar, in1=in1, op0=op0, op1=op1), reads, writes)

    def cp(self, eng, out, in_, reads, writes):
        if eng == "act":
            return self.S.op("act", lambda e: e.activation(out=out, in_=in_, func=AF.Copy), reads, writes)
        return self.S.op(eng, lambda e: e.tensor_copy(out=out, in_=in_), reads, writes)

    def memset(self, eng, ap, val, writes):
        return self.S.op(eng, lambda e: e.memset(ap, val), (), writes)

    def asel(self, out, in_, pattern, cmp, fill, base, cm, reads, writes):
        return self.S.op("pool", lambda e: e.affine_select(out=out, in_=in_, pattern=pattern, compare_op=cmp, fill=fill,
                                                           base=base, channel_multiplier=cm), reads, writes)

    def iota(self, out, pattern, base, cm, writes):
        return self.S.op("pool", lambda e: e.iota(out, pattern=pattern, base=base, channel_multiplier=cm), (), writes)

    def load(self, dst_ap, src_ap, buf, reads=(), queue="sp", n=1):
        return self.S.dma(lambda e, s: e.dma_start(out=dst_ap, in_=src_ap).then_inc(s, 16), reads=reads, writes=[buf], key=buf, queue=queue, n=n)

    def store(self, dst_ap, src_ap, buf, dstbuf=None, queue="sp"):
        return self.S.dma(lambda e, s: e.dma_start(out=dst_ap, in_=src_ap).then_inc(s, 16), reads=[buf],
                          writes=[dstbuf] if dstbuf is not None else [], key=buf, queue=queue)


def build_consts(S, k):
    C = {}
    ones = S.sb("c_ones", [128, 512]); k.memset("pool", ones[:], 1.0, [ones])
    C["ones"] = ones
    identf = S.sb("c_identf", [128, 128])
    k.asel(identf[:], ones[:, 0:128], [[-1, 128]], ALU.is_equal, 0.0, 0, 1, [ones], [identf])
    identb = S.sb("c_identb", [128, 128], BF16)
    k.cp("dve", identb[:], identf[:], [identf], [identb])
    C["identf"], C["identb"] = identf, identb
    return C


def m_phase(nc, S, k, C, T, layer, src_d, prm_d, wf_d, wt_d, wr_d, ycat_d, src_is_x, after_store=None):
    NJ = T // 512
    NB = T // 256
    NKT = T // 128
    ones, identf, identb = C["ones"], C["identf"], C["identb"]

    prm = S.sb("prm", [128, NPRM])
    k.load(prm[:], prm_d, prm)
    wF = S.sb("wF", [128, 8, NF], BF16)
    wT = S.sb("wT", [128, 8, NT], BF16)
    k.load(wF[:], wf_d.rearrange("(c p) n -> p c n", p=128), wF, queue="pool")
    k.load(wT[:], wt_d.rearrange("(c p) n -> p c n", p=128), wT, queue="pool")
    wR = S.sb("wR", [128, 8, NR], F32R)

    def pcol(off, n=1):
        return prm[:, off:off + n]

    triU = S.sb("triU", [128, 128])
    k.asel(triU[:], ones[:, 0:128], [[1, 128]], ALU.is_ge, 0.0, 0, -1, [ones], [triU])
    triLs = S.sb("triLs", [128, 128])
    k.asel(triLs[:], ones[:, 0:128], [[-1, 128]], ALU.is_ge, 0.0, -1, 1, [ones], [triLs])
    maskneg = S.sb("maskneg", [128, 128])
    k.ts("pool", maskneg[:], triU[:], BIG, -BIG, ALU.mult, ALU.add, [triU], [maskneg])
    hmask = S.sb("hmask", [128, 2, 128])
    for h in range(2):
        k.cp("pool", hmask[:, h, :], triU[:], [triU], [hmask])
        k.memset("pool", hmask[0:64, h, 64:128], 0.0, [hmask])
    dli = S.sb("dli", [128, 128], I32)
    k.iota(dli[:], [[1, 128]], 0, -1, [dli])
    dlf = S.sb("dlf", [128, 128])
    k.cp("dve", dlf[:], dli[:], [dli], [dlf])
    k.ts("dve", dlf[:], dlf[:], 0.0, None, ALU.max, None, [dlf], [dlf])
    rdec = S.sb("rdec", [128, 2, 128])
    for h in range(2):
        k.act(rdec[:, h, :], dlf[:], AF.Exp, [dlf, prm], [rdec], scale=pcol(P_RLGB + h))
        k.tt("dve", rdec[:, h, :], rdec[:, h, :], triU[:], ALU.mult, [rdec, triU], [rdec])
    pidx_i = S.sb("pidx_i", [128, 8], I32)
    k.iota(pidx_i[:], [[128, 8]], 0, 1, [pidx_i])
    pidx = S.sb("pidx", [128, 8])
    k.cp("dve", pidx[:], pidx_i[:], [pidx_i], [pidx])
    rkdec = S.sb("rkdec", [128, 2])
    tmpc = S.sb("tmpc", [128, 8])
    k.ts("dve", tmpc[:, 0:1], pidx[:, 0:1], -1.0, 127.0, ALU.mult, ALU.add, [pidx], [tmpc])
    for h in range(2):
        k.act(rkdec[:, h:h + 1], tmpc[:, 0:1], AF.Exp, [tmpc, prm], [rkdec], scale=pcol(P_RLGB + h))
    k.ts("dve", rkdec[:], rkdec[:], 0.125, None, ALU.mult, None, [rkdec], [rkdec])
    rcg = S.sb("rcg", [128, 1])
    k.memset("dve", tmpc[:, 1:2], 128.0, [tmpc])
    k.act(rcg[:], tmpc[:, 1:2], AF.Exp, [tmpc, prm], [rcg], scale=pcol(P_RLGP))
    qdi = S.sb("qdi", [128, 4, 128], I32)
    k.iota(qdi[:], [[0, 4], [1, 128]], 1, 0, [qdi])
    qdec = S.sb("qdec", [128, 512])
    k.cp("dve", qdec[:], qdi[:].rearrange("p a b -> p (a b)"), [qdi], [qdec])
    k.act(qdec[:], qdec[:], AF.Exp, [qdec, prm], [qdec], scale=pcol(P_RLGP))
    rst = S.sb("rst", [128, 8, 64])
    k.memset("pool", rst[:], 1.0, [rst])
    k.memset("pool", rst[:, :, 0:1], 0.0, [rst])
    shsel_f = S.sb("shsel_f", [128, 64])
    k.asel(shsel_f[:], ones[:, 0:64], [[-1, 64]], ALU.is_equal, 0.0, -64, 1, [ones], [shsel_f])
    shsel = S.sb("shsel", [128, 64], BF16)
    k.cp("dve", shsel[:], shsel_f[:], [shsel_f], [shsel])
    Abc = S.sb("Abc", [128, 2])
    k.act(Abc[:], pcol(P_ALOG, 2), AF.Exp, [prm], [Abc])
    k.ts("dve", Abc[:], Abc[:], -1.0, None, ALU.mult, None, [Abc], [Abc])
    lb = S.sb("lb", [128, 1]); oml = S.sb("oml", [128, 1])
    if layer == 0:
        k.memset("dve", lb[:], 0.0, [lb])
    else:
        k.tt("dve", tmpc[:, 2:3], pcol(P_LBL), pcol(P_LBL + 1), ALU.subtract, [prm], [tmpc])
        k.act(tmpc[:, 2:3], tmpc[:, 2:3], AF.Exp, [tmpc], [tmpc])
        k.ts("dve", tmpc[:, 2:3], tmpc[:, 2:3], 1.0, None, ALU.add, None, [tmpc], [tmpc])
        S.op("dve", lambda e: e.reciprocal(out=lb[:], in_=tmpc[:, 2:3]), [tmpc], [lb])
    k.ts("dve", oml[:], lb[:], -1.0, 1.0, ALU.mult, ALU.add, [lb], [oml])
    bci = S.sb("bci", [128, 64], I32)
    k.iota(bci[:], [[128, 64]], -60 * 128, 1, [bci])
    bcol = S.sb("bcol", [128, 2, 64])
    for h in range(2):
        k.cp("dve", bcol[:, h, :], bci[:], [bci], [bcol])
        k.ts("dve", bcol[:, h, :], bcol[:, h, :], pcol(P_SLOPE + h), None, ALU.mult, None, [bcol, prm], [bcol])
    stl = S.sb("stl", [128, 2, 4])
    for h in range(2):
        k.ts("dve", stl[:, h, :], pidx[:, 0:4], pcol(P_SLOPE + h), None, ALU.mult, None, [pidx, prm], [stl])

    Kaug = [S.sb("Kaug%d" % h, [96, T], BF16) for h in range(2)]
    for h in range(2):
        k.memset("pool", Kaug[h][64:96, :], 1.0, [Kaug[h]])
        k.asel(Kaug[h][64:96, :], Kaug[h][64:96, :], [[1, T]], ALU.is_ge, 0.0, 0, -256, [Kaug[h]], [Kaug[h]])
        k.asel(Kaug[h][64:96, :], Kaug[h][64:96, :], [[-1, T]], ALU.is_ge, 0.0, 255, 256, [Kaug[h]], [Kaug[h]])
    Vaug = S.sb("Vaug", [128, NKT, 2, 65], BF16)
    k.memset("pool", Vaug[:], 1.0, [Vaug])
    ksum = [S.sb("ksum%d" % h, [64, 32]) for h in range(2)]
    kabs = [S.sb("kabs%d" % h, [64, 2]) for h in range(2)]
    kabsb = [S.sb("kabsb%d" % h, [64, 1], BF16) for h in range(2)]
    for h in range(2):
        k.memset("dve", ksum[h][:], 0.0, [ksum[h]])
        k.memset("dve", kabs[h][:], 0.0, [kabs[h]])
    gtok = S.sb("gtok", [128, 96], BF16)
    k.memset("dve", gtok[:], 0.0, [gtok])
    gsc = [S.sb("gsc%d" % h, [128, 32]) for h in range(2)]
    selb = [S.sb("selb%d" % h, [128, 32]) for h in range(2)]
    for h in range(2):
        k.memset("dve", gsc[h][:], NEG, [gsc[h]])
        k.memset("dve", selb[h][:], 0.0, [selb[h]])
    raw_xs = S.sb("raw_xs", [128, 515]); raw_bc = S.sb("raw_bc", [128, 515])
    k.memset("dve", raw_xs[:, 0:3], 0.0, [raw_xs]); k.memset("dve", raw_bc[:, 0:3], 0.0, [raw_bc])
    ST = S.sb("ssd_ST", [64, 128])
    STb = [S.sb("ssd_STb%d" % i, [64, 128], BF16) for i in range(2)]
    k.memset("dve", ST[:], 0.0, [ST]); k.memset("dve", STb[0][:], 0.0, [STb[0]])
    Rr = S.sb("ret_R", [128, 64])
    Rb = [S.sb("ret_Rb%d" % i, [128, 64], BF16) for i in range(2)]
    k.memset("dve", Rr[:], 0.0, [Rr]); k.memset("dve", Rb[0][:], 0.0, [Rb[0]])
    Hs = S.sb("hg_S", [128, 64])
    Hb = [S.sb("hg_Sb%d" % i, [128, 64], BF16) for i in range(3)]
    k.memset("dve", Hs[:], 0.0, [Hs]); k.memset("dve", Hb[0][:], 0.0, [Hb[0]])
    st_i = {"ssd": 0, "ret": 0, "hg": 0}

    hTr = S.sb("hTr", [128, 8, 512], F32R)
    xt = [S.sb("xt%d" % i, [128, 1024]) for i in range(2)]
    for hf in range(2):
        k.load(xt[hf][:].rearrange("p (c n) -> p c n", c=4), wr_d.rearrange("(c p) n -> p c n", p=128)[:, 4 * hf:4 * hf + 4, :], xt[hf])
        k.cp("dve", wR[:, 4 * hf:4 * hf + 4, :], xt[hf][:].rearrange("p (c n) -> p c n", c=4), [xt[hf]], [wR])
    mqk = S.sb("mqk", [128, NR])
    kpart = [S.sb("kpart%d" % h, [64, 8]) for h in range(2)]
    hT = S.sb("hT", [128, 8, 512], BF16)
    st12 = S.sb("st12", [128, 12]); mv = S.sb("mv", [128, 2]); rstd = S.sb("rstd", [128, 1])
    fA = S.sb("fA", [128, 512]); fB = S.sb("fB", [128, 512]); fC = S.sb("fC", [128, 512]); fD = S.sb("fD", [128, 512])
    hq = S.sb("hq", [128, 512])
    cA = fC; cB = fD
    xsT = S.sb("xsT", [128, 512], BF16); bcT = S.sb("bcT", [128, 512], BF16); cTb = S.sb("cTb", [64, 512], BF16)
    hqt = S.sb("hqt", [128, 512], BF16); hkt = S.sb("hkt", [128, 512], BF16); hkh = S.sb("hkh", [128, 512], BF16)
    hdec = S.sb("hdec", [128, 8])
    rqT = S.sb("rqT", [128, 512], BF16); rqTd = S.sb("rqTd", [128, 512], BF16); rkT = S.sb("rkT", [128, 512], BF16)
    mqf = [S.sb("mqf%d" % h, [64, 512]) for h in range(2)]
    mqa = [S.sb("mqa%d" % h, [64, 512], BF16) for h in range(2)]
    Qaug = [S.sb("Qaug%d" % h, [96, 512], BF16) for h in range(2)]
    Pb = [S.sb("Pb%d" % i, [128, 512], BF16) for i in range(2)]
    oTs = [S.sb("oTs%d" % h, [65, 512]) for h in range(2)]
    gates = [S.sb("gates%d" % i, [128, 512], BF16) for i in range(4)]
    gtmp = S.sb("gtmp", [128, 512])
    hi_t = [S.sb("hi%d" % i, [128, 128], BF16) for i in range(4)]
    rkd_t = [S.sb("rkd%d" % i, [128, 128], BF16) for i in range(4)]
    rv_t = [S.sb("rv%d" % i, [128, 128], BF16) for i in range(4)]
    dt_t = [S.sb("dt%d" % i, [128, 2]) for i in range(4)]
    a_t = [S.sb("a%d" % i, [128, 2]) for i in range(4)]
    ycat = [S.sb("ycat%d" % i, [128, 512], BF16) for i in range(4)]
    sm = S.sb("sm", [128, 16])
    xs_tok = S.sb("xs_tok", [128, 128], BF16); xdt = S.sb("xdt", [128, 128], BF16); xdd = S.sb("xdd", [128, 128], BF16)
    b_tok = S.sb("b_tok", [128, 64], BF16)
    eac = S.sb("eac", [128, 4]); ecd = S.sb("ecd", [64, 2])
    segl = S.sb("segl", [128, 2, 128]); LT = S.sb("LT", [128, 2, 128]); CBm = S.sb("CBm", [128, 128])
    MT = S.sb("MT", [128, 2, 128], BF16)
    t1 = S.sb("t1", [128, 128]); yv = S.sb("yv", [128, 128]); gy = S.sb("gy", [128, 128]); sq = t1
    rMT = S.sb("rMT", [128, 2, 128], BF16); hAT = S.sb("hAT", [128, 2, 128], BF16)
    khat_tok = S.sb("khat_tok", [128, 128], BF16)
    yn = S.sb("yn", [128, 128])
    m8 = S.sb("m8", [128, 8]); negm = S.sb("negm", [128, 2])

    pA = [S.ps("pA%d" % i, [128, 512]) for i in range(2)]
    pB = [S.ps("pB%d" % i, [128, 512]) for i in range(2)]
    pTb = S.ps("pT", [128, 8, 128], BF16)
    pD_t = S.ps("pD", [128, 512])
    pE_t = S.ps("pE", [128, 512])
    pF_t = S.ps("pF", [128, 512])
    pD_dt = Buf("pD_dt", pD_t[:, 0:8], lock=pD_t.lock); pD_cum = Buf("pD_cum", pD_t[:, 8:16], lock=pD_t.lock); pD_g = Buf("pD_g", pD_t[:, 16:64], lock=pD_t.lock)
    pD_fin = Buf("pD_fin", pD_t[:, 64:196], lock=pD_t.lock)
    pA1v = Buf("pA1v", pA[1][:, :], lock=pA[1].lock)
    pE_seg = Buf("pE_seg", pE_t[:, 0:256], lock=pE_t.lock); pE_sc = Buf("pE_sc", pE_t[:, 256:384], lock=pE_t.lock)
    pF_y = Buf("pF_y", pF_t[:, 0:128], lock=pF_t.lock); pF_yo = Buf("pF_yo", pF_t[:, 128:256], lock=pF_t.lock); pF_st = Buf("pF_st", pF_t[:, 256:384], lock=pF_t.lock)
    pF_u = Buf("pF_u", pF_t[:, 384:512], lock=pF_t.lock)

    src_b = Buf("src_d", src_d)
    ycat_b = Buf("ycat_d", None)

    def rstd_from(var_ap, eps, out_ap, rd, wr):
        k.ts("dve", out_ap, var_ap, eps, None, ALU.add, None, rd, wr)
        k.act(out_ap, out_ap, AF.Ln, wr, wr)
        k.act(out_ap, out_ap, AF.Exp, wr, wr, scale=-0.5)

    for J in range(NJ):
        if src_is_x:
            for tl in range(4):
                i = 4 * J + tl
                xb = xt[i % 2]
                k.load(xb[:], src_d[i * 128:(i + 1) * 128, :], xb, reads=[src_b])
                S.op("dve", lambda e, xb=xb: (e.bn_stats(out=st12[:, 0:6], in_=xb[:, 0:512]),
                                              e.bn_stats(out=st12[:, 6:12], in_=xb[:, 512:1024]))[1], [xb], [st12])
                S.op("dve", lambda e: e.bn_aggr(out=mv[:], in_=st12[:]), [st12], [mv])
                rstd_from(mv[:, 1:2], LN_EPS, rstd[:], [mv], [rstd])
                k.ts("dve", xb[:], xb[:], mv[:, 0:1], rstd[:], ALU.subtract, ALU.mult, [xb, mv, rstd], [xb])
                k.tt("pool", xb[:], xb[:], pcol(P_LNG, 1024), ALU.mult, [xb, prm], [xb])
                k.tt("pool", xb[:], xb[:], pcol(P_LNB, 1024), ALU.add, [xb, prm], [xb])
                for hf in range(2):
                    k.trs([(pA[hf][:, c * 128:(c + 1) * 128], xb[:, (4 * hf + c) * 128:(4 * hf + c + 1) * 128], identf[:])
                           for c in range(4)], [xb, identf], [pA[hf]])
                    k.cp("act", hT[:, 4 * hf:4 * hf + 4, tl * 128:(tl + 1) * 128],
                         pA[hf][:, :].rearrange("p (c t) -> p c t", c=4), [pA[hf]], [hT])
                    k.cp("dve", hTr[:, 4 * hf:4 * hf + 4, tl * 128:(tl + 1) * 128],
                         pA[hf][:, :].rearrange("p (c t) -> p c t", c=4), [pA[hf]], [hTr])
        else:
            for tl in range(4):
                i = 4 * J + tl
                xb = xt[i % 2]
                xv = xb[:].rearrange("p (c t) -> p c t", c=8)
                k.load(xv, src_d[:, :, i * 128:(i + 1) * 128], xb, reads=[src_b])
                k.cp("act", hT[:, :, tl * 128:(tl + 1) * 128], xv, [xb], [hT])
                k.cp("dve", hTr[:, :, tl * 128:(tl + 1) * 128], xv, [xb], [hTr])

        for _stage in ([0] if 'b' in STAGES else []):
            def fgroup(gi, off, M):
                p = pA[gi % 2]
                k.mms([(p[0:M, :], wF[:, c, off:off + M], hT[:, c, :], c == 0, c == 7) for c in range(8)], [wF, hT], [p])
                return p
            if BLIM > 0:
                p = fgroup(0, 0, 128); k.cp("act", raw_xs[:, 3:515], p[:, :], [p], [raw_xs])
            if BLIM > 1:
                p = fgroup(1, 128, 128); k.cp("act", raw_bc[:, 3:515], p[:, :], [p], [raw_bc])
            if BLIM > 2:
                p = fgroup(2, 256, 128); k.cp("act", hq[:], p[:, :], [p], [hq])
            if BLIM > 3:
                p = fgroup(3, 384, 128); k.act(fA[:], p[:, :], AF.Exp, [p], [fA], scale=-1.0)
            if BLIM > 4:
                p = fgroup(4, 512, 128)
                k.cp("act", rqT[:], p[:, :], [p], [rqT])
                k.tt("dve", rqTd[:], p[:, :], qdec[:], ALU.mult, [p, qdec], [rqTd])
            if BLIM > 5:
                p = fgroup(5, 640, 128); k.act(rkT[:], p[:, :], AF.Copy, [p], [rkT], scale=0.125)
            for tl in range(4 if BLIM > 6 else 0):
                ts_ = slice(tl * 128, (tl + 1) * 128)
                k.mms([(pB[0][:, 0:NR], hTr[:, c, ts_], wR[:, c, :], c == 0, c == 7) for c in range(8)],
                      [hTr, wR], [pB[0]])
                k.cp("act", mqk[:], pB[0][:, 0:NR], [pB[0]], [mqk])
                k.trs([(pB[1][0:64, j * 128:(j + 1) * 128], mqk[:, j * 64:(j + 1) * 64], identf[:]) for j in range(4)],
                      [mqk, identf], [pB[1]])
                for h in range(2):
                    k.act(mqf[h][:, ts_], pB[1][0:64, h * 128:(h + 1) * 128], AF.Copy, [pB[1]], [mqf[h]], scale=0.125)
                    kp = pB[1][0:64, (2 + h) * 128:(3 + h) * 128]
                    k.cp("act", Kaug[h][0:64, J * 512 + tl * 128:J * 512 + (tl + 1) * 128], kp, [pB[1]], [Kaug[h]])
                    S.op("dve", lambda e, kp=kp, h=h, tl=tl: e.tensor_reduce(out=kpart[h][:, tl:tl + 1], in_=kp, axis=AX.X, op=ALU.add),
                         [pB[1]], [kpart[h]])
                    S.op("dve", lambda e, kp=kp, h=h, tl=tl: e.tensor_reduce(out=kpart[h][:, 4 + tl:5 + tl], in_=kp, axis=AX.X, op=ALU.max,
                                                                             apply_absolute_value=True), [pB[1]], [kpart[h]])
            for h in range(2 if BLIM > 6 else 0):
                k.stt(mqa[h][:, :], mqf[h][:, :], -1.0, mqf[h][:, :], ALU.mult, ALU.max, [mqf[h]], [mqa[h]])
                for bb in range(2):
                    k.tt("dve", ksum[h][:, 2 * J + bb:2 * J + bb + 1], kpart[h][:, 2 * bb:2 * bb + 1], kpart[h][:, 2 * bb + 1:2 * bb + 2],
                         ALU.add, [kpart[h]], [ksum[h]])
                S.op("dve", lambda e, h=h: e.tensor_reduce(out=kabs[h][:, 1:2], in_=kpart[h][:, 4:8], axis=AX.X, op=ALU.max),
                     [kpart[h]], [kabs[h]])
                k.tt("dve", kabs[h][:, 0:1], kabs[h][:, 0:1], kabs[h][:, 1:2], ALU.max, [kabs[h]], [kabs[h]])
                k.cp("dve", kabsb[h][:, :], kabs[h][:, 0:1], [kabs[h]], [kabsb[h]])

        for _stage in ([0] if 'c' in STAGES else []):
            for tl in range(4):
                i = 4 * J + tl
                lst = []
                for c in range(8):
                    lst.append((pB[0][:, :], hT[:, c, tl * 128:(tl + 1) * 128], wT[:, c, 0:512], c == 0, c == 7))
                for c in range(8):
                    lst.append((pB[1][:, :], hT[:, c, tl * 128:(tl + 1) * 128], wT[:, c, 512:1024], c == 0, c == 7))
                for c in range(8):
                    lst.append((pD_dt[:, 0:2], hT[:, c, tl * 128:(tl + 1) * 128], wT[:, c, 1024:1026], c == 0, c == 7))
                k.mms(lst, [hT, wT], [pB[0], pB[1], pD_dt])
                k.act(gtmp[:], pB[0][:, :], AF.Exp, [pB[0]], [gtmp], scale=-1.0)
                k.act(gtmp[:], gtmp[:], AF.Ln, [gtmp], [gtmp], bias=1.0)
                k.act(gtmp[:], gtmp[:], AF.Exp, [gtmp], [gtmp], scale=-1.0)
                k.tt("dve", gates[tl][:], pB[0][:, :], gtmp[:], ALU.mult, [pB[0], gtmp], [gates[tl]])
                k.cp("act", hi_t[tl][:], pB[1][:, 0:128], [pB[1]], [hi_t[tl]])
                for h in range(2):
                    k.ts("dve", rkd_t[tl][:, h * 64:(h + 1) * 64], pB[1][:, 128 + h * 64:128 + (h + 1) * 64], rkdec[:, h:h + 1], None,
                         ALU.mult, None, [pB[1], rkdec], [rkd_t[tl]])
                k.cp("act", rv_t[tl][:], pB[1][:, 256:384], [pB[1]], [rv_t[tl]])
                k.cp("dve", Vaug[:, i, :, 0:64], pB[1][:, 384:512].rearrange("p (h d) -> p h d", h=2), [pB[1]], [Vaug])
                k.tt("dve", sm[:, 0:2], pD_dt[:, 0:2], pcol(P_DTB, 2), ALU.add, [pD_dt, prm], [sm])
                k.stt(sm[:, 2:4], sm[:, 0:2], -1.0, sm[:, 0:2], ALU.mult, ALU.max, [sm], [sm])
                k.act(sm[:, 2:4], sm[:, 2:4], AF.Exp, [sm], [sm], scale=-1.0)
                k.act(sm[:, 2:4], sm[:, 2:4], AF.Ln, [sm], [sm], bias=1.0)
                k.stt(dt_t[tl][:], sm[:, 0:2], 0.0, sm[:, 2:4], ALU.max, ALU.add, [sm], [dt_t[tl]])
                k.tt("dve", a_t[tl][:], dt_t[tl][:], Abc[:], ALU.mult, [dt_t[tl], Abc], [a_t[tl]])

        for _stage in ([0] if 'd' in STAGES else []):
            def conv_silu(raw, cw_off, cb_off, outT, outbuf):
                k.ts("dve", cA[:], raw[:, 0:512], pcol(cw_off), pcol(cb_off), ALU.mult, ALU.add, [raw, prm], [cA])
                for j in range(1, 4):
                    k.stt(cA[:], raw[:, j:j + 512], pcol(cw_off + j), cA[:], ALU.mult, ALU.add, [raw, prm, cA], [cA])
                k.cp("pool", raw[:, 0:3], raw[:, 512:515], [raw], [raw])
                k.act(cB[:], cA[:], AF.Exp, [cA], [cB], scale=-1.0)
                k.act(cB[:], cB[:], AF.Ln, [cB], [cB], bias=1.0)
                k.act(cB[:], cB[:], AF.Exp, [cB], [cB], scale=-1.0)
                k.tt("dve", outT[:], cA[:], cB[:], ALU.mult, [cA, cB], [outbuf])
            conv_silu(raw_xs, P_CWX, P_CBX, xsT, xsT)
            conv_silu(raw_bc, P_CWB, P_CBB, bcT, bcT)
            k.mm(pA[0][0:64, :], shsel[:], bcT[:], [shsel, bcT], [pA[0]])
            k.cp("act", cTb[:], pA[0][0:64, :], [pA[0]], [cTb])

            for tl in range(4):
                ts_ = slice(tl * 128, (tl + 1) * 128)
                k.trs([(pTb[:, 0, :], xsT[:, ts_], identb[:]), (pTb[:, 1, 0:64], bcT[0:64, ts_], identb[0:64, 0:64])],
                      [xsT, bcT, identb], [pTb])
                k.cp("act", xs_tok[:], pTb[:, 0, :], [pTb], [xs_tok])
                for h in range(2):
                    k.ts("dve", xdt[:, h * 64:(h + 1) * 64], pTb[:, 0, h * 64:(h + 1) * 64], dt_t[tl][:, h:h + 1], None, ALU.mult, None,
                         [pTb, dt_t[tl]], [xdt])
                k.cp("act", b_tok[:], pTb[:, 1, 0:64], [pTb], [b_tok])
                k.mms([(pD_cum[:, 0:2], triU[:], a_t[tl][:], True, True),
                       (pD_cum[:, 2:4], triLs[:], a_t[tl][:], True, True),
                       (pD_cum[0:64, 4:6], ones[:, 0:64], a_t[tl][:], True, True)], [triU, triLs, ones, a_t[tl]], [pD_cum])
                k.act(eac[:], pD_cum[:, 0:4], AF.Exp, [pD_cum], [eac])
                k.act(ecd[:], pD_cum[0:64, 4:6], AF.Exp, [pD_cum], [ecd])
                for h in range(2):
                    k.ts("dve", xdd[:, h * 64:(h + 1) * 64], xdt[:, h * 64:(h + 1) * 64], eac[:, 2 + h:3 + h], None, ALU.mult, None,
                         [xdt, eac], [xdd])
                for h in range(2):
                    k.ts("dve", segl[:, h, :], triLs[:], a_t[tl][:, h:h + 1], None, ALU.mult, None, [triLs, a_t[tl]], [segl])
                k.mms([(pE_seg[:, h * 128:(h + 1) * 128], segl[:, h, :], triU[:], True, True) for h in range(2)], [segl, triU], [pE_seg])
                k.act(LT[:].rearrange("p a b -> p (a b)"), pE_seg[:, :], AF.Exp, [pE_seg], [LT])
                k.mm(pE_sc[:, :], bcT[0:64, ts_], cTb[:, ts_], [bcT, cTb], [pE_sc])
                k.tt("dve", CBm[:], pE_sc[:, :], triU[:], ALU.mult, [pE_sc, triU], [CBm])
                for h in range(2):
                    k.tt("dve", MT[:, h, :], LT[:, h, :], CBm[:], ALU.mult, [LT, CBm], [MT])
                sb_old = STb[st_i["ssd"] % 2]; sb_new = STb[(st_i["ssd"] + 1) % 2]; st_i["ssd"] += 1
                k.mms([(pF_y[:, h * 64:(h + 1) * 64], MT[:, h, :], xdt[:, h * 64:(h + 1) * 64], True, True) for h in range(2)] +
                      [(pF_yo[:, h * 64:(h + 1) * 64], cTb[:, ts_], sb_old[:, h * 64:(h + 1) * 64], True, True) for h in range(2)] +
                      [(pF_st[0:64, h * 64:(h + 1) * 64], b_tok[:], xdd[:, h * 64:(h + 1) * 64], True, True) for h in range(2)],
                      [MT, xdt, cTb, sb_old, b_tok, xdd], [pF_y, pF_yo, pF_st])
                for h in range(2):
                    k.stt(ST[:, h * 64:(h + 1) * 64], ST[:, h * 64:(h + 1) * 64], ecd[:, h:h + 1], pF_st[0:64, h * 64:(h + 1) * 64],
                          ALU.mult, ALU.add, [ST, ecd, pF_st], [ST])
                k.cp("act", sb_new[:], ST[:], [ST], [sb_new])
                for h in range(2):
                    k.act(t1[:, h * 64:(h + 1) * 64], pF_yo[:, h * 64:(h + 1) * 64], AF.Identity, [pF_yo, eac], [t1], scale=eac[:, h:h + 1])
                    k.stt(t1[:, h * 64:(h + 1) * 64], xs_tok[:, h * 64:(h + 1) * 64], pcol(P_DSK + h), t1[:, h * 64:(h + 1) * 64],
                          ALU.mult, ALU.add, [xs_tok, prm, t1], [t1])
                k.tt("dve", yv[:], pF_y[:, :], t1[:], ALU.add, [pF_y, t1], [yv])
                k.tt("dve", gy[:], yv[:], gates[tl][:, 0:128], ALU.mult, [yv, gates[tl]], [gy])
                k.act(sq[:], gy[:], AF.Square, [gy], [sq, sm], accum=sm[:, 4:5])
                k.ts("dve", sm[:, 4:5], sm[:, 4:5], 1.0 / 128.0, RMS_EPS, ALU.mult, ALU.add, [sm], [sm])
                k.act(sm[:, 4:5], sm[:, 4:5], AF.Ln, [sm], [sm])
                k.act(sm[:, 4:5], sm[:, 4:5], AF.Exp, [sm], [sm], scale=-0.5)
                k.stt(ycat[tl][:, 0:128], gy[:], sm[:, 4:5], pcol(P_SNW, 128), ALU.mult, ALU.mult, [gy, sm, prm], [ycat[tl]])

        for _stage in ([0] if 'e' in STAGES else []):
            for tl in range(4):
                ts_ = slice(tl * 128, (tl + 1) * 128)
                k.mms([(pE_seg[:, 0:128], rkT[0:64, ts_], rqT[0:64, ts_], True, True),
                       (pA1v[:, 0:128], rkT[64:128, ts_], rqT[64:128, ts_], True, True)], [rkT, rqT], [pE_seg, pA1v])
                k.tt("dve", rMT[:, 0, :], pE_seg[:, 0:128], rdec[:, 0, :], ALU.mult, [pE_seg, rdec], [rMT])
                k.tt("dve", rMT[:, 1, :], pA1v[:, 0:128], rdec[:, 1, :], ALU.mult, [pA1v, rdec], [rMT])
                rb_old = Rb[st_i["ret"] % 2]; rb_new = Rb[(st_i["ret"] + 1) % 2]; st_i["ret"] += 1
                lst = []
                for h in range(2):
                    hs = slice(h * 64, (h + 1) * 64)
                    lst.append((pF_y[:, hs], rMT[:, h, :], rv_t[tl][:, hs], True, False))
                    lst.append((pF_y[:, hs], rqTd[hs, ts_], rb_old[hs, :], False, True))
                for h in range(2):
                    hs = slice(h * 64, (h + 1) * 64)
                    lst.append((pF_u[hs, 0:64], rkd_t[tl][:, hs], rv_t[tl][:, hs], True, True))
                k.mms(lst, [rMT, rv_t[tl], rqTd, rb_old, rkd_t[tl]], [pF_y, pF_u])
                k.stt(Rr[:], Rr[:], rcg[:], pF_u[:, 0:64], ALU.mult, ALU.add, [Rr, rcg, pF_u], [Rr])
                k.cp("act", rb_new[:], Rr[:], [Rr], [rb_new])
                for h in range(2):
                    hs = slice(h * 64, (h + 1) * 64)
                    S.op("dve", lambda e, hs=hs: e.bn_stats(out=st12[:, 0:6], in_=pF_y[:, hs]), [pF_y], [st12])
                    S.op("dve", lambda e: e.bn_aggr(out=mv[:], in_=st12[:, 0:6]), [st12], [mv])
                    rstd_from(mv[:, 1:2], LN_EPS, rstd[:], [mv], [rstd])
                    k.ts("dve", yn[:, hs], pF_y[:, hs], mv[:, 0:1], rstd[:], ALU.subtract, ALU.mult, [pF_y, mv, rstd], [yn])
                k.tt("dve", ycat[tl][:, 256:384], yn[:], gates[tl][:, 256:384], ALU.mult, [yn, gates[tl]], [ycat[tl]])

        for _stage in ([0] if 'f' in STAGES else []):
            k.act(fA[:], fA[:], AF.Ln, [fA], [fA], bias=1.0)
            k.act(fA[:], fA[:], AF.Exp, [fA], [fA], scale=-1.0)
            k.ts("dve", fA[:], fA[:], oml[:], lb[:], ALU.mult, ALU.add, [fA, oml, lb], [fA])
            k.act(fB[:], fA[:], AF.Ln, [fA], [fB])
            k.ts("dve", fC[:], fA[:], -1.0, 1.0, ALU.mult, ALU.add, [fA], [fC])
            S.op("dve", lambda e: e.tensor_tensor_scan(out=fD[:], data0=rst[:].rearrange("p a b -> p (a b)"), data1=fB[:], initial=0.0,
                                                       op0=ALU.mult, op1=ALU.add), [rst, fB], [fD])
            k.act(fA[:], fD[:], AF.Exp, [fD], [fA])
            k.cp("dve", hdec[:], fA[:].rearrange("p (a b) -> p a b", b=64)[:, :, 63], [fA], [hdec])
            k.tt("dve", hqt[:], hq[:], fA[:], ALU.mult, [hq, fA], [hqt])
            k.act(fB[:], fD[:], AF.Exp, [fD], [fB], scale=-1.0)
            k.tt("dve", hkt[:], fC[:], fB[:], ALU.mult, [fC, fB], [hkt])
            for c in range(8):
                k.act(fA[:, c * 64:(c + 1) * 64], fD[:, c * 64:(c + 1) * 64], AF.Exp, [fD], [fA], scale=-1.0,
                      bias=fD[:, c * 64 + 63:c * 64 + 64])
            k.tt("dve", hkh[:], fC[:], fA[:], ALU.mult, [fC, fA], [hkh])
            for tl in range(4):
                ts_ = slice(tl * 128, (tl + 1) * 128)
                k.trs([(pTb[:, 2, :], hkh[:, ts_], identb[:])], [hkh, identb], [pTb])
                k.cp("act", khat_tok[:], pTb[:, 2, :], [pTb], [khat_tok])
                k.mms([(pE_seg[:, 0:128], hkt[0:64, ts_], hqt[0:64, ts_], True, True),
                       (pA1v[:, 0:128], hkt[64:128, ts_], hqt[64:128, ts_], True, True)], [hkt, hqt], [pE_seg, pA1v])
                k.tt("dve", hAT[:, 0, :], pE_seg[:, 0:128], hmask[:, 0, :], ALU.mult, [pE_seg, hmask], [hAT])
                k.tt("dve", hAT[:, 1, :], pA1v[:, 0:128], hmask[:, 1, :], ALU.mult, [pA1v, hmask], [hAT])
                s0 = Hb[st_i["hg"] % 3]; s1 = Hb[(st_i["hg"] + 1) % 3]; s2 = Hb[(st_i["hg"] + 2) % 3]; st_i["hg"] += 2
                k.mms([(pF_u[h * 64:(h + 1) * 64, 0:64], khat_tok[0:64, h * 64:(h + 1) * 64], hi_t[tl][0:64, h * 64:(h + 1) * 64], True, True)
                       for h in range(2)] +
                      [(pA1v[h * 64:(h + 1) * 64, 128:192], khat_tok[64:128, h * 64:(h + 1) * 64], hi_t[tl][64:128, h * 64:(h + 1) * 64], True, True)
                       for h in range(2)], [khat_tok, hi_t[tl]], [pF_u, pA1v])
                k.stt(Hs[:], Hs[:], hdec[:, 2 * tl:2 * tl + 1], pF_u[:, 0:64], ALU.mult, ALU.add, [Hs, hdec, pF_u], [Hs])
                k.cp("act", s1[:], Hs[:], [Hs], [s1])
                k.stt(Hs[:], Hs[:], hdec[:, 2 * tl + 1:2 * tl + 2], pA1v[:, 128:192], ALU.mult, ALU.add, [Hs, hdec, pA1v], [Hs])
                k.cp("act", s2[:], Hs[:], [Hs], [s2])
                lst = []
                for h in range(2):
                    hs = slice(h * 64, (h + 1) * 64)
                    lst.append((pF_y[:, hs], hAT[:, h, :], hi_t[tl][:, hs], True, False))
                    lst.append((pF_y[0:64, hs], hqt[hs, tl * 128:tl * 128 + 64], s0[hs, :], False, False))
                    lst.append((pF_y[64:128, hs], hqt[hs, tl * 128 + 64:tl * 128 + 128], s1[hs, :], False, True))
                k.mms(lst, [hAT, hi_t[tl], hqt, s0, s1], [pF_y])
                for h in range(2):
                    hs = slice(h * 64, (h + 1) * 64)
                    k.act(sq[:, hs], pF_y[:, hs], AF.Square, [pF_y], [sq, sm], accum=sm[:, 6 + h:7 + h])
                k.ts("dve", sm[:, 6:8], sm[:, 6:8], 1.0 / 64.0, RMS_EPS, ALU.mult, ALU.add, [sm], [sm])
                k.act(sm[:, 6:8], sm[:, 6:8], AF.Ln, [sm], [sm])
                k.act(sm[:, 6:8], sm[:, 6:8], AF.Exp, [sm], [sm], scale=-0.5)
                for h in range(2):
                    hs = slice(h * 64, (h + 1) * 64)
                    k.stt(yn[:, hs], pF_y[:, hs], sm[:, 6 + h:7 + h], prm[:, P_HNW + h * 64:P_HNW + (h + 1) * 64], ALU.mult, ALU.mult,
                          [pF_y, sm, prm], [yn])
                k.tt("dve", ycat[tl][:, 128:256], yn[:], gates[tl][:, 128:256], ALU.mult, [yn, gates[tl]], [ycat[tl]])

        for _stage in ([0] if 'g' in STAGES else []):
            for h in range(2):
                k.cp("pool", Qaug[h][0:64, :], mqf[h][:, :], [mqf[h]], [Qaug[h]])
            for tl in range(4):
                own = 2 * J + tl // 2
                ts_ = slice(tl * 128, (tl + 1) * 128)
                for h in range(2):
                    lst = [(pD_g[:, 32:33], mqa[h][:, ts_], kabsb[h][:, :], True, True)]
                    if own > 0:
                        lst.append((pD_g[:, 0:own], mqf[h][:, ts_], ksum[h][:, 0:own], True, True))
                    k.mms(lst, [mqa[h], kabsb[h], mqf[h], ksum[h]], [pD_g])
                    k.ts("dve", negm[:, h:h + 1], pD_g[:, 32:33], stl[:, h, tl:tl + 1], -1.0, ALU.add, ALU.mult, [pD_g, stl], [negm])
                    k.ts("dve", negm[:, h:h + 1], negm[:, h:h + 1], BIG, None, ALU.add, None, [negm], [negm])
                    if own > 0:
                        k.cp("dve", gsc[h][:, 0:own], pD_g[:, 0:own], [pD_g], [gsc[h]])
                        S.op("dve", lambda e, h=h: e.max(out=m8[:], in_=gsc[h][:]), [gsc[h]], [m8])
                        k.ts("dve", selb[h][:, 0:own], gsc[h][:, 0:own], m8[:, 2:3], None, ALU.is_ge, None, [gsc[h], m8], [selb[h]])
                    k.memset("dve", selb[h][:, own:own + 1], 1.0, [selb[h]])
                    k.ts("dve", gtok[:, 64:96], selb[h][:], negm[:, h:h + 1], -BIG, ALU.mult, ALU.add, [selb[h], negm], [gtok])
                    k.trs([(pTb[0:96, 3, :], gtok[:], identb[:])], [gtok, identb], [pTb])
                    k.cp("act", Qaug[h][64:96, ts_], pTb[64:96, 3, :], [pTb], [Qaug[h]])
            pi = 0
            for h in range(2):
                nkt = 4 * J + 4
                for kt in range(nkt):
                    dl = kt - 4 * J
                    c0 = 0 if dl < 0 else 128 * dl
                    pq = pA[pi % 2]; Pt = Pb[pi % 2]; pi += 1
                    k.mm(pq[:, c0:512], Kaug[h][:, kt * 128:(kt + 1) * 128], Qaug[h][:, c0:512], [Kaug[h], Qaug[h]], [pq])
                    if dl >= 0:
                        k.tt("dve", pq[:, c0:c0 + 128], pq[:, c0:c0 + 128], maskneg[:], ALU.add, [pq, maskneg], [pq])
                    k.act(Pt[:, c0:512], pq[:, c0:512], AF.Exp, [pq, bcol], [Pt], bias=bcol[:, h, dl + 60:dl + 61])
                    k.mm(pB[h][0:65, c0:512], Vaug[:, kt, h, :], Pt[:, c0:512], [Vaug, Pt], [pB[h]], start=(kt == 0), stop=(kt == nkt - 1))
                k.cp("act", oTs[h][:], pB[h][0:65, :], [pB[h]], [oTs[h]])
            for tl in range(4):
                ts_ = slice(tl * 128, (tl + 1) * 128)
                k.trs([(pD_fin[:, h * 65:h * 65 + 65], oTs[h][0:65, ts_], identf[0:65, 0:65]) for h in range(2)],
                      [oTs[0], oTs[1], identf], [pD_fin])
                for h in range(2):
                    S.op("dve", lambda e, h=h: e.reciprocal(out=sm[:, 8 + h:9 + h], in_=pD_fin[:, h * 65 + 64:h * 65 + 65]), [pD_fin], [sm])
                for h in range(2):
                    hs = slice(h * 64, (h + 1) * 64)
                    k.ts("dve", yn[:, hs], pD_fin[:, h * 65:h * 65 + 64], sm[:, 8 + h:9 + h], None, ALU.mult, None, [pD_fin, sm], [yn])
                k.tt("dve", ycat[tl][:, 384:512], yn[:], gates[tl][:, 384:512], ALU.mult, [yn, gates[tl]], [ycat[tl]])

        for _stage in ([0] if 'h' in STAGES else []):
            for tl in range(4):
                i = 4 * J + tl
                if callable(ycat_d):
                    k.store(ycat_d(i)[0], ycat[tl][:], ycat[tl], dstbuf=ycat_d(i)[1])
                else:
                    k.store(ycat_d[i * 128:(i + 1) * 128, :], ycat[tl][:], ycat[tl], dstbuf=ycat_b)
            if after_store is not None:
                after_store(J)


def build_M(T, layer, src_is_x):
    nc = bass.Bass("TRN2", target_bir_lowering=False)
    if src_is_x:
        src = nc.dram_tensor("src", [T, D_MODEL], F32, kind="ExternalInput").ap()
    else:
        src = nc.dram_tensor("src", [128, 8, T], F32, kind="ExternalInput").ap()
    prm = nc.dram_tensor("prm", [128, NPRM], F32, kind="ExternalInput").ap()
    wf = nc.dram_tensor("wf", [D_MODEL, NF], F32, kind="ExternalInput").ap()
    wt = nc.dram_tensor("wt", [D_MODEL, NT], F32, kind="ExternalInput").ap()
    wr = nc.dram_tensor("wr", [D_MODEL, NR], F32, kind="ExternalInput").ap()
    ycat = nc.dram_tensor("ycat", [T, 512], BF16, kind="ExternalOutput").ap()
    with contextlib.ExitStack() as st:
        S = Sched(nc, st)
        k = K(S)
        C = build_consts(S, k)
        m_phase(nc, S, k, C, T, layer, src, prm, wf, wt, wr, ycat, src_is_x)
        S.final_wait()
        S.emit()
    return nc


OFFS = np.cumsum((0, 256, 512, 4, 256, 256, 256, 256, 256, 256, 256, 256, 256, 256, 256, 256))


def core_weights(w_in_l, g):
    o = OFFS
    hp = slice(g * 128, (g + 1) * 128)

    def seg(i, sl):
        return w_in_l[:, o[i] + sl.start:o[i] + sl.stop]
    xbc = w_in_l[:, o[1]:o[2]]
    xs = xbc[:, g * 128:(g + 1) * 128]
    Bm = xbc[:, 256 + g * 64:256 + (g + 1) * 64]
    Cm = xbc[:, 384 + g * 64:384 + (g + 1) * 64]
    mq = seg(11, hp); mk = seg(12, hp)
    wf = np.concatenate([xs, Bm, Cm, seg(3, hp), seg(4, hp), seg(7, hp), seg(8, hp)], axis=1)
    wr = np.concatenate([mq, mk], axis=1)
    dt = w_in_l[:, o[2] + 2 * g:o[2] + 2 * g + 2]
    wt = np.concatenate([seg(0, hp), seg(6, hp), seg(10, hp), seg(14, hp),
                         seg(5, hp), seg(8, hp), seg(9, hp), seg(13, hp), dt], axis=1)
    return np.ascontiguousarray(wf, np.float32), np.ascontiguousarray(wt, np.float32), np.ascontiguousarray(wr, np.float32)


def core_params(inp, l, g, ln_g, ln_b):
    prm = np.zeros((128, NPRM), np.float32)
    bc = lambda v: np.broadcast_to(np.asarray(v, np.float32)[None, :], (128, len(v)))
    prm[:, P_LNG:P_LNG + 1024] = bc(ln_g)
    prm[:, P_LNB:P_LNB + 1024] = bc(ln_b)
    cw = inp["ssd_conv_w"][l]
    cb = inp["ssd_conv_b"][l]
    ch_xs = np.arange(g * 128, (g + 1) * 128)
    ch_bc = np.concatenate([256 + g * 64 + np.arange(64), 384 + g * 64 + np.arange(64)])
    prm[:, P_CWX:P_CWX + 4] = cw[:, ch_xs].T
    prm[:, P_CWB:P_CWB + 4] = cw[:, ch_bc].T
    prm[:, P_CBX] = cb[ch_xs]
    prm[:, P_CBB] = cb[ch_bc]
    hh = slice(2 * g, 2 * g + 2)
    prm[:, P_DTB:P_DTB + 2] = bc(inp["ssd_dt_bias"][l][hh])
    prm[:, P_ALOG:P_ALOG + 2] = bc(inp["ssd_a_log"][l][hh])
    prm[:, P_DSK:P_DSK + 2] = bc(inp["ssd_d"][l][hh])
    prm[:, P_SNW:P_SNW + 128] = bc(inp["ssd_norm_w"][l][g * 128:(g + 1) * 128])
    prm[:, P_HNW:P_HNW + 128] = bc(inp["hgrn_norm_w"][l][g * 128:(g + 1) * 128])
    prm[:, P_LBL:P_LBL + 2] = inp["hgrn_lb_logits"][:, g * 128:(g + 1) * 128].T
    heads = np.arange(2 * g, 2 * g + 2, dtype=np.float64)
    logg = np.log(1.0 - 2.0 ** (-5.0 - heads)).astype(np.float32)
    prm[0:64, P_RLGP] = logg[0]
    prm[64:128, P_RLGP] = logg[1]
    prm[:, P_RLGB:P_RLGB + 2] = bc(logg)
    slopes = (2.0 ** (-8.0 * (heads + 1.0) / 4.0)).astype(np.float32)
    prm[:, P_SLOPE:P_SLOPE + 2] = bc(slopes)
    return prm


def o_phase(nc, S, k, C, NTOK, layer, res_d, y0_d, y1_d, prmo_d, wo_d, out_d, hT_d, res_is_x):
    ident = C["identb"]
    ntile = NTOK // 128
    po = S.sb("prmo", [128, 4096])
    k.load(po[:], prmo_d, po)
    wo = S.sb("wo", [128, 8, 1024], BF16)
    k.load(wo[:], wo_d.rearrange("(c p) n -> p c n", p=128), wo, queue="pool")
    yt = [S.sb("o_yt%d" % i, [128, 1024], BF16) for i in range(2)]
    yT = S.sb("o_yT", [128, 8, 128], BF16)
    rt = [S.sb("o_rt%d" % i, [128, 1024]) for i in range(2)]
    zt = S.sb("o_zt", [128, 1024])
    ot = [S.sb("o_ot%d" % i, [128, 1024]) for i in range(2)]
    oT = [S.sb("o_oT%d" % i, [128, 8, 128]) for i in range(2)]
    identf = C["identf"]
    st12 = S.sb("o_st12", [128, 12]); mv = S.sb("o_mv", [128, 2]); rstd = S.sb("o_rstd", [128, 1])
    pT = S.ps("o_pT", [128, 8, 128], BF16)
    pY = [S.ps("o_pY%d" % i, [128, 512]) for i in range(2)]
    res_b = Buf("res_d", res_d); y_b = Buf("y_d", None); out_b = Buf("out_d", out_d)

    def ln(src, dst, dst_buf, goff, boff, bf_out=None):
        S.op("dve", lambda e: (e.bn_stats(out=st12[:, 0:6], in_=src[:, 0:512]),
                               e.bn_stats(out=st12[:, 6:12], in_=src[:, 512:1024]))[1], [src], [st12])
        S.op("dve", lambda e: e.bn_aggr(out=mv[:], in_=st12[:]), [st12], [mv])
        k.ts("dve", rstd[:], mv[:, 1:2], LN_EPS, None, ALU.add, None, [mv], [rstd])
        k.act(rstd[:], rstd[:], AF.Ln, [rstd], [rstd])
        k.act(rstd[:], rstd[:], AF.Exp, [rstd], [rstd], scale=-0.5)
        k.ts("dve", src[:], src[:], mv[:, 0:1], rstd[:], ALU.subtract, ALU.mult, [src, mv, rstd], [src])
        k.tt("pool", src[:], src[:], po[:, goff:goff + 1024], ALU.mult, [src, po], [src])
        k.tt("pool", dst[:], src[:], po[:, boff:boff + 1024], ALU.add, [src, po], [dst_buf])

    for i in range(ntile):
        tsl = slice(i * 128, (i + 1) * 128)
        y = yt[i % 2]; r = rt[i % 2]; o = ot[i % 2]
        ya, yb_ = (y0_d(i) if callable(y0_d) else (y0_d[tsl, :], y1_d[tsl, :]))
        S.dma(lambda e, s, y=y, ya=ya, yb_=yb_: (e.dma_start(out=y[:, 0:512], in_=ya).then_inc(s, 16),
                                                 e.dma_start(out=y[:, 512:1024], in_=yb_).then_inc(s, 16)),
              reads=[y_b], writes=[y], key=y, n=2)
        k.load(r[:], res_d[tsl, :], r, reads=[res_b])
        k.trs([(pT[:, c, :], y[:, c * 128:(c + 1) * 128], ident[:]) for c in range(8)], [y, ident], [pT])
        k.cp("act", yT[:], pT[:], [pT], [yT])
        if res_is_x:
            ln(r, r, r, 0, 1024)
        for hf in range(2):
            k.mms([(pY[hf][:, :], yT[:, c, :], wo[:, c, hf * 512:(hf + 1) * 512], c == 0, c == 7) for c in range(8)], [yT, wo], [pY[hf]])
            k.stt(zt[:, hf * 512:(hf + 1) * 512], r[:, hf * 512:(hf + 1) * 512], ALPHA, pY[hf][:, :], ALU.mult, ALU.add, [r, pY[hf]], [zt])
        ln(zt, o, o, 2048, 3072)
        k.store(out_d[tsl, :], o[:], o, dstbuf=out_b)
        if hT_d is not None:
            t_ = oT[i % 2]
            for hf in range(2):
                k.trs([(pY[hf][:, c * 128:(c + 1) * 128], o[:, (4 * hf + c) * 128:(4 * hf + c + 1) * 128], identf[:]) for c in range(4)],
                      [o, identf], [pY[hf]])
                k.cp("act", t_[:, 4 * hf:4 * hf + 4, :], pY[hf][:, :].rearrange("p (c t) -> p c t", c=4), [pY[hf]], [t_])
            k.store(hT_d[:, :, tsl], t_[:], t_, dstbuf=out_b)


def build_O(NTOK, layer, res_is_x, want_hT):
    nc = bass.Bass("TRN2", target_bir_lowering=False)
    res = nc.dram_tensor("res", [NTOK, D_MODEL], F32, kind="ExternalInput").ap()
    y0 = nc.dram_tensor("y0", [NTOK, 512], BF16, kind="ExternalInput").ap()
    y1 = nc.dram_tensor("y1", [NTOK, 512], BF16, kind="ExternalInput").ap()
    prmo = nc.dram_tensor("prmo", [128, 4096], F32, kind="ExternalInput").ap()
    wo = nc.dram_tensor("wo", [D_MODEL, D_MODEL], F32, kind="ExternalInput").ap()
    out = nc.dram_tensor("out", [NTOK, D_MODEL], F32, kind="ExternalOutput").ap()
    hT = nc.dram_tensor("hT", [128, 8, NTOK], F32, kind="ExternalOutput").ap() if want_hT else None
    with contextlib.ExitStack() as st:
        S = Sched(nc, st)
        k = K(S)
        C = build_consts(S, k)
        o_phase(nc, S, k, C, NTOK, layer, res, y0, y1, prmo, wo, out, hT, res_is_x)
        S.final_wait()
        S.emit()
    return nc


def wout_perm():
    rows = []
    for g in range(2):
        for m in range(4):
            rows.append(np.arange(m * 256 + g * 128, m * 256 + (g + 1) * 128))
    return np.concatenate(rows)


def o_params(emb_g, emb_b, ln_g, ln_b):
    bc = lambda v: np.broadcast_to(np.asarray(v, np.float32)[None, :], (128, 1024))
    return np.ascontiguousarray(np.concatenate([bc(emb_g), bc(emb_b), bc(ln_g), bc(ln_b)], axis=1))


_NC_CACHE = {}
PAIRS = [[0, 1], [2, 3], [4, 5], [6, 7]]


def build_fused(T):
    nc = bass.Bass("TRN2", target_bir_lowering=False)
    src = nc.dram_tensor("src", [T, D_MODEL], F32, kind="ExternalInput").ap()
    ins = []
    for l in range(DEPTH):
        ins.append(dict(
            prm=nc.dram_tensor("prm%d" % l, [128, NPRM], F32, kind="ExternalInput").ap(),
            wf=nc.dram_tensor("wf%d" % l, [D_MODEL, NF], F32, kind="ExternalInput").ap(),
            wt=nc.dram_tensor("wt%d" % l, [D_MODEL, NT], F32, kind="ExternalInput").ap(),
            wr=nc.dram_tensor("wr%d" % l, [D_MODEL, NR], F32, kind="ExternalInput").ap(),
            prmo=nc.dram_tensor("prmo%d" % l, [128, 4096], F32, kind="ExternalInput").ap(),
            wo=nc.dram_tensor("wo%d" % l, [D_MODEL, D_MODEL], F32, kind="ExternalInput").ap()))
    out = nc.dram_tensor("out", [T, D_MODEL], F32, kind="ExternalOutput").ap()
    CH = 1024
    NCH = max(T // CH, 1)
    CH = T // NCH
    yc_mine = [[nc.dram_tensor("yc_mine%d_%d" % (l, c), [CH, 512], BF16) for c in range(NCH)] for l in range(DEPTH)]
    yc_pair = [[nc.dram_tensor("yc_pair%d_%d" % (l, c), [2 * CH, 512], BF16) for c in range(NCH)] for l in range(DEPTH)]
    TPC = CH // 128
    h1 = nc.dram_tensor("h1", [T, D_MODEL], F32)
    h1T = nc.dram_tensor("h1T", [128, 8, T], F32)
    with contextlib.ExitStack() as top:
        S = Sched(nc, top)
        k = K(S)
        for l in range(DEPTH):
            with contextlib.ExitStack() as st:
                S.stack = st
                S.prefix = "M%d_" % l
                C = build_consts(S, k)
                tile_bufs = [Buf("ycd%d_%d" % (l, i), None) for i in range(T // 128)]

                def ycat_dst(i, l=l, tile_bufs=tile_bufs):
                    return yc_mine[l][i // TPC].ap()[(i % TPC) * 128:(i % TPC + 1) * 128, :], tile_bufs[i]

                def after_store(J, l=l, tile_bufs=tile_bufs):
                    last = 4 * J + 3
                    if (last + 1) % TPC != 0:
                        return
                    c = last // TPC

                    def cc(e, s):
                        e.collective_compute("AllGather", ALU.bypass, replica_groups=PAIRS,
                                             ins=[yc_mine[l][c].ap()], outs=[yc_pair[l][c].ap()]).then_inc(s, 1)
                    S.dma(cc, reads=tile_bufs[c * TPC:(c + 1) * TPC], key="cc", queue="pool", inc=1)
                m_phase(nc, S, k, C, T, l, src if l == 0 else h1T.ap(), ins[l]["prm"], ins[l]["wf"], ins[l]["wt"], ins[l]["wr"],
                        ycat_dst, l == 0, after_store=after_store)
            S.barrier()
            with contextlib.ExitStack() as st:
                S.stack = st
                S.prefix = "O%d_" % l
                C = build_consts(S, k)
                def ysrc(i, l=l):
                    yp = yc_pair[l][i // TPC].ap()
                    r0 = (i % TPC) * 128
                    return yp[r0:r0 + 128, :], yp[CH + r0:CH + r0 + 128, :]
                o_phase(nc, S, k, C, T, l, src if l == 0 else h1.ap(), ysrc, None, ins[l]["prmo"], ins[l]["wo"],
                        h1.ap() if l < DEPTH - 1 else out, h1T.ap() if l < DEPTH - 1 else None, l == 0)
            S.barrier()
        S.final_wait()
        S.emit()
    return nc


def kernel(x, emb_ln_g, emb_ln_b, w_in, ssd_conv_w, ssd_conv_b, ssd_dt_bias, ssd_a_log, ssd_d, ssd_norm_w,
           hgrn_lb_logits, hgrn_norm_w, w_out, ln_g, ln_b):
    inp = dict(x=x, emb_ln_g=emb_ln_g, emb_ln_b=emb_ln_b, w_in=w_in, ssd_conv_w=ssd_conv_w, ssd_conv_b=ssd_conv_b,
               ssd_dt_bias=ssd_dt_bias, ssd_a_log=ssd_a_log, ssd_d=ssd_d, ssd_norm_w=ssd_norm_w,
               hgrn_lb_logits=hgrn_lb_logits, hgrn_norm_w=hgrn_norm_w, w_out=w_out, ln_g=ln_g, ln_b=ln_b)
    inp = {k_: np.asarray(v, np.float32) for k_, v in inp.items()}
    B, T, D = inp["x"].shape
    assert B == 4 and D == D_MODEL
    cores = list(range(8))
    perm = wout_perm()
    if ("F", T) not in _NC_CACHE:
        _NC_CACHE[("F", T)] = build_fused(T)
    nc = _NC_CACHE[("F", T)]
    shared = {}
    for l in range(DEPTH):
        shared["prmo%d" % l] = o_params(inp["emb_ln_g"], inp["emb_ln_b"], inp["ln_g"][l], inp["ln_b"][l])
        shared["wo%d" % l] = np.ascontiguousarray(inp["w_out"][l][perm])
    percore_g = []
    for g in range(2):
        d = {}
        for l in range(DEPTH):
            wf, wt, wr = core_weights(inp["w_in"][l], g)
            d["wf%d" % l], d["wt%d" % l], d["wr%d" % l] = wf, wt, wr
            d["prm%d" % l] = core_params(inp, l, g, inp["emb_ln_g"], inp["emb_ln_b"])
        percore_g.append(d)
    maps = []
    for c in cores:
        m = {"src": np.ascontiguousarray(inp["x"][c // 2])}
        m.update(shared)
        m.update(percore_g[c % 2])
        maps.append(m)
    r = run_bass_kernel_spmd(nc, maps, core_ids=cores)
    out = np.stack([r.results[2 * b]["out"] for b in range(B)])
    return np.ascontiguousarray(out.astype(np.float32))
```

```python
import contextlib
import threading
import math
import numpy as np
import ml_dtypes
import concourse.bass as bass
import concourse.mybir as mybir
from concourse.bass_utils import run_bass_kernel_spmd

F32 = mybir.dt.float32
BF16 = mybir.dt.bfloat16
I32 = mybir.dt.int32
F32R = mybir.dt.float32r
AF = mybir.ActivationFunctionType
ALU = mybir.AluOpType
AX = mybir.AxisListType

ENGS = ("pe", "act", "dve", "pool", "sp")

D_MODEL = 1024
DEPTH = 2
NF = 6 * 128
NR = 256
NT = 1026
LN_EPS = 1e-5
RMS_EPS = 1e-6
ALPHA = (2.0 * DEPTH) ** 0.25
BIG = 30000.0
STAGES = set('abcdefgh')
BLIM = 99
NEG = -1.0e30

P_LNG, P_LNB = 0, 1024
P_CWX, P_CWB, P_CBX, P_CBB = 2048, 2052, 2056, 2057
P_DTB, P_ALOG, P_DSK = 2058, 2060, 2062
P_SNW, P_HNW = 2064, 2192
P_LBL = 2320
P_RLGP, P_RLGB, P_SLOPE = 2322, 2323, 2325
NPRM = 2328


class Buf:
    __slots__ = ("name", "t", "writer", "readers", "lock")

    def __init__(self, name, t=None, lock=None):
        self.name = name
        self.t = t
        self.writer = None
        self.readers = []
        self.lock = lock

    def __getitem__(self, idx):
        return self.t[idx]


class Interleaver:
    def __init__(self):
        self.active = False
        self.tls = threading.local()

    def run(self, fns):
        fns = list(fns)
        if len(fns) == 0:
            return
        if len(fns) == 1:
            fns[0]()
            return
        n = len(fns)
        self.alive = [True] * n
        self.turn = 0
        self.cv = threading.Condition()
        self.exc = None
        self.active = True

        def worker(i):
            self.tls.idx = i
            with self.cv:
                while self.turn != i:
                    self.cv.wait()
            try:
                fns[i]()
            except BaseException as e:
                self.exc = e
            with self.cv:
                self.alive[i] = False
                self._advance(i)
                self.cv.notify_all()
            self.tls.idx = None
        ths = [threading.Thread(target=worker, args=(i,)) for i in range(n)]
        for t in ths:
            t.start()
        for t in ths:
            t.join()
        self.active = False
        if self.exc is not None:
            raise self.exc

    def _advance(self, i):
        n = len(self.alive)
        for d in range(1, n + 1):
            j = (i + d) % n
            if self.alive[j]:
                self.turn = j
                return
        self.turn = -1

    def switch(self):
        if not self.active:
            return
        i = getattr(self.tls, "idx", None)
        if i is None:
            return
        with self.cv:
            self._advance(i)
            if self.turn != i:
                self.cv.notify_all()
                while self.turn != i:
                    self.cv.wait()


class Sched:
    def __init__(self, nc, stack):
        self.il = Interleaver()
        self.nc = nc
        self.stack = stack
        self.ops = {e: [] for e in ENGS}
        self.count = {e: 0 for e in ENGS}
        self.waited = {e: {} for e in ENGS}
        self.sems = {}
        self.dma_count = {}
        self.semstack = stack
        self.prefix = ""
        for e in ENGS:
            if e != "sp":
                self.sems[e] = stack.enter_context(nc.semaphore("s_" + e))

    def sb(self, name, shape, dtype=F32):
        name = self.prefix + name
        return Buf(name, self.stack.enter_context(self.nc.sbuf_tensor("sb_" + name, list(shape), dtype)))

    def ps(self, name, shape, dtype=F32):
        name = self.prefix + name
        return Buf(name, self.stack.enter_context(self.nc.psum_tensor("ps_" + name, list(shape), dtype)), lock=[None])

    def barrier(self):
        toks = [(e, self.count[e]) for e in ENGS if e != "sp" and self.count[e] > 0]
        toks += [(k_, v) for k_, v in self.dma_count.items() if v > 0]
        for e in ENGS:
            waits = []
            for (k_, v) in toks:
                if self.waited[e].get(k_, 0) < v:
                    self.waited[e][k_] = v
                    waits.append((k_, v))
            self.ops[e].append((waits, None, None))

    def _deps(self, eng, reads, writes):
        toks = []
        for b in list(reads) + list(writes):
            if b.lock is not None and b.lock[0] is not None and b.lock[0][0] != eng:
                toks.append(b.lock[0])
        for b in reads:
            if b.writer is not None:
                toks.append(b.writer)
        for b in writes:
            if b.writer is not None:
                toks.append(b.writer)
            toks.extend(b.readers)
        need = {}
        for (k, v) in toks:
            if v > need.get(k, 0):
                need[k] = v
        w = self.waited[eng]
        out = []
        for k, v in need.items():
            if w.get(k, 0) >= v:
                continue
            w[k] = v
            out.append((k, v))
        return out

    def _commit(self, tok, reads, writes):
        for b in list(reads) + list(writes):
            if b.lock is not None:
                b.lock[0] = tok
        for b in writes:
            b.writer = tok
            b.readers = []
        for b in reads:
            if b not in writes:
                b.readers.append(tok)
                if len(b.readers) > 48:
                    m = {}
                    for (k, v) in b.readers:
                        if v > m.get(k, 0):
                            m[k] = v
                    b.readers = list(m.items())

    def op(self, eng, fn, reads=(), writes=()):
        reads = [b for b in reads if b is not None]
        writes = [b for b in writes if b is not None]
        waits = self._deps(eng, reads, writes)
        self.count[eng] += 1
        tok = (eng, self.count[eng])
        self.ops[eng].append((waits, fn, (eng, 1)))
        self._commit(tok, reads, writes)
        self.il.switch()
        return tok

    def dma(self, fn, reads=(), writes=(), key=None, n=1, queue="sp", inc=16):
        reads = [b for b in reads if b is not None]
        writes = [b for b in writes if b is not None]
        semkey = key if isinstance(key, str) else "d_" + key.name
        if semkey not in self.sems:
            self.sems[semkey] = self.semstack.enter_context(self.nc.semaphore(semkey))
            self.dma_count[semkey] = 0
        waits = self._deps(queue, reads, writes)
        self.dma_count[semkey] += inc * n
        tok = (semkey, self.dma_count[semkey])
        self.ops[queue].append((waits, fn, (semkey, 0)))
        self._commit(tok, reads, writes)
        return tok

    def final_wait(self, queue="sp"):
        waits = []
        for e in ENGS:
            if e != "sp" and self.count[e] > 0:
                waits.append((e, self.count[e]))
        for k, v in self.dma_count.items():
            if v > 0:
                waits.append((k, v))
        self.ops[queue].append((waits, None, None))

    def emit(self):
        nc = self.nc
        sems = self.sems
        engmap = {"pe": "tensor", "act": "scalar", "dve": "vector", "pool": "gpsimd", "sp": "sync"}
        with nc.Block() as block:
            for e in ENGS:
                oplist = self.ops[e]

                def body(engine, oplist=oplist):
                    for (waits, fn, inc) in oplist:
                        for (k, v) in waits:
                            engine.wait_ge(sems[k], v)
                        if fn is None:
                            continue
                        if inc[1] == 0:
                            fn(engine, sems[inc[0]])
                        else:
                            fn(engine).then_inc(sems[inc[0]], 1)
                getattr(block, engmap[e])(body)


class K:
    def __init__(self, S):
        self.S = S

    def mm(self, out, lhsT, rhs, reads, writes, start=True, stop=True):
        return self.S.op("pe", lambda e: e.matmul(out, lhsT=lhsT, rhs=rhs, start=start, stop=stop), reads, writes)

    def mms(self, lst, reads, writes):
        def fn(e):
            ins = None
            for (out, lhsT, rhs, start, stop) in lst:
                ins = e.matmul(out, lhsT=lhsT, rhs=rhs, start=start, stop=stop)
            return ins
        return self.S.op("pe", fn, reads, writes)

    def trs(self, lst, reads, writes):
        def fn(e):
            ins = None
            for (out, in_, ident) in lst:
                ins = e.transpose(out=out, in_=in_, identity=ident)
            return ins
        return self.S.op("pe", fn, reads, writes)

    def act(self, out, in_, func, reads, writes, bias=None, scale=None, accum=None):
        kw = {}
        if bias is not None:
            kw["bias"] = bias
        if scale is not None:
            kw["scale"] = scale
        if accum is not None:
            kw["accum_out"] = accum
        return self.S.op("act", lambda e: e.activation(out=out, in_=in_, func=func, **kw), reads, writes)

    def tt(self, eng, out, in0, in1, op, reads, writes):
        return self.S.op(eng, lambda e: e.tensor_tensor(out=out, in0=in0, in1=in1, op=op), reads, writes)

    def ts(self, eng, out, in0, s1, s2, op0, op1, reads, writes):
        if op1 is None:
            return self.S.op(eng, lambda e: e.tensor_scalar(out=out, in0=in0, scalar1=s1, scalar2=None, op0=op0), reads, writes)
        return self.S.op(eng, lambda e: e.tensor_scalar(out=out, in0=in0, scalar1=s1, scalar2=s2, op0=op0, op1=op1), reads, writes)

    def stt(self, out, in0, scalar, in1, op0, op1, reads, writes):
        return self.S.op("dve", lambda e: e.scalar_tensor_tensor(out=out, in0=in0, scalar=scalar, in1=in1, op0=op0, op1=op1), reads, writes)

    def cp(self, eng, out, in_, reads, writes):
        if eng == "act":
            return self.S.op("act", lambda e: e.activation(out=out, in_=in_, func=AF.Copy), reads, writes)
        return self.S.op(eng, lambda e: e.tensor_copy(out=out, in_=in_), reads, writes)

    def memset(self, eng, ap, val, writes):
        return self.S.op(eng, lambda e: e.memset(ap, val), (), writes)

    def asel(self, out, in_, pattern, cmp, fill, base, cm, reads, writes):
        return self.S.op("pool", lambda e: e.affine_select(out=out, in_=in_, pattern=pattern, compare_op=cmp, fill=fill,
                                                           base=base, channel_multiplier=cm), reads, writes)

    def iota(self, out, pattern, base, cm, writes):
        return self.S.op("pool", lambda e: e.iota(out, pattern=pattern, base=base, channel_multiplier=cm), (), writes)

    def load(self, dst_ap, src_ap, buf, reads=(), queue="sp", n=1):
        return self.S.dma(lambda e, s: e.dma_start(out=dst_ap, in_=src_ap).then_inc(s, 16), reads=reads, writes=[buf], key=buf, queue=queue, n=n)

    def store(self, dst_ap, src_ap, buf, dstbuf=None, queue="sp"):
        return self.S.dma(lambda e, s: e.dma_start(out=dst_ap, in_=src_ap).then_inc(s, 16), reads=[buf],
                          writes=[dstbuf] if dstbuf is not None else [], key=buf, queue=queue)


def build_consts(S, k):
    C = {}
    ones = S.sb("c_ones", [128, 128]); k.memset("pool", ones[:], 1.0, [ones])
    C["ones"] = ones
    identf = S.sb("c_identf", [128, 128])
    k.asel(identf[:], ones[:, 0:128], [[-1, 128]], ALU.is_equal, 0.0, 0, 1, [ones], [identf])
    identb = S.sb("c_identb", [128, 128], BF16)
    k.cp("dve", identb[:], identf[:], [identf], [identb])
    C["identf"], C["identb"] = identf, identb
    return C


def m_phase(nc, S, k, C, T, layer, src_d, prm_d, wf_d, wt_d, wr_d, ycat_d, src_is_x, after_store=None):
    NJ = T // 512
    NB = T // 256
    NKT = T // 128
    ones, identf, identb = C["ones"], C["identf"], C["identb"]

    prm = S.sb("prm", [128, NPRM])
    k.load(prm[:], prm_d, prm)
    wF = S.sb("wF", [128, 8, NF], BF16)
    wT = S.sb("wT", [128, 8, NT], BF16)
    k.load(wF[:], wf_d.rearrange("(c p) n -> p c n", p=128), wF, queue="pool")
    k.load(wT[:], wt_d.rearrange("(c p) n -> p c n", p=128), wT, queue="pool")
    wR = S.sb("wR", [128, 8, NR], F32R)

    def pcol(off, n=1):
        return prm[:, off:off + n]

    triU = S.sb("triU", [128, 128])
    k.asel(triU[:], ones[:, 0:128], [[1, 128]], ALU.is_ge, 0.0, 0, -1, [ones], [triU])
    triLs = S.sb("triLs", [128, 128])
    k.asel(triLs[:], ones[:, 0:128], [[-1, 128]], ALU.is_ge, 0.0, -1, 1, [ones], [triLs])
    maskneg = S.sb("maskneg", [128, 128])
    k.ts("pool", maskneg[:], triU[:], BIG, -BIG, ALU.mult, ALU.add, [triU], [maskneg])
    hmask = S.sb("hmask", [128, 2, 128])
    for h in range(2):
        k.cp("pool", hmask[:, h, :], triU[:], [triU], [hmask])
        k.memset("pool", hmask[0:64, h, 64:128], 0.0, [hmask])
    dli = S.sb("dli", [128, 128], I32)
    k.iota(dli[:], [[1, 128]], 0, -1, [dli])
    dlf = S.sb("dlf", [128, 128])
    k.cp("dve", dlf[:], dli[:], [dli], [dlf])
    k.ts("dve", dlf[:], dlf[:], 0.0, None, ALU.max, None, [dlf], [dlf])
    rdec = S.sb("rdec", [128, 2, 128])
    for h in range(2):
        k.act(rdec[:, h, :], dlf[:], AF.Exp, [dlf, prm], [rdec], scale=pcol(P_RLGB + h))
        k.tt("dve", rdec[:, h, :], rdec[:, h, :], triU[:], ALU.mult, [rdec, triU], [rdec])
    pidx_i = S.sb("pidx_i", [128, 8], I32)
    k.iota(pidx_i[:], [[128, 8]], 0, 1, [pidx_i])
    pidx = S.sb("pidx", [128, 8])
    k.cp("dve", pidx[:], pidx_i[:], [pidx_i], [pidx])
    rkdec = S.sb("rkdec", [128, 2])
    tmpc = S.sb("tmpc", [128, 8])
    k.ts("dve", tmpc[:, 0:1], pidx[:, 0:1], -1.0, 127.0, ALU.mult, ALU.add, [pidx], [tmpc])
    for h in range(2):
        k.act(rkdec[:, h:h + 1], tmpc[:, 0:1], AF.Exp, [tmpc, prm], [rkdec], scale=pcol(P_RLGB + h))
    k.ts("dve", rkdec[:], rkdec[:], 0.125, None, ALU.mult, None, [rkdec], [rkdec])
    rcg = S.sb("rcg", [128, 1])
    k.memset("dve", tmpc[:, 1:2], 128.0, [tmpc])
    k.act(rcg[:], tmpc[:, 1:2], AF.Exp, [tmpc, prm], [rcg], scale=pcol(P_RLGP))
    k.iota(dli[:], [[1, 128]], 1, 0, [dli])
    qdec = S.sb("qdec", [128, 512])
    for j in range(4):
        k.cp("dve", qdec[:, j * 128:(j + 1) * 128], dli[:], [dli], [qdec])
    k.act(qdec[:], qdec[:], AF.Exp, [qdec, prm], [qdec], scale=pcol(P_RLGP))
    rst = S.sb("rst", [128, 8, 64], BF16)
    k.memset("pool", rst[:], 1.0, [rst])
    k.memset("pool", rst[:, :, 0:1], 0.0, [rst])
    shsel_f = S.sb("shsel_f", [128, 64])
    k.asel(shsel_f[:], ones[:, 0:64], [[-1, 64]], ALU.is_equal, 0.0, -64, 1, [ones], [shsel_f])
    shsel = S.sb("shsel", [128, 64], BF16)
    k.cp("dve", shsel[:], shsel_f[:], [shsel_f], [shsel])
    Abc = S.sb("Abc", [128, 2])
    k.act(Abc[:], pcol(P_ALOG, 2), AF.Exp, [prm], [Abc])
    k.ts("dve", Abc[:], Abc[:], -1.0, None, ALU.mult, None, [Abc], [Abc])
    lb = S.sb("lb", [128, 1]); oml = S.sb("oml", [128, 1])
    if layer == 0:
        k.memset("dve", lb[:], 0.0, [lb])
    else:
        k.tt("dve", tmpc[:, 2:3], pcol(P_LBL), pcol(P_LBL + 1), ALU.subtract, [prm], [tmpc])
        k.act(tmpc[:, 2:3], tmpc[:, 2:3], AF.Exp, [tmpc], [tmpc])
        k.ts("dve", tmpc[:, 2:3], tmpc[:, 2:3], 1.0, None, ALU.add, None, [tmpc], [tmpc])
        S.op("dve", lambda e: e.reciprocal(out=lb[:], in_=tmpc[:, 2:3]), [tmpc], [lb])
    k.ts("dve", oml[:], lb[:], -1.0, 1.0, ALU.mult, ALU.add, [lb], [oml])
    bci = S.sb("bci", [128, 64], I32)
    k.iota(bci[:], [[128, 64]], -60 * 128, 1, [bci])
    bcol = S.sb("bcol", [128, 2, 64])
    for h in range(2):
        k.cp("dve", bcol[:, h, :], bci[:], [bci], [bcol])
        k.ts("dve", bcol[:, h, :], bcol[:, h, :], pcol(P_SLOPE + h), None, ALU.mult, None, [bcol, prm], [bcol])
    stl = S.sb("stl", [128, 2, 4])
    for h in range(2):
        k.ts("dve", stl[:, h, :], pidx[:, 0:4], pcol(P_SLOPE + h), None, ALU.mult, None, [pidx, prm], [stl])

    Kaug = [S.sb("Kaug%d" % h, [96, T], BF16) for h in range(2)]
    for h in range(2):
        k.memset("pool", Kaug[h][64:96, :], 1.0, [Kaug[h]])
        k.asel(Kaug[h][64:96, :], Kaug[h][64:96, :], [[1, T]], ALU.is_ge, 0.0, 0, -256, [Kaug[h]], [Kaug[h]])
        k.asel(Kaug[h][64:96, :], Kaug[h][64:96, :], [[-1, T]], ALU.is_ge, 0.0, 255, 256, [Kaug[h]], [Kaug[h]])
    Vaug = S.sb("Vaug", [128, NKT, 2, 65], BF16)
    k.memset("pool", Vaug[:], 1.0, [Vaug])
    ksum = [S.sb("ksum%d" % h, [64, 32]) for h in range(2)]
    kabs = [S.sb("kabs%d" % h, [64, 2]) for h in range(2)]
    kabsb = [S.sb("kabsb%d" % h, [64, 1], BF16) for h in range(2)]
    for h in range(2):
        k.memset("dve", ksum[h][:], 0.0, [ksum[h]])
        k.memset("dve", kabs[h][:], 0.0, [kabs[h]])
    gtok = S.sb("gtok", [128, 96], BF16)
    k.memset("dve", gtok[:], 0.0, [gtok])
    gsc = [S.sb("gsc%d" % h, [128, 32]) for h in range(2)]
    selb = [S.sb("selb%d" % h, [128, 32]) for h in range(2)]
    for h in range(2):
        k.memset("dve", gsc[h][:], NEG, [gsc[h]])
        k.memset("dve", selb[h][:], 0.0, [selb[h]])
    raw_xs = S.sb("raw_xs", [128, 515]); raw_bc = S.sb("raw_bc", [128, 515])
    k.memset("dve", raw_xs[:, 0:3], 0.0, [raw_xs]); k.memset("dve", raw_bc[:, 0:3], 0.0, [raw_bc])
    ST = S.sb("ssd_ST", [64, 128])
    STb = [S.sb("ssd_STb%d" % i, [64, 128], BF16) for i in range(2)]
    k.memset("dve", ST[:], 0.0, [ST]); k.memset("dve", STb[0][:], 0.0, [STb[0]])
    Rr = S.sb("ret_R", [128, 64])
    Rb = [S.sb("ret_Rb%d" % i, [128, 64], BF16) for i in range(2)]
    k.memset("dve", Rr[:], 0.0, [Rr]); k.memset("dve", Rb[0][:], 0.0, [Rb[0]])
    Hs = S.sb("hg_S", [128, 64])
    Hb = [S.sb("hg_Sb%d" % i, [128, 64], BF16) for i in range(3)]
    k.memset("dve", Hs[:], 0.0, [Hs]); k.memset("dve", Hb[0][:], 0.0, [Hb[0]])
    st_i = {"ssd": 0, "ret": 0, "hg": 0}

    hTr = S.sb("hTr", [128, 8, 512], F32R)
    xt = [S.sb("xt%d" % i, [128, 1024]) for i in range(2)]
    for hf in range(2):
        k.load(xt[hf][:].rearrange("p (c n) -> p c n", c=4), wr_d.rearrange("(c p) n -> p c n", p=128)[:, 4 * hf:4 * hf + 4, :], xt[hf])
        k.cp("dve", wR[:, 4 * hf:4 * hf + 4, :], xt[hf][:].rearrange("p (c n) -> p c n", c=4), [xt[hf]], [wR])
    mqk = S.sb("mqk", [128, NR])
    kpart = [S.sb("kpart%d" % h, [64, 8]) for h in range(2)]
    hT = S.sb("hT", [128, 8, 512], BF16)
    st12 = S.sb("st12", [128, 12]); mv = S.sb("mv", [128, 2]); rstd = S.sb("rstd", [128, 1])
    fA = S.sb("fA", [128, 512]); fB = S.sb("fB", [128, 512]); fC = S.sb("fC", [128, 512]); fD = S.sb("fD", [128, 512])
    hq = S.sb("hq", [128, 512])
    cA = S.sb("cA", [128, 512]); cB = S.sb("cB", [128, 512])
    xsT = S.sb("xsT", [128, 512], BF16); bcT = S.sb("bcT", [128, 512], BF16); cTb = S.sb("cTb", [64, 512], BF16)
    hqt = S.sb("hqt", [128, 512], BF16); hkt = S.sb("hkt", [128, 512], BF16); hkh = S.sb("hkh", [128, 512], BF16)
    hdec = S.sb("hdec", [128, 8])
    rqT = S.sb("rqT", [128, 512], BF16); rqTd = S.sb("rqTd", [128, 512], BF16); rkT = S.sb("rkT", [128, 512], BF16)
    mqf = [S.sb("mqf%d" % h, [64, 512]) for h in range(2)]
    mqa = [S.sb("mqa%d" % h, [64, 512], BF16) for h in range(2)]
    Qaug = [S.sb("Qaug%d" % h, [96, 512], BF16) for h in range(2)]
    Pb = [S.sb("Pb%d" % i, [128, 512], BF16) for i in range(3)]
    oTs = [S.sb("oTs%d" % h, [65, 512]) for h in range(2)]
    gates = [S.sb("gates%d" % i, [128, 512], BF16) for i in range(4)]
    gtmp = S.sb("gtmp", [128, 512])
    hi_t = [S.sb("hi%d" % i, [128, 128], BF16) for i in range(4)]
    rkd_t = [S.sb("rkd%d" % i, [128, 128], BF16) for i in range(4)]
    rv_t = [S.sb("rv%d" % i, [128, 128], BF16) for i in range(4)]
    dt_t = [S.sb("dt%d" % i, [128, 2]) for i in range(4)]
    a_t = [S.sb("a%d" % i, [128, 2]) for i in range(4)]
    ycat = [S.sb("ycat%d" % i, [128, 512], BF16) for i in range(4)]
    sm = S.sb("sm", [128, 16]); sm_d = S.sb("sm_d", [128, 16]); sm_f = S.sb("sm_f", [128, 16]); sm_g = S.sb("sm_g", [128, 16])
    st12r = S.sb("st12r", [128, 12]); mvr = S.sb("mvr", [128, 2]); rstdr = S.sb("rstdr", [128, 1])
    yn_e = S.sb("yn_e", [128, 128]); yn_f = S.sb("yn_f", [128, 128]); yn_g = S.sb("yn_g", [128, 128]); sq_f = S.sb("sq_f", [128, 128])
    xs_tok = S.sb("xs_tok", [128, 128], BF16); xdt = S.sb("xdt", [128, 128], BF16); xdd = S.sb("xdd", [128, 128], BF16)
    b_tok = S.sb("b_tok", [128, 64], BF16)
    eac = S.sb("eac", [128, 4]); ecd = S.sb("ecd", [64, 2])
    segl = S.sb("segl", [128, 2, 128]); LT = S.sb("LT", [128, 2, 128]); CBm = S.sb("CBm", [128, 128])
    MT = S.sb("MT", [128, 2, 128], BF16)
    t1 = S.sb("t1", [128, 128]); yv = S.sb("yv", [128, 128]); gy = S.sb("gy", [128, 128]); sq = t1
    rMT = S.sb("rMT", [128, 2, 128], BF16); hAT = S.sb("hAT", [128, 2, 128], BF16)
    khat_tok = S.sb("khat_tok", [128, 128], BF16)
    m8 = S.sb("m8", [128, 8]); negm = S.sb("negm", [128, 2])

    pA = [S.ps("pA%d" % i, [128, 512]) for i in range(2)]
    pB = [S.ps("pB%d" % i, [128, 512]) for i in range(2)]
    pTb = S.ps("pT", [128, 8, 128], BF16)
    pD_t = S.ps("pD", [128, 512])
    pE_t = S.ps("pE", [128, 512])
    pF_t = S.ps("pF", [128, 512])
    pD_dt = Buf("pD_dt", pD_t[:, 0:8], lock=pD_t.lock); pD_cum = Buf("pD_cum", pD_t[:, 8:16], lock=pD_t.lock); pD_g = Buf("pD_g", pD_t[:, 16:64], lock=pD_t.lock)
    pD_fin = Buf("pD_fin", pD_t[:, 64:196], lock=pD_t.lock)
    pA1v = Buf("pA1v", pA[1][:, :], lock=pA[1].lock)
    pB0_s = Buf("pB0_s", pB[0][:, 0:128], lock=pB[0].lock); pB0_y = Buf("pB0_y", pB[0][:, 128:256], lock=pB[0].lock)
    pB0_u = Buf("pB0_u", pB[0][:, 256:384], lock=pB[0].lock)
    pB1_s = Buf("pB1_s", pB[1][:, 0:128], lock=pB[1].lock); pB1_y = Buf("pB1_y", pB[1][:, 128:256], lock=pB[1].lock)
    pB1_u = Buf("pB1_u", pB[1][:, 256:384], lock=pB[1].lock)
    pE_w = Buf("pE_w", pE_t[:, :], lock=pE_t.lock); pF_w = Buf("pF_w", pF_t[:, :], lock=pF_t.lock)
    pE_seg = Buf("pE_seg", pE_t[:, 0:256], lock=pE_t.lock); pE_sc = Buf("pE_sc", pE_t[:, 256:384], lock=pE_t.lock)
    pF_y = Buf("pF_y", pF_t[:, 0:128], lock=pF_t.lock); pF_yo = Buf("pF_yo", pF_t[:, 128:256], lock=pF_t.lock); pF_st = Buf("pF_st", pF_t[:, 256:384], lock=pF_t.lock)
    pF_u = Buf("pF_u", pF_t[:, 384:512], lock=pF_t.lock)

    src_b = Buf("src_d", src_d)
    ycat_b = Buf("ycat_d", None)

    def rstd_from(var_ap, eps, out_ap, rd, wr):
        k.ts("dve", out_ap, var_ap, eps, None, ALU.add, None, rd, wr)
        k.act(out_ap, out_ap, AF.Ln, wr, wr)
        k.act(out_ap, out_ap, AF.Exp, wr, wr, scale=-0.5)

    for J in range(NJ):
        if src_is_x:
            for tl in range(4):
                i = 4 * J + tl
                xb = xt[i % 2]
                k.load(xb[:], src_d[i * 128:(i + 1) * 128, :], xb, reads=[src_b])
                S.op("dve", lambda e, xb=xb: (e.bn_stats(out=st12[:, 0:6], in_=xb[:, 0:512]),
                                              e.bn_stats(out=st12[:, 6:12], in_=xb[:, 512:1024]))[1], [xb], [st12])
                S.op("dve", lambda e: e.bn_aggr(out=mv[:], in_=st12[:]), [st12], [mv])
                rstd_from(mv[:, 1:2], LN_EPS, rstd[:], [mv], [rstd])
                k.ts("dve", xb[:], xb[:], mv[:, 0:1], rstd[:], ALU.subtract, ALU.mult, [xb, mv, rstd], [xb])
                k.tt("pool", xb[:], xb[:], pcol(P_LNG, 1024), ALU.mult, [xb, prm], [xb])
                k.tt("pool", xb[:], xb[:], pcol(P_LNB, 1024), ALU.add, [xb, prm], [xb])
                for hf in range(2):
                    k.trs([(pA[hf][:, c * 128:(c + 1) * 128], xb[:, (4 * hf + c) * 128:(4 * hf + c + 1) * 128], identf[:])
                           for c in range(4)], [xb, identf], [pA[hf]])
                    k.cp("act", hT[:, 4 * hf:4 * hf + 4, tl * 128:(tl + 1) * 128],
                         pA[hf][:, :].rearrange("p (c t) -> p c t", c=4), [pA[hf]], [hT])
                    k.cp("dve", hTr[:, 4 * hf:4 * hf + 4, tl * 128:(tl + 1) * 128],
                         pA[hf][:, :].rearrange("p (c t) -> p c t", c=4), [pA[hf]], [hTr])
        else:
            for tl in range(4):
                i = 4 * J + tl
                xb = xt[i % 2]
                xv = xb[:].rearrange("p (c t) -> p c t", c=8)
                k.load(xv, src_d[:, :, i * 128:(i + 1) * 128], xb, reads=[src_b])
                k.cp("act", hT[:, :, tl * 128:(tl + 1) * 128], xv, [xb], [hT])
                k.cp("dve", hTr[:, :, tl * 128:(tl + 1) * 128], xv, [xb], [hTr])

        for _stage in ([0] if 'b' in STAGES else []):
            def fgroup(gi, off, M):
                p = pA[gi % 2]
                k.mms([(p[0:M, :], wF[:, c, off:off + M], hT[:, c, :], c == 0, c == 7) for c in range(8)], [wF, hT], [p])
                return p
            if BLIM > 0:
                p = fgroup(0, 0, 128); k.cp("act", raw_xs[:, 3:515], p[:, :], [p], [raw_xs])
            if BLIM > 1:
                p = fgroup(1, 128, 128); k.cp("act", raw_bc[:, 3:515], p[:, :], [p], [raw_bc])
            if BLIM > 2:
                p = fgroup(2, 256, 128); k.cp("act", hq[:], p[:, :], [p], [hq])
            if BLIM > 3:
                p = fgroup(3, 384, 128); k.act(fA[:], p[:, :], AF.Exp, [p], [fA], scale=-1.0)
            if BLIM > 4:
                p = fgroup(4, 512, 128)
                k.cp("act", rqT[:], p[:, :], [p], [rqT])
                k.tt("dve", rqTd[:], p[:, :], qdec[:], ALU.mult, [p, qdec], [rqTd])
            if BLIM > 5:
                p = fgroup(5, 640, 128); k.act(rkT[:], p[:, :], AF.Copy, [p], [rkT], scale=0.125)
            for tl in range(4 if BLIM > 6 else 0):
                ts_ = slice(tl * 128, (tl + 1) * 128)
                k.mms([(pB[0][:, 0:NR], hTr[:, c, ts_], wR[:, c, :], c == 0, c == 7) for c in range(8)],
                      [hTr, wR], [pB[0]])
                k.cp("act", mqk[:], pB[0][:, 0:NR], [pB[0]], [mqk])
                k.trs([(pB[1][0:64, j * 128:(j + 1) * 128], mqk[:, j * 64:(j + 1) * 64], identf[:]) for j in range(4)],
                      [mqk, identf], [pB[1]])
                for h in range(2):
                    k.act(mqf[h][:, ts_], pB[1][0:64, h * 128:(h + 1) * 128], AF.Copy, [pB[1]], [mqf[h]], scale=0.125)
                    kp = pB[1][0:64, (2 + h) * 128:(3 + h) * 128]
                    k.cp("act", Kaug[h][0:64, J * 512 + tl * 128:J * 512 + (tl + 1) * 128], kp, [pB[1]], [Kaug[h]])
                    S.op("dve", lambda e, kp=kp, h=h, tl=tl: e.tensor_reduce(out=kpart[h][:, tl:tl + 1], in_=kp, axis=AX.X, op=ALU.add),
                         [pB[1]], [kpart[h]])
                    S.op("dve", lambda e, kp=kp, h=h, tl=tl: e.tensor_reduce(out=kpart[h][:, 4 + tl:5 + tl], in_=kp, axis=AX.X, op=ALU.max,
                                                                             apply_absolute_value=True), [pB[1]], [kpart[h]])
            for h in range(2 if BLIM > 6 else 0):
                k.stt(mqa[h][:, :], mqf[h][:, :], -1.0, mqf[h][:, :], ALU.mult, ALU.max, [mqf[h]], [mqa[h]])
                for bb in range(2):
                    k.tt("dve", ksum[h][:, 2 * J + bb:2 * J + bb + 1], kpart[h][:, 2 * bb:2 * bb + 1], kpart[h][:, 2 * bb + 1:2 * bb + 2],
                         ALU.add, [kpart[h]], [ksum[h]])
                S.op("dve", lambda e, h=h: e.tensor_reduce(out=kabs[h][:, 1:2], in_=kpart[h][:, 4:8], axis=AX.X, op=ALU.max),
                     [kpart[h]], [kabs[h]])
                k.tt("dve", kabs[h][:, 0:1], kabs[h][:, 0:1], kabs[h][:, 1:2], ALU.max, [kabs[h]], [kabs[h]])
                k.cp("dve", kabsb[h][:, :], kabs[h][:, 0:1], [kabs[h]], [kabsb[h]])

        for _stage in ([0] if 'c' in STAGES else []):
            for tl in range(4):
                i = 4 * J + tl
                lst = []
                pG_, pV_ = (pB[0], pB[1]) if tl % 2 == 0 else (pE_w, pF_w)
                for c in range(8):
                    lst.append((pG_[:, :], hT[:, c, tl * 128:(tl + 1) * 128], wT[:, c, 0:512], c == 0, c == 7))
                for c in range(8):
                    lst.append((pV_[:, :], hT[:, c, tl * 128:(tl + 1) * 128], wT[:, c, 512:1024], c == 0, c == 7))
                for c in range(8):
                    lst.append((pD_dt[:, 0:2], hT[:, c, tl * 128:(tl + 1) * 128], wT[:, c, 1024:1026], c == 0, c == 7))
                k.mms(lst, [hT, wT], [pG_, pV_, pD_dt])
                k.act(gtmp[:], pG_[:, :], AF.Exp, [pG_], [gtmp], scale=-1.0)
                k.act(gtmp[:], gtmp[:], AF.Ln, [gtmp], [gtmp], bias=1.0)
                k.act(gtmp[:], gtmp[:], AF.Exp, [gtmp], [gtmp], scale=-1.0)
                k.tt("dve", gates[tl][:], pG_[:, :], gtmp[:], ALU.mult, [pG_, gtmp], [gates[tl]])
                k.cp("act", hi_t[tl][:], pV_[:, 0:128], [pV_], [hi_t[tl]])
                for h in range(2):
                    k.ts("dve", rkd_t[tl][:, h * 64:(h + 1) * 64], pV_[:, 128 + h * 64:128 + (h + 1) * 64], rkdec[:, h:h + 1], None,
                         ALU.mult, None, [pV_, rkdec], [rkd_t[tl]])
                k.cp("act", rv_t[tl][:], pV_[:, 256:384], [pV_], [rv_t[tl]])
                k.cp("dve", Vaug[:, i, :, 0:64], pV_[:, 384:512].rearrange("p (h d) -> p h d", h=2), [pV_], [Vaug])
                k.tt("dve", sm[:, 0:2], pD_dt[:, 0:2], pcol(P_DTB, 2), ALU.add, [pD_dt, prm], [sm])
                k.stt(sm[:, 2:4], sm[:, 0:2], -1.0, sm[:, 0:2], ALU.mult, ALU.max, [sm], [sm])
                k.act(sm[:, 2:4], sm[:, 2:4], AF.Exp, [sm], [sm], scale=-1.0)
                k.act(sm[:, 2:4], sm[:, 2:4], AF.Ln, [sm], [sm], bias=1.0)
                k.stt(dt_t[tl][:], sm[:, 0:2], 0.0, sm[:, 2:4], ALU.max, ALU.add, [sm], [dt_t[tl]])
                k.tt("dve", a_t[tl][:], dt_t[tl][:], Abc[:], ALU.mult, [dt_t[tl], Abc], [a_t[tl]])

        def chain_d():
            def conv_silu(raw, cw_off, cb_off, outT, outbuf):
                k.ts("dve", cA[:], raw[:, 0:512], pcol(cw_off), pcol(cb_off), ALU.mult, ALU.add, [raw, prm], [cA])
                for j in range(1, 4):
                    k.stt(cA[:], raw[:, j:j + 512], pcol(cw_off + j), cA[:], ALU.mult, ALU.add, [raw, prm, cA], [cA])
                k.cp("pool", raw[:, 0:3], raw[:, 512:515], [raw], [raw])
                k.act(cB[:], cA[:], AF.Exp, [cA], [cB], scale=-1.0)
                k.act(cB[:], cB[:], AF.Ln, [cB], [cB], bias=1.0)
                k.act(cB[:], cB[:], AF.Exp, [cB], [cB], scale=-1.0)
                k.tt("dve", outT[:], cA[:], cB[:], ALU.mult, [cA, cB], [outbuf])
            conv_silu(raw_xs, P_CWX, P_CBX, xsT, xsT)
            conv_silu(raw_bc, P_CWB, P_CBB, bcT, bcT)
            k.mm(pA[0][0:64, :], shsel[:], bcT[:], [shsel, bcT], [pA[0]])
            k.cp("act", cTb[:], pA[0][0:64, :], [pA[0]], [cTb])

            for tl in range(4):
                ts_ = slice(tl * 128, (tl + 1) * 128)
                k.trs([(pTb[:, 0, :], xsT[:, ts_], identb[:]), (pTb[:, 1, 0:64], bcT[0:64, ts_], identb[0:64, 0:64])],
                      [xsT, bcT, identb], [pTb])
                k.cp("act", xs_tok[:], pTb[:, 0, :], [pTb], [xs_tok])
                for h in range(2):
                    k.ts("dve", xdt[:, h * 64:(h + 1) * 64], pTb[:, 0, h * 64:(h + 1) * 64], dt_t[tl][:, h:h + 1], None, ALU.mult, None,
                         [pTb, dt_t[tl]], [xdt])
                k.cp("act", b_tok[:], pTb[:, 1, 0:64], [pTb], [b_tok])
                k.mms([(pD_cum[:, 0:2], triU[:], a_t[tl][:], True, True),
                       (pD_cum[:, 2:4], triLs[:], a_t[tl][:], True, True),
                       (pD_cum[0:64, 4:6], ones[:, 0:64], a_t[tl][:], True, True)], [triU, triLs, ones, a_t[tl]], [pD_cum])
                k.act(eac[:], pD_cum[:, 0:4], AF.Exp, [pD_cum], [eac])
                k.act(ecd[:], pD_cum[0:64, 4:6], AF.Exp, [pD_cum], [ecd])
                for h in range(2):
                    k.ts("dve", xdd[:, h * 64:(h + 1) * 64], xdt[:, h * 64:(h + 1) * 64], eac[:, 2 + h:3 + h], None, ALU.mult, None,
                         [xdt, eac], [xdd])
                for h in range(2):
                    k.ts("dve", segl[:, h, :], triLs[:], a_t[tl][:, h:h + 1], None, ALU.mult, None, [triLs, a_t[tl]], [segl])
                k.mms([(pE_seg[:, h * 128:(h + 1) * 128], segl[:, h, :], triU[:], True, True) for h in range(2)], [segl, triU], [pE_seg])
                k.act(LT[:].rearrange("p a b -> p (a b)"), pE_seg[:, :], AF.Exp, [pE_seg], [LT])
                k.mm(pE_sc[:, :], bcT[0:64, ts_], cTb[:, ts_], [bcT, cTb], [pE_sc])
                k.tt("dve", CBm[:], pE_sc[:, :], triU[:], ALU.mult, [pE_sc, triU], [CBm])
                for h in range(2):
                    k.tt("dve", MT[:, h, :], LT[:, h, :], CBm[:], ALU.mult, [LT, CBm], [MT])
                sb_old = STb[st_i["ssd"] % 2]; sb_new = STb[(st_i["ssd"] + 1) % 2]; st_i["ssd"] += 1
                k.mms([(pF_y[:, h * 64:(h + 1) * 64], MT[:, h, :], xdt[:, h * 64:(h + 1) * 64], True, True) for h in range(2)] +
                      [(pF_yo[:, h * 64:(h + 1) * 64], cTb[:, ts_], sb_old[:, h * 64:(h + 1) * 64], True, True) for h in range(2)] +
                      [(pF_st[0:64, h * 64:(h + 1) * 64], b_tok[:], xdd[:, h * 64:(h + 1) * 64], True, True) for h in range(2)],
                      [MT, xdt, cTb, sb_old, b_tok, xdd], [pF_y, pF_yo, pF_st])
                for h in range(2):
                    k.stt(ST[:, h * 64:(h + 1) * 64], ST[:, h * 64:(h + 1) * 64], ecd[:, h:h + 1], pF_st[0:64, h * 64:(h + 1) * 64],
                          ALU.mult, ALU.add, [ST, ecd, pF_st], [ST])
                k.cp("act", sb_new[:], ST[:], [ST], [sb_new])
                for h in range(2):
                    k.act(t1[:, h * 64:(h + 1) * 64], pF_yo[:, h * 64:(h + 1) * 64], AF.Identity, [pF_yo, eac], [t1], scale=eac[:, h:h + 1])
                    k.stt(t1[:, h * 64:(h + 1) * 64], xs_tok[:, h * 64:(h + 1) * 64], pcol(P_DSK + h), t1[:, h * 64:(h + 1) * 64],
                          ALU.mult, ALU.add, [xs_tok, prm, t1], [t1])
                k.tt("dve", yv[:], pF_y[:, :], t1[:], ALU.add, [pF_y, t1], [yv])
                k.tt("dve", gy[:], yv[:], gates[tl][:, 0:128], ALU.mult, [yv, gates[tl]], [gy])
                k.act(sq[:], gy[:], AF.Square, [gy], [sq, sm_d], accum=sm_d[:, 4:5])
                k.ts("dve", sm_d[:, 4:5], sm_d[:, 4:5], 1.0 / 128.0, RMS_EPS, ALU.mult, ALU.add, [sm_d], [sm_d])
                k.act(sm_d[:, 4:5], sm_d[:, 4:5], AF.Ln, [sm_d], [sm_d])
                k.act(sm_d[:, 4:5], sm_d[:, 4:5], AF.Exp, [sm_d], [sm_d], scale=-0.5)
                k.stt(ycat[tl][:, 0:128], gy[:], sm_d[:, 4:5], pcol(P_SNW, 128), ALU.mult, ALU.mult, [gy, sm_d, prm], [ycat[tl]])

        def chain_e():
            for tl in range(4):
                ts_ = slice(tl * 128, (tl + 1) * 128)
                k.mms([(pB0_s[:, 0:128], rkT[0:64, ts_], rqT[0:64, ts_], True, True),
                       (pA1v[:, 0:128], rkT[64:128, ts_], rqT[64:128, ts_], True, True)], [rkT, rqT], [pB0_s, pA1v])
                k.tt("dve", rMT[:, 0, :], pB0_s[:, 0:128], rdec[:, 0, :], ALU.mult, [pB0_s, rdec], [rMT])
                k.tt("dve", rMT[:, 1, :], pA1v[:, 0:128], rdec[:, 1, :], ALU.mult, [pA1v, rdec], [rMT])
                rb_old = Rb[st_i["ret"] % 2]; rb_new = Rb[(st_i["ret"] + 1) % 2]; st_i["ret"] += 1
                lst = []
                for h in range(2):
                    hs = slice(h * 64, (h + 1) * 64)
                    lst.append((pB0_y[:, hs], rMT[:, h, :], rv_t[tl][:, hs], True, False))
                    lst.append((pB0_y[:, hs], rqTd[hs, ts_], rb_old[hs, :], False, True))
                for h in range(2):
                    hs = slice(h * 64, (h + 1) * 64)
                    lst.append((pB0_u[hs, 0:64], rkd_t[tl][:, hs], rv_t[tl][:, hs], True, True))
                k.mms(lst, [rMT, rv_t[tl], rqTd, rb_old, rkd_t[tl]], [pB0_y, pB0_u])
                k.stt(Rr[:], Rr[:], rcg[:], pB0_u[:, 0:64], ALU.mult, ALU.add, [Rr, rcg, pB0_u], [Rr])
                k.cp("act", rb_new[:], Rr[:], [Rr], [rb_new])
                for h in range(2):
                    hs = slice(h * 64, (h + 1) * 64)
                    S.op("dve", lambda e, hs=hs: e.bn_stats(out=st12r[:, 0:6], in_=pB0_y[:, hs]), [pB0_y], [st12r])
                    S.op("dve", lambda e: e.bn_aggr(out=mvr[:], in_=st12r[:, 0:6]), [st12r], [mvr])
                    rstd_from(mvr[:, 1:2], LN_EPS, rstdr[:], [mvr], [rstdr])
                    k.ts("dve", yn_e[:, hs], pB0_y[:, hs], mvr[:, 0:1], rstdr[:], ALU.subtract, ALU.mult, [pB0_y, mvr, rstdr], [yn_e])
                k.tt("dve", ycat[tl][:, 256:384], yn_e[:], gates[tl][:, 256:384], ALU.mult, [yn_e, gates[tl]], [ycat[tl]])

        def chain_f():
            k.act(fA[:], fA[:], AF.Ln, [fA], [fA], bias=1.0)
            k.act(fA[:], fA[:], AF.Exp, [fA], [fA], scale=-1.0)
            k.ts("dve", fA[:], fA[:], oml[:], lb[:], ALU.mult, ALU.add, [fA, oml, lb], [fA])
            k.act(fB[:], fA[:], AF.Ln, [fA], [fB])
            k.ts("dve", fC[:], fA[:], -1.0, 1.0, ALU.mult, ALU.add, [fA], [fC])
            S.op("dve", lambda e: e.tensor_tensor_scan(out=fD[:], data0=rst[:].rearrange("p a b -> p (a b)"), data1=fB[:], initial=0.0,
                                                       op0=ALU.mult, op1=ALU.add), [rst, fB], [fD])
            k.act(fA[:], fD[:], AF.Exp, [fD], [fA])
            k.cp("dve", hdec[:], fA[:].rearrange("p (a b) -> p a b", b=64)[:, :, 63], [fA], [hdec])
            k.tt("dve", hqt[:], hq[:], fA[:], ALU.mult, [hq, fA], [hqt])
            k.act(fB[:], fD[:], AF.Exp, [fD], [fB], scale=-1.0)
            k.tt("dve", hkt[:], fC[:], fB[:], ALU.mult, [fC, fB], [hkt])
            for c in range(8):
                k.act(fA[:, c * 64:(c + 1) * 64], fD[:, c * 64:(c + 1) * 64], AF.Exp, [fD], [fA], scale=-1.0,
                      bias=fD[:, c * 64 + 63:c * 64 + 64])
            k.tt("dve", hkh[:], fC[:], fA[:], ALU.mult, [fC, fA], [hkh])
            for tl in range(4):
                ts_ = slice(tl * 128, (tl + 1) * 128)
                k.trs([(pTb[:, 2, :], hkh[:, ts_], identb[:])], [hkh, identb], [pTb])
                k.cp("act", khat_tok[:], pTb[:, 2, :], [pTb], [khat_tok])
                k.mms([(pB1_s[:, 0:128], hkt[0:64, ts_], hqt[0:64, ts_], True, True),
                       (pA1v[:, 128:256], hkt[64:128, ts_], hqt[64:128, ts_], True, True)], [hkt, hqt], [pB1_s, pA1v])
                k.tt("dve", hAT[:, 0, :], pB1_s[:, 0:128], hmask[:, 0, :], ALU.mult, [pB1_s, hmask], [hAT])
                k.tt("dve", hAT[:, 1, :], pA1v[:, 128:256], hmask[:, 1, :], ALU.mult, [pA1v, hmask], [hAT])
                s0 = Hb[st_i["hg"] % 3]; s1 = Hb[(st_i["hg"] + 1) % 3]; s2 = Hb[(st_i["hg"] + 2) % 3]; st_i["hg"] += 2
                k.mms([(pB1_u[h * 64:(h + 1) * 64, 0:64], khat_tok[0:64, h * 64:(h + 1) * 64], hi_t[tl][0:64, h * 64:(h + 1) * 64], True, True)
                       for h in range(2)] +
                      [(pA1v[h * 64:(h + 1) * 64, 256:320], khat_tok[64:128, h * 64:(h + 1) * 64], hi_t[tl][64:128, h * 64:(h + 1) * 64], True, True)
                       for h in range(2)], [khat_tok, hi_t[tl]], [pB1_u, pA1v])
                k.stt(Hs[:], Hs[:], hdec[:, 2 * tl:2 * tl + 1], pB1_u[:, 0:64], ALU.mult, ALU.add, [Hs, hdec, pB1_u], [Hs])
                k.cp("act", s1[:], Hs[:], [Hs], [s1])
                k.stt(Hs[:], Hs[:], hdec[:, 2 * tl + 1:2 * tl + 2], pA1v[:, 256:320], ALU.mult, ALU.add, [Hs, hdec, pA1v], [Hs])
                k.cp("act", s2[:], Hs[:], [Hs], [s2])
                lst = []
                for h in range(2):
                    hs = slice(h * 64, (h + 1) * 64)
                    lst.append((pB1_y[:, hs], hAT[:, h, :], hi_t[tl][:, hs], True, False))
                    lst.append((pB1_y[0:64, hs], hqt[hs, tl * 128:tl * 128 + 64], s0[hs, :], False, False))
                    lst.append((pB1_y[64:128, hs], hqt[hs, tl * 128 + 64:tl * 128 + 128], s1[hs, :], False, True))
                k.mms(lst, [hAT, hi_t[tl], hqt, s0, s1], [pB1_y])
                for h in range(2):
                    hs = slice(h * 64, (h + 1) * 64)
                    k.act(sq_f[:, hs], pB1_y[:, hs], AF.Square, [pB1_y], [sq_f, sm_f], accum=sm_f[:, 6 + h:7 + h])
                k.ts("dve", sm_f[:, 6:8], sm_f[:, 6:8], 1.0 / 64.0, RMS_EPS, ALU.mult, ALU.add, [sm_f], [sm_f])
                k.act(sm_f[:, 6:8], sm_f[:, 6:8], AF.Ln, [sm_f], [sm_f])
                k.act(sm_f[:, 6:8], sm_f[:, 6:8], AF.Exp, [sm_f], [sm_f], scale=-0.5)
                for h in range(2):
                    hs = slice(h * 64, (h + 1) * 64)
                    k.stt(yn_f[:, hs], pB1_y[:, hs], sm_f[:, 6 + h:7 + h], prm[:, P_HNW + h * 64:P_HNW + (h + 1) * 64], ALU.mult, ALU.mult,
                          [pB1_y, sm_f, prm], [yn_f])
                k.tt("dve", ycat[tl][:, 128:256], yn_f[:], gates[tl][:, 128:256], ALU.mult, [yn_f, gates[tl]], [ycat[tl]])

        def chain_g():
            for h in range(2):
                k.cp("pool", Qaug[h][0:64, :], mqf[h][:, :], [mqf[h]], [Qaug[h]])
            for tl in range(4):
                own = 2 * J + tl // 2
                ts_ = slice(tl * 128, (tl + 1) * 128)
                for h in range(2):
                    lst = [(pD_g[:, 32:33], mqa[h][:, ts_], kabsb[h][:, :], True, True)]
                    if own > 0:
                        lst.append((pD_g[:, 0:own], mqf[h][:, ts_], ksum[h][:, 0:own], True, True))
                    k.mms(lst, [mqa[h], kabsb[h], mqf[h], ksum[h]], [pD_g])
                    k.ts("dve", negm[:, h:h + 1], pD_g[:, 32:33], stl[:, h, tl:tl + 1], -1.0, ALU.add, ALU.mult, [pD_g, stl], [negm])
                    k.ts("dve", negm[:, h:h + 1], negm[:, h:h + 1], BIG, None, ALU.add, None, [negm], [negm])
                    if own > 0:
                        k.cp("dve", gsc[h][:, 0:own], pD_g[:, 0:own], [pD_g], [gsc[h]])
                        S.op("dve", lambda e, h=h: e.max(out=m8[:], in_=gsc[h][:]), [gsc[h]], [m8])
                        k.ts("dve", selb[h][:, 0:own], gsc[h][:, 0:own], m8[:, 2:3], None, ALU.is_ge, None, [gsc[h], m8], [selb[h]])
                    k.memset("dve", selb[h][:, own:own + 1], 1.0, [selb[h]])
                    k.ts("dve", gtok[:, 64:96], selb[h][:], negm[:, h:h + 1], -BIG, ALU.mult, ALU.add, [selb[h], negm], [gtok])
                    k.trs([(pTb[0:96, 3, :], gtok[:], identb[:])], [gtok, identb], [pTb])
                    k.cp("act", Qaug[h][64:96, ts_], pTb[64:96, 3, :], [pTb], [Qaug[h]])
        def moba_attn():
            nkt = 4 * J + 4
            steps = [(kt, h) for kt in range(nkt) for h in range(2)]
            pqs = [pA[0], pA[1], pE_w]
            LA = 2

            def qk(si):
                kt, h = steps[si]
                dl = kt - 4 * J
                c0 = 0 if dl < 0 else 128 * dl
                pq = pqs[si % 3]
                k.mm(pq[:, c0:512], Kaug[h][:, kt * 128:(kt + 1) * 128], Qaug[h][:, c0:512], [Kaug[h], Qaug[h]], [pq])
                if dl >= 0:
                    k.tt("dve", pq[:, c0:c0 + 128], pq[:, c0:c0 + 128], maskneg[:], ALU.add, [pq, maskneg], [pq])

            def pv(si):
                kt, h = steps[si]
                dl = kt - 4 * J
                c0 = 0 if dl < 0 else 128 * dl
                pq = pqs[si % 3]; Pt = Pb[si % 3]
                k.act(Pt[:, c0:512], pq[:, c0:512], AF.Exp, [pq, bcol], [Pt], bias=bcol[:, h, dl + 60:dl + 61])
                k.mm(pB[h][0:65, c0:512], Vaug[:, kt, h, :], Pt[:, c0:512], [Vaug, Pt], [pB[h]], start=(kt == 0), stop=(kt == nkt - 1))
            n = len(steps)
            for si in range(n + LA):
                if si < n:
                    qk(si)
                if si - LA >= 0:
                    pv(si - LA)
            for h in range(2):
                k.cp("act", oTs[h][:], pB[h][0:65, :], [pB[h]], [oTs[h]])
            for tl in range(4):
                ts_ = slice(tl * 128, (tl + 1) * 128)
                k.trs([(pD_fin[:, h * 65:h * 65 + 65], oTs[h][0:65, ts_], identf[0:65, 0:65]) for h in range(2)],
                      [oTs[0], oTs[1], identf], [pD_fin])
                for h in range(2):
                    S.op("dve", lambda e, h=h: e.reciprocal(out=sm_g[:, 8 + h:9 + h], in_=pD_fin[:, h * 65 + 64:h * 65 + 65]), [pD_fin], [sm_g])
                for h in range(2):
                    hs = slice(h * 64, (h + 1) * 64)
                    k.ts("dve", yn_g[:, hs], pD_fin[:, h * 65:h * 65 + 64], sm_g[:, 8 + h:9 + h], None, ALU.mult, None, [pD_fin, sm_g], [yn_g])
                k.tt("dve", ycat[tl][:, 384:512], yn_g[:], gates[tl][:, 384:512], ALU.mult, [yn_g, gates[tl]], [ycat[tl]])

        S.il.run([fn for fn, c_ in ((chain_d, 'd'), (chain_e, 'e'), (chain_f, 'f'), (chain_g, 'g')) if c_ in STAGES])
        if 'g' in STAGES:
            moba_attn()

        for _stage in ([0] if 'h' in STAGES else []):
            for tl in range(4):
                i = 4 * J + tl
                if callable(ycat_d):
                    k.store(ycat_d(i)[0], ycat[tl][:], ycat[tl], dstbuf=ycat_d(i)[1])
                else:
                    k.store(ycat_d[i * 128:(i + 1) * 128, :], ycat[tl][:], ycat[tl], dstbuf=ycat_b)
            if after_store is not None:
                after_store(J)


def build_M(T, layer, src_is_x):
    nc = bass.Bass("TRN2", target_bir_lowering=False)
    if src_is_x:
        src = nc.dram_tensor("src", [T, D_MODEL], F32, kind="ExternalInput").ap()
    else:
        src = nc.dram_tensor("src", [128, 8, T], F32, kind="ExternalInput").ap()
    prm = nc.dram_tensor("prm", [128, NPRM], F32, kind="ExternalInput").ap()
    wf = nc.dram_tensor("wf", [D_MODEL, NF], F32, kind="ExternalInput").ap()
    wt = nc.dram_tensor("wt", [D_MODEL, NT], F32, kind="ExternalInput").ap()
    wr = nc.dram_tensor("wr", [D_MODEL, NR], F32, kind="ExternalInput").ap()
    ycat = nc.dram_tensor("ycat", [T, 512], BF16, kind="ExternalOutput").ap()
    with contextlib.ExitStack() as st:
        S = Sched(nc, st)
        k = K(S)
        C = build_consts(S, k)
        m_phase(nc, S, k, C, T, layer, src, prm, wf, wt, wr, ycat, src_is_x)
        S.final_wait()
        S.emit()
    return nc


OFFS = np.cumsum((0, 256, 512, 4, 256, 256, 256, 256, 256, 256, 256, 256, 256, 256, 256, 256))


def core_weights(w_in_l, g):
    o = OFFS
    hp = slice(g * 128, (g + 1) * 128)

    def seg(i, sl):
        return w_in_l[:, o[i] + sl.start:o[i] + sl.stop]
    xbc = w_in_l[:, o[1]:o[2]]
    xs = xbc[:, g * 128:(g + 1) * 128]
    Bm = xbc[:, 256 + g * 64:256 + (g + 1) * 64]
    Cm = xbc[:, 384 + g * 64:384 + (g + 1) * 64]
    mq = seg(11, hp); mk = seg(12, hp)
    wf = np.concatenate([xs, Bm, Cm, seg(3, hp), seg(4, hp), seg(7, hp), seg(8, hp)], axis=1)
    wr = np.concatenate([mq, mk], axis=1)
    dt = w_in_l[:, o[2] + 2 * g:o[2] + 2 * g + 2]
    wt = np.concatenate([seg(0, hp), seg(6, hp), seg(10, hp), seg(14, hp),
                         seg(5, hp), seg(8, hp), seg(9, hp), seg(13, hp), dt], axis=1)
    return np.ascontiguousarray(wf, np.float32), np.ascontiguousarray(wt, np.float32), np.ascontiguousarray(wr, np.float32)


def core_params(inp, l, g, ln_g, ln_b):
    prm = np.zeros((128, NPRM), np.float32)
    bc = lambda v: np.broadcast_to(np.asarray(v, np.float32)[None, :], (128, len(v)))
    prm[:, P_LNG:P_LNG + 1024] = bc(ln_g)
    prm[:, P_LNB:P_LNB + 1024] = bc(ln_b)
    cw = inp["ssd_conv_w"][l]
    cb = inp["ssd_conv_b"][l]
    ch_xs = np.arange(g * 128, (g + 1) * 128)
    ch_bc = np.concatenate([256 + g * 64 + np.arange(64), 384 + g * 64 + np.arange(64)])
    prm[:, P_CWX:P_CWX + 4] = cw[:, ch_xs].T
    prm[:, P_CWB:P_CWB + 4] = cw[:, ch_bc].T
    prm[:, P_CBX] = cb[ch_xs]
    prm[:, P_CBB] = cb[ch_bc]
    hh = slice(2 * g, 2 * g + 2)
    prm[:, P_DTB:P_DTB + 2] = bc(inp["ssd_dt_bias"][l][hh])
    prm[:, P_ALOG:P_ALOG + 2] = bc(inp["ssd_a_log"][l][hh])
    prm[:, P_DSK:P_DSK + 2] = bc(inp["ssd_d"][l][hh])
    prm[:, P_SNW:P_SNW + 128] = bc(inp["ssd_norm_w"][l][g * 128:(g + 1) * 128])
    prm[:, P_HNW:P_HNW + 128] = bc(inp["hgrn_norm_w"][l][g * 128:(g + 1) * 128])
    prm[:, P_LBL:P_LBL + 2] = inp["hgrn_lb_logits"][:, g * 128:(g + 1) * 128].T
    heads = np.arange(2 * g, 2 * g + 2, dtype=np.float64)
    logg = np.log(1.0 - 2.0 ** (-5.0 - heads)).astype(np.float32)
    prm[0:64, P_RLGP] = logg[0]
    prm[64:128, P_RLGP] = logg[1]
    prm[:, P_RLGB:P_RLGB + 2] = bc(logg)
    slopes = (2.0 ** (-8.0 * (heads + 1.0) / 4.0)).astype(np.float32)
    prm[:, P_SLOPE:P_SLOPE + 2] = bc(slopes)
    return prm


def o_phase(nc, S, k, C, NTOK, layer, res_d, y0_d, y1_d, prmo_d, wo_d, out_d, hT_d, res_is_x):
    ident = C["identb"]
    ntile = NTOK // 128
    po = S.sb("prmo", [128, 4096])
    k.load(po[:], prmo_d, po)
    wo = S.sb("wo", [128, 8, 1024], BF16)
    k.load(wo[:], wo_d.rearrange("(c p) n -> p c n", p=128), wo, queue="pool")
    yt = [S.sb("o_yt%d" % i, [128, 1024], BF16) for i in range(2)]
    yT = [S.sb("o_yT%d" % i, [128, 8, 128], BF16) for i in range(2)]
    rt = [S.sb("o_rt%d" % i, [128, 1024]) for i in range(2)]
    zt = [S.sb("o_zt%d" % i, [128, 1024]) for i in range(2)]
    ot = [S.sb("o_ot%d" % i, [128, 1024]) for i in range(2)]
    oT = [S.sb("o_oT%d" % i, [128, 8, 128]) for i in range(2)]
    identf = C["identf"]
    st12 = [S.sb("o_st12%d" % i, [128, 12]) for i in range(2)]
    mv = [S.sb("o_mv%d" % i, [128, 2]) for i in range(2)]
    rstd = [S.sb("o_rstd%d" % i, [128, 1]) for i in range(2)]
    pT = [S.ps("o_pT%d" % i, [128, 8, 128], BF16) for i in range(2)]
    pY = [[S.ps("o_pY%d_%d" % (i, hf), [128, 512]) for hf in range(2)] for i in range(2)]
    res_b = Buf("res_d", res_d); y_b = Buf("y_d", None); out_b = Buf("out_d", out_d)

    def ln(q, src, dst, dst_buf, goff, boff):
        S.op("dve", lambda e: (e.bn_stats(out=st12[q][:, 0:6], in_=src[:, 0:512]),
                               e.bn_stats(out=st12[q][:, 6:12], in_=src[:, 512:1024]))[1], [src], [st12[q]])
        S.op("dve", lambda e: e.bn_aggr(out=mv[q][:], in_=st12[q][:]), [st12[q]], [mv[q]])
        k.ts("dve", rstd[q][:], mv[q][:, 1:2], LN_EPS, None, ALU.add, None, [mv[q]], [rstd[q]])
        k.act(rstd[q][:], rstd[q][:], AF.Ln, [rstd[q]], [rstd[q]])
        k.act(rstd[q][:], rstd[q][:], AF.Exp, [rstd[q]], [rstd[q]], scale=-0.5)
        k.ts("dve", src[:], src[:], mv[q][:, 0:1], rstd[q][:], ALU.subtract, ALU.mult, [src, mv[q], rstd[q]], [src])
        k.tt("pool", src[:], src[:], po[:, goff:goff + 1024], ALU.mult, [src, po], [src])
        k.tt("pool", dst[:], src[:], po[:, boff:boff + 1024], ALU.add, [src, po], [dst_buf])

    def tile(i):
        q = i % 2
        tsl = slice(i * 128, (i + 1) * 128)
        y = yt[q]; r = rt[q]; o = ot[q]
        ya, yb_ = (y0_d(i) if callable(y0_d) else (y0_d[tsl, :], y1_d[tsl, :]))
        S.dma(lambda e, s, y=y, ya=ya, yb_=yb_: (e.dma_start(out=y[:, 0:512], in_=ya).then_inc(s, 16),
                                                 e.dma_start(out=y[:, 512:1024], in_=yb_).then_inc(s, 16)),
              reads=[y_b], writes=[y], key=y, n=2)
        k.load(r[:], res_d[tsl, :], r, reads=[res_b])
        k.trs([(pT[q][:, c, :], y[:, c * 128:(c + 1) * 128], ident[:]) for c in range(8)], [y, ident], [pT[q]])
        k.cp("act", yT[q][:], pT[q][:], [pT[q]], [yT[q]])
        if res_is_x:
            ln(q, r, r, r, 0, 1024)
        for hf in range(2):
            k.mms([(pY[q][hf][:, :], yT[q][:, c, :], wo[:, c, hf * 512:(hf + 1) * 512], c == 0, c == 7) for c in range(8)],
                  [yT[q], wo], [pY[q][hf]])
            k.stt(zt[q][:, hf * 512:(hf + 1) * 512], r[:, hf * 512:(hf + 1) * 512], ALPHA, pY[q][hf][:, :], ALU.mult, ALU.add,
                  [r, pY[q][hf]], [zt[q]])
        ln(q, zt[q], o, o, 2048, 3072)
        k.store(out_d[tsl, :], o[:], o, dstbuf=out_b)
        if hT_d is not None:
            t_ = oT[q]
            for hf in range(2):
                k.trs([(pY[q][hf][:, c * 128:(c + 1) * 128], o[:, (4 * hf + c) * 128:(4 * hf + c + 1) * 128], identf[:]) for c in range(4)],
                      [o, identf], [pY[q][hf]])
                k.cp("act", t_[:, 4 * hf:4 * hf + 4, :], pY[q][hf][:, :].rearrange("p (c t) -> p c t", c=4), [pY[q][hf]], [t_])
            k.store(hT_d[:, :, tsl], t_[:], t_, dstbuf=out_b)

    for i in range(0, ntile, 2):
        S.il.run([(lambda i=i: tile(i)), (lambda i=i: tile(i + 1))] if i + 1 < ntile else [lambda i=i: tile(i)])


def build_O(NTOK, layer, res_is_x, want_hT):
    nc = bass.Bass("TRN2", target_bir_lowering=False)
    res = nc.dram_tensor("res", [NTOK, D_MODEL], F32, kind="ExternalInput").ap()
    y0 = nc.dram_tensor("y0", [NTOK, 512], BF16, kind="ExternalInput").ap()
    y1 = nc.dram_tensor("y1", [NTOK, 512], BF16, kind="ExternalInput").ap()
    prmo = nc.dram_tensor("prmo", [128, 4096], F32, kind="ExternalInput").ap()
    wo = nc.dram_tensor("wo", [D_MODEL, D_MODEL], F32, kind="ExternalInput").ap()
    out = nc.dram_tensor("out", [NTOK, D_MODEL], F32, kind="ExternalOutput").ap()
    hT = nc.dram_tensor("hT", [128, 8, NTOK], F32, kind="ExternalOutput").ap() if want_hT else None
    with contextlib.ExitStack() as st:
        S = Sched(nc, st)
        k = K(S)
        C = build_consts(S, k)
        o_phase(nc, S, k, C, NTOK, layer, res, y0, y1, prmo, wo, out, hT, res_is_x)
        S.final_wait()
        S.emit()
    return nc


def wout_perm():
    rows = []
    for g in range(2):
        for m in range(4):
            rows.append(np.arange(m * 256 + g * 128, m * 256 + (g + 1) * 128))
    return np.concatenate(rows)


def o_params(emb_g, emb_b, ln_g, ln_b):
    bc = lambda v: np.broadcast_to(np.asarray(v, np.float32)[None, :], (128, 1024))
    return np.ascontiguousarray(np.concatenate([bc(emb_g), bc(emb_b), bc(ln_g), bc(ln_b)], axis=1))


_NC_CACHE = {}
PAIRS = [[0, 1], [2, 3], [4, 5], [6, 7]]


def build_fused(T):
    nc = bass.Bass("TRN2", target_bir_lowering=False)
    src = nc.dram_tensor("src", [T, D_MODEL], F32, kind="ExternalInput").ap()
    ins = []
    for l in range(DEPTH):
        ins.append(dict(
            prm=nc.dram_tensor("prm%d" % l, [128, NPRM], F32, kind="ExternalInput").ap(),
            wf=nc.dram_tensor("wf%d" % l, [D_MODEL, NF], F32, kind="ExternalInput").ap(),
            wt=nc.dram_tensor("wt%d" % l, [D_MODEL, NT], F32, kind="ExternalInput").ap(),
            wr=nc.dram_tensor("wr%d" % l, [D_MODEL, NR], F32, kind="ExternalInput").ap(),
            prmo=nc.dram_tensor("prmo%d" % l, [128, 4096], F32, kind="ExternalInput").ap(),
            wo=nc.dram_tensor("wo%d" % l, [D_MODEL, D_MODEL], F32, kind="ExternalInput").ap()))
    out = nc.dram_tensor("out", [T, D_MODEL], F32, kind="ExternalOutput").ap()
    CH = 1024
    NCH = max(T // CH, 1)
    CH = T // NCH
    yc_mine = [[nc.dram_tensor("yc_mine%d_%d" % (l, c), [CH, 512], BF16) for c in range(NCH)] for l in range(DEPTH)]
    yc_pair = [[nc.dram_tensor("yc_pair%d_%d" % (l, c), [2 * CH, 512], BF16) for c in range(NCH)] for l in range(DEPTH)]
    TPC = CH // 128
    h1 = nc.dram_tensor("h1", [T, D_MODEL], F32)
    h1T = nc.dram_tensor("h1T", [128, 8, T], F32)
    with contextlib.ExitStack() as top:
        S = Sched(nc, top)
        k = K(S)
        for l in range(DEPTH):
            with contextlib.ExitStack() as st:
                S.stack = st
                S.prefix = "M%d_" % l
                C = build_consts(S, k)
                tile_bufs = [Buf("ycd%d_%d" % (l, i), None) for i in range(T // 128)]

                def ycat_dst(i, l=l, tile_bufs=tile_bufs):
                    return yc_mine[l][i // TPC].ap()[(i % TPC) * 128:(i % TPC + 1) * 128, :], tile_bufs[i]

                def after_store(J, l=l, tile_bufs=tile_bufs):
                    last = 4 * J + 3
                    if (last + 1) % TPC != 0:
                        return
                    c = last // TPC

                    def cc(e, s):
                        e.collective_compute("AllGather", ALU.bypass, replica_groups=PAIRS,
                                             ins=[yc_mine[l][c].ap()], outs=[yc_pair[l][c].ap()]).then_inc(s, 1)
                    S.dma(cc, reads=tile_bufs[c * TPC:(c + 1) * TPC], key="cc", queue="pool", inc=1)
                m_phase(nc, S, k, C, T, l, src if l == 0 else h1T.ap(), ins[l]["prm"], ins[l]["wf"], ins[l]["wt"], ins[l]["wr"],
                        ycat_dst, l == 0, after_store=after_store)
            S.barrier()
            with contextlib.ExitStack() as st:
                S.stack = st
                S.prefix = "O%d_" % l
                C = build_consts(S, k)
                def ysrc(i, l=l):
                    yp = yc_pair[l][i // TPC].ap()
                    r0 = (i % TPC) * 128
                    return yp[r0:r0 + 128, :], yp[CH + r0:CH + r0 + 128, :]
                o_phase(nc, S, k, C, T, l, src if l == 0 else h1.ap(), ysrc, None, ins[l]["prmo"], ins[l]["wo"],
                        h1.ap() if l < DEPTH - 1 else out, h1T.ap() if l < DEPTH - 1 else None, l == 0)
            S.barrier()
        S.final_wait()
        S.emit()
    return nc


def kernel(x, emb_ln_g, emb_ln_b, w_in, ssd_conv_w, ssd_conv_b, ssd_dt_bias, ssd_a_log, ssd_d, ssd_norm_w,
           hgrn_lb_logits, hgrn_norm_w, w_out, ln_g, ln_b):
    inp = dict(x=x, emb_ln_g=emb_ln_g, emb_ln_b=emb_ln_b, w_in=w_in, ssd_conv_w=ssd_conv_w, ssd_conv_b=ssd_conv_b,
               ssd_dt_bias=ssd_dt_bias, ssd_a_log=ssd_a_log, ssd_d=ssd_d, ssd_norm_w=ssd_norm_w,
               hgrn_lb_logits=hgrn_lb_logits, hgrn_norm_w=hgrn_norm_w, w_out=w_out, ln_g=ln_g, ln_b=ln_b)
    inp = {k_: np.asarray(v, np.float32) for k_, v in inp.items()}
    B, T, D = inp["x"].shape
    assert B == 4 and D == D_MODEL
    cores = list(range(8))
    perm = wout_perm()
    if ("F", T) not in _NC_CACHE:
        _NC_CACHE[("F", T)] = build_fused(T)
    nc = _NC_CACHE[("F", T)]
    shared = {}
    for l in range(DEPTH):
        shared["prmo%d" % l] = o_params(inp["emb_ln_g"], inp["emb_ln_b"], inp["ln_g"][l], inp["ln_b"][l])
        shared["wo%d" % l] = np.ascontiguousarray(inp["w_out"][l][perm])
    percore_g = []
    for g in range(2):
        d = {}
        for l in range(DEPTH):
            wf, wt, wr = core_weights(inp["w_in"][l], g)
            d["wf%d" % l], d["wt%d" % l], d["wr%d" % l] = wf, wt, wr
            d["prm%d" % l] = core_params(inp, l, g, inp["emb_ln_g"], inp["emb_ln_b"])
        percore_g.append(d)
    maps = []
    for c in cores:
        m = {"src": np.ascontiguousarray(inp["x"][c // 2])}
        m.update(shared)
        m.update(percore_g[c % 2])
        maps.append(m)
    r = run_bass_kernel_spmd(nc, maps, core_ids=cores)
    out = np.stack([r.results[2 * b]["out"] for b in range(B)])
    return np.ascontiguousarray(out.astype(np.float32))
```

```python
import contextlib
import threading
import math
import numpy as np
import ml_dtypes
import concourse.bass as bass
import concourse.mybir as mybir
from concourse.bass_utils import run_bass_kernel_spmd

F32 = mybir.dt.float32
BF16 = mybir.dt.bfloat16
I32 = mybir.dt.int32
F32R = mybir.dt.float32r
AF = mybir.ActivationFunctionType
ALU = mybir.AluOpType
AX = mybir.AxisListType

ENGS = ("pe", "act", "dve", "pool", "sp")

D_MODEL = 1024
DEPTH = 2
NF = 6 * 128
NR = 256
NT = 1026
LN_EPS = 1e-5
RMS_EPS = 1e-6
ALPHA = (2.0 * DEPTH) ** 0.25
BIG = 30000.0
STAGES = set('abcdefgh')
BLIM = 99
NEG = -1.0e30

P_LNG, P_LNB = 0, 1024
P_CWX, P_CWB, P_CBX, P_CBB = 2048, 2052, 2056, 2057
P_DTB, P_ALOG, P_DSK = 2058, 2060, 2062
P_SNW, P_HNW = 2064, 2192
P_LBL = 2320
P_RLGP, P_RLGB, P_SLOPE = 2322, 2323, 2325
NPRM = 2328


class Buf:
    __slots__ = ("name", "t", "writer", "readers", "lock")

    def __init__(self, name, t=None, lock=None):
        self.name = name
        self.t = t
        self.writer = None
        self.readers = []
        self.lock = lock

    def __getitem__(self, idx):
        return self.t[idx]


class Interleaver:
    def __init__(self):
        self.active = False
        self.tls = threading.local()

    def run(self, fns):
        fns = list(fns)
        if len(fns) == 0:
            return
        if len(fns) == 1:
            fns[0]()
            return
        n = len(fns)
        self.alive = [True] * n
        self.turn = 0
        self.cv = threading.Condition()
        self.exc = None
        self.active = True

        def worker(i):
            self.tls.idx = i
            with self.cv:
                while self.turn != i:
                    self.cv.wait()
            try:
                fns[i]()
            except BaseException as e:
                self.exc = e
            with self.cv:
                self.alive[i] = False
                self._advance(i)
                self.cv.notify_all()
            self.tls.idx = None
        ths = [threading.Thread(target=worker, args=(i,)) for i in range(n)]
        for t in ths:
            t.start()
        for t in ths:
            t.join()
        self.active = False
        if self.exc is not None:
            raise self.exc

    def _advance(self, i):
        n = len(self.alive)
        for d in range(1, n + 1):
            j = (i + d) % n
            if self.alive[j]:
                self.turn = j
                return
        self.turn = -1

    def switch(self):
        if not self.active:
            return
        i = getattr(self.tls, "idx", None)
        if i is None:
            return
        with self.cv:
            self._advance(i)
            if self.turn != i:
                self.cv.notify_all()
                while self.turn != i:
                    self.cv.wait()


class Sched:
    def __init__(self, nc, stack):
        self.il = Interleaver()
        self.nc = nc
        self.stack = stack
        self.ops = {e: [] for e in ENGS}
        self.count = {e: 0 for e in ENGS}
        self.waited = {e: {} for e in ENGS}
        self.sems = {}
        self.dma_count = {}
        self.semstack = stack
        self.prefix = ""
        for e in ENGS:
            if e != "sp":
                self.sems[e] = stack.enter_context(nc.semaphore("s_" + e))

    def sb(self, name, shape, dtype=F32):
        name = self.prefix + name
        return Buf(name, self.stack.enter_context(self.nc.sbuf_tensor("sb_" + name, list(shape), dtype)))

    def ps(self, name, shape, dtype=F32):
        name = self.prefix + name
        return Buf(name, self.stack.enter_context(self.nc.psum_tensor("ps_" + name, list(shape), dtype)), lock=[None])

    def barrier(self):
        toks = [(e, self.count[e]) for e in ENGS if e != "sp" and self.count[e] > 0]
        toks += [(k_, v) for k_, v in self.dma_count.items() if v > 0]
        for e in ENGS:
            waits = []
            for (k_, v) in toks:
                if self.waited[e].get(k_, 0) < v:
                    self.waited[e][k_] = v
                    waits.append((k_, v))
            self.ops[e].append((waits, None, None))

    def _deps(self, eng, reads, writes):
        toks = []
        for b in list(reads) + list(writes):
            if b.lock is not None and b.lock[0] is not None and b.lock[0][0] != eng:
                toks.append(b.lock[0])
        for b in reads:
            if b.writer is not None:
                toks.append(b.writer)
        for b in writes:
            if b.writer is not None:
                toks.append(b.writer)
            toks.extend(b.readers)
        need = {}
        for (k, v) in toks:
            if v > need.get(k, 0):
                need[k] = v
        w = self.waited[eng]
        out = []
        for k, v in need.items():
            if w.get(k, 0) >= v:
                continue
            w[k] = v
            out.append((k, v))
        return out

    def _commit(self, tok, reads, writes):
        for b in list(reads) + list(writes):
            if b.lock is not None:
                b.lock[0] = tok
        for b in writes:
            b.writer = tok
            b.readers = []
        for b in reads:
            if b not in writes:
                b.readers.append(tok)
                if len(b.readers) > 48:
                    m = {}
                    for (k, v) in b.readers:
                        if v > m.get(k, 0):
                            m[k] = v
                    b.readers = list(m.items())

    def op(self, eng, fn, reads=(), writes=()):
        reads = [getattr(b, "buf", b) for b in reads if b is not None]
        writes = [getattr(b, "buf", b) for b in writes if b is not None]
        waits = self._deps(eng, reads, writes)
        self.count[eng] += 1
        tok = (eng, self.count[eng])
        self.ops[eng].append((waits, fn, (eng, 1)))
        self._commit(tok, reads, writes)
        self.il.switch()
        return tok

    def dma(self, fn, reads=(), writes=(), key=None, n=1, queue="sp", inc=16):
        reads = [b for b in reads if b is not None]
        writes = [b for b in writes if b is not None]
        semkey = key if isinstance(key, str) else "d_" + key.name
        if semkey not in self.sems:
            self.sems[semkey] = self.semstack.enter_context(self.nc.semaphore(semkey))
            self.dma_count[semkey] = 0
        waits = self._deps(queue, reads, writes)
        self.dma_count[semkey] += inc * n
        tok = (semkey, self.dma_count[semkey])
        self.ops[queue].append((waits, fn, (semkey, 0)))
        self._commit(tok, reads, writes)
        return tok

    def final_wait(self, queue="sp"):
        waits = []
        for e in ENGS:
            if e != "sp" and self.count[e] > 0:
                waits.append((e, self.count[e]))
        for k, v in self.dma_count.items():
            if v > 0:
                waits.append((k, v))
        self.ops[queue].append((waits, None, None))

    def emit(self):
        nc = self.nc
        sems = self.sems
        engmap = {"pe": "tensor", "act": "scalar", "dve": "vector", "pool": "gpsimd", "sp": "sync"}
        with nc.Block() as block:
            for e in ENGS:
                oplist = self.ops[e]

                def body(engine, oplist=oplist):
                    for (waits, fn, inc) in oplist:
                        for (k, v) in waits:
                            engine.wait_ge(sems[k], v)
                        if fn is None:
                            continue
                        if inc[1] == 0:
                            fn(engine, sems[inc[0]])
                        else:
                            fn(engine).then_inc(sems[inc[0]], 1)
                getattr(block, engmap[e])(body)


class K:
    def __init__(self, S):
        self.S = S

    def mm(self, out, lhsT, rhs, reads, writes, start=True, stop=True):
        return self.S.op("pe", lambda e: e.matmul(out, lhsT=lhsT, rhs=rhs, start=start, stop=stop), reads, writes)

    def mms(self, lst, reads, writes):
        def fn(e):
            ins = None
            for (out, lhsT, rhs, start, stop) in lst:
                ins = e.matmul(out, lhsT=lhsT, rhs=rhs, start=start, stop=stop)
            return ins
        return self.S.op("pe", fn, reads, writes)

    def trs(self, lst, reads, writes):
        def fn(e):
            ins = None
            for (out, in_, ident) in lst:
                ins = e.transpose(out=out, in_=in_, identity=ident)
            return ins
        return self.S.op("pe", fn, reads, writes)

    def act(self, out, in_, func, reads, writes, bias=None, scale=None, accum=None):
        kw = {}
        if bias is not None:
            kw["bias"] = bias
        if scale is not None:
            kw["scale"] = scale
        if accum is not None:
            kw["accum_out"] = accum
        return self.S.op("act", lambda e: e.activation(out=out, in_=in_, func=func, **kw), reads, writes)

    def tt(self, eng, out, in0, in1, op, reads, writes):
        return self.S.op(eng, lambda e: e.tensor_tensor(out=out, in0=in0, in1=in1, op=op), reads, writes)

    def ts(self, eng, out, in0, s1, s2, op0, op1, reads, writes):
        if op1 is None:
            return self.S.op(eng, lambda e: e.tensor_scalar(out=out, in0=in0, scalar1=s1, scalar2=None, op0=op0), reads, writes)
        return self.S.op(eng, lambda e: e.tensor_scalar(out=out, in0=in0, scalar1=s1, scalar2=s2, op0=op0, op1=op1), reads, writes)

    def stt(self, out, in0, scalar, in1, op0, op1, reads, writes):
        return self.S.op("dve", lambda e: e.scalar_tensor_tensor(out=out, in0=in0, scalar=scalar, in1=in1, op0=op0, op1=op1), reads, writes)

    def cp(self, eng, out, in_, reads, writes):
        if eng == "act":
            return self.S.op("act", lambda e: e.activation(out=out, in_=in_, func=AF.Copy), reads, writes)
        return self.S.op(eng, lambda e: e.tensor_copy(out=out, in_=in_), reads, writes)

    def memset(self, eng, ap, val, writes):
        return self.S.op(eng, lambda e: e.memset(ap, val), (), writes)

    def asel(self, out, in_, pattern, cmp, fill, base, cm, reads, writes):
        return self.S.op("pool", lambda e: e.affine_select(out=out, in_=in_, pattern=pattern, compare_op=cmp, fill=fill,
                                                           base=base, channel_multiplier=cm), reads, writes)

    def iota(self, out, pattern, base, cm, writes):
        return self.S.op("pool", lambda e: e.iota(out, pattern=pattern, base=base, channel_multiplier=cm), (), writes)

    def load(self, dst_ap, src_ap, buf, reads=(), queue="sp", n=1):
        return self.S.dma(lambda e, s: e.dma_start(out=dst_ap, in_=src_ap).then_inc(s, 16), reads=reads, writes=[buf], key=buf, queue=queue, n=n)

    def store(self, dst_ap, src_ap, buf, dstbuf=None, queue="sp"):
        return self.S.dma(lambda e, s: e.dma_start(out=dst_ap, in_=src_ap).then_inc(s, 16), reads=[buf],
                          writes=[dstbuf] if dstbuf is not None else [], key=buf, queue=queue)


def build_consts(S, k):
    C = {}
    ones = S.sb("c_ones", [128, 128]); k.memset("pool", ones[:], 1.0, [ones])
    C["ones"] = ones
    identf = S.sb("c_identf", [128, 128])
    k.asel(identf[:], ones[:, 0:128], [[-1, 128]], ALU.is_equal, 0.0, 0, 1, [ones], [identf])
    identb = S.sb("c_identb", [128, 128], BF16)
    k.cp("dve", identb[:], identf[:], [identf], [identb])
    C["identf"], C["identb"] = identf, identb
    return C


def m_phase(nc, S, k, C, T, layer, src_d, prm_d, wf_d, wt_d, wr_d, ycat_d, src_is_x, after_store=None):
    NJ = T // 512
    NB = T // 256
    NKT = T // 128
    ones, identf, identb = C["ones"], C["identf"], C["identb"]

    prm = S.sb("prm", [128, NPRM])
    k.load(prm[:], prm_d, prm)
    wF = S.sb("wF", [128, 8, NF], BF16)
    wT = S.sb("wT", [128, 8, NT], BF16)
    wR = S.sb("wR", [128, 8, NR])
    k.load(wR[:], wr_d.rearrange("(c p) n -> p c n", p=128), wR)

    def pcol(off, n=1):
        return prm[:, off:off + n]

    triU = S.sb("triU", [128, 128])
    k.asel(triU[:], ones[:, 0:128], [[1, 128]], ALU.is_ge, 0.0, 0, -1, [ones], [triU])
    triLs = S.sb("triLs", [128, 128])
    k.asel(triLs[:], ones[:, 0:128], [[-1, 128]], ALU.is_ge, 0.0, -1, 1, [ones], [triLs])
    maskneg = S.sb("maskneg", [128, 128])
    k.ts("pool", maskneg[:], triU[:], BIG, -BIG, ALU.mult, ALU.add, [triU], [maskneg])
    hmask = S.sb("hmask", [128, 2, 128])
    for h in range(2):
        k.cp("pool", hmask[:, h, :], triU[:], [triU], [hmask])
        k.memset("pool", hmask[0:64, h, 64:128], 0.0, [hmask])
    dli = S.sb("dli", [128, 128], I32)
    k.iota(dli[:], [[1, 128]], 0, -1, [dli])
    dlf = S.sb("dlf", [128, 128])
    k.cp("dve", dlf[:], dli[:], [dli], [dlf])
    k.ts("dve", dlf[:], dlf[:], 0.0, None, ALU.max, None, [dlf], [dlf])
    rdec = S.sb("rdec", [128, 2, 128])
    for h in range(2):
        k.act(rdec[:, h, :], dlf[:], AF.Exp, [dlf, prm], [rdec], scale=pcol(P_RLGB + h))
        k.tt("dve", rdec[:, h, :], rdec[:, h, :], triU[:], ALU.mult, [rdec, triU], [rdec])
    pidx_i = S.sb("pidx_i", [128, 8], I32)
    k.iota(pidx_i[:], [[128, 8]], 0, 1, [pidx_i])
    pidx = S.sb("pidx", [128, 8])
    k.cp("dve", pidx[:], pidx_i[:], [pidx_i], [pidx])
    rkdec = S.sb("rkdec", [128, 2])
    tmpc = S.sb("tmpc", [128, 8])
    k.ts("dve", tmpc[:, 0:1], pidx[:, 0:1], -1.0, 127.0, ALU.mult, ALU.add, [pidx], [tmpc])
    for h in range(2):
        k.act(rkdec[:, h:h + 1], tmpc[:, 0:1], AF.Exp, [tmpc, prm], [rkdec], scale=pcol(P_RLGB + h))
    k.ts("dve", rkdec[:], rkdec[:], 0.125, None, ALU.mult, None, [rkdec], [rkdec])
    rcg = S.sb("rcg", [128, 1])
    k.memset("dve", tmpc[:, 1:2], 128.0, [tmpc])
    k.act(rcg[:], tmpc[:, 1:2], AF.Exp, [tmpc, prm], [rcg], scale=pcol(P_RLGP))
    k.iota(dli[:], [[1, 128]], 1, 0, [dli])
    qdec = S.sb("qdec", [128, 512])
    for j in range(4):
        k.cp("dve", qdec[:, j * 128:(j + 1) * 128], dli[:], [dli], [qdec])
    k.act(qdec[:], qdec[:], AF.Exp, [qdec, prm], [qdec], scale=pcol(P_RLGP))
    rst = S.sb("rst", [128, 8, 64], BF16)
    k.memset("pool", rst[:], 1.0, [rst])
    k.memset("pool", rst[:, :, 0:1], 0.0, [rst])
    shsel_f = S.sb("shsel_f", [128, 64])
    k.asel(shsel_f[:], ones[:, 0:64], [[-1, 64]], ALU.is_equal, 0.0, -64, 1, [ones], [shsel_f])
    shsel = S.sb("shsel", [128, 64], BF16)
    k.cp("dve", shsel[:], shsel_f[:], [shsel_f], [shsel])
    Abc = S.sb("Abc", [128, 2])
    k.act(Abc[:], pcol(P_ALOG, 2), AF.Exp, [prm], [Abc])
    k.ts("dve", Abc[:], Abc[:], -1.0, None, ALU.mult, None, [Abc], [Abc])
    lb = S.sb("lb", [128, 1]); oml = S.sb("oml", [128, 1])
    if layer == 0:
        k.memset("dve", lb[:], 0.0, [lb])
    else:
        k.tt("dve", tmpc[:, 2:3], pcol(P_LBL), pcol(P_LBL + 1), ALU.subtract, [prm], [tmpc])
        k.act(tmpc[:, 2:3], tmpc[:, 2:3], AF.Exp, [tmpc], [tmpc])
        k.ts("dve", tmpc[:, 2:3], tmpc[:, 2:3], 1.0, None, ALU.add, None, [tmpc], [tmpc])
        S.op("dve", lambda e: e.reciprocal(out=lb[:], in_=tmpc[:, 2:3]), [tmpc], [lb])
    k.ts("dve", oml[:], lb[:], -1.0, 1.0, ALU.mult, ALU.add, [lb], [oml])
    bci = S.sb("bci", [128, 64], I32)
    k.iota(bci[:], [[128, 64]], -60 * 128, 1, [bci])
    bcol = S.sb("bcol", [128, 2, 64])
    for h in range(2):
        k.cp("dve", bcol[:, h, :], bci[:], [bci], [bcol])
        k.ts("dve", bcol[:, h, :], bcol[:, h, :], pcol(P_SLOPE + h), None, ALU.mult, None, [bcol, prm], [bcol])
    stl = S.sb("stl", [128, 2, 4])
    for h in range(2):
        k.ts("dve", stl[:, h, :], pidx[:, 0:4], pcol(P_SLOPE + h), None, ALU.mult, None, [pidx, prm], [stl])

    Kaug = [S.sb("Kaug%d" % h, [96, T], BF16) for h in range(2)]
    for h in range(2):
        k.memset("pool", Kaug[h][64:96, :], 1.0, [Kaug[h]])
        k.asel(Kaug[h][64:96, :], Kaug[h][64:96, :], [[1, T]], ALU.is_ge, 0.0, 0, -256, [Kaug[h]], [Kaug[h]])
        k.asel(Kaug[h][64:96, :], Kaug[h][64:96, :], [[-1, T]], ALU.is_ge, 0.0, 255, 256, [Kaug[h]], [Kaug[h]])
    Vaug = S.sb("Vaug", [128, NKT, 2, 65], BF16)
    k.memset("pool", Vaug[:], 1.0, [Vaug])
    ksum = [S.sb("ksum%d" % h, [64, 32]) for h in range(2)]
    kabs = [S.sb("kabs%d" % h, [64, 2]) for h in range(2)]
    kabsb = [S.sb("kabsb%d" % h, [64, 1], BF16) for h in range(2)]
    for h in range(2):
        k.memset("dve", ksum[h][:], 0.0, [ksum[h]])
        k.memset("dve", kabs[h][:], 0.0, [kabs[h]])
    gtok = S.sb("gtok", [128, 96], BF16)
    k.memset("dve", gtok[:], 0.0, [gtok])
    gsc = [S.sb("gsc%d" % h, [128, 32]) for h in range(2)]
    selb = [S.sb("selb%d" % h, [128, 32]) for h in range(2)]
    for h in range(2):
        k.memset("dve", gsc[h][:], NEG, [gsc[h]])
        k.memset("dve", selb[h][:], 0.0, [selb[h]])
    raw_xs = S.sb("raw_xs", [128, 515]); raw_bc = S.sb("raw_bc", [128, 515])
    k.memset("dve", raw_xs[:, 0:3], 0.0, [raw_xs]); k.memset("dve", raw_bc[:, 0:3], 0.0, [raw_bc])
    ST = S.sb("ssd_ST", [64, 128])
    STb = [S.sb("ssd_STb%d" % i, [64, 128], BF16) for i in range(2)]
    k.memset("dve", ST[:], 0.0, [ST]); k.memset("dve", STb[0][:], 0.0, [STb[0]])
    Rr = S.sb("ret_R", [128, 64])
    Rb = [S.sb("ret_Rb%d" % i, [128, 64], BF16) for i in range(2)]
    k.memset("dve", Rr[:], 0.0, [Rr]); k.memset("dve", Rb[0][:], 0.0, [Rb[0]])
    Hs = S.sb("hg_S", [128, 64])
    Hb = [S.sb("hg_Sb%d" % i, [128, 64], BF16) for i in range(3)]
    k.memset("dve", Hs[:], 0.0, [Hs]); k.memset("dve", Hb[0][:], 0.0, [Hb[0]])
    st_i = {"ssd": 0, "ret": 0, "hg": 0}

    hTr = S.sb("hTr", [128, 8, 512])
    xt = [S.sb("xt%d" % i, [128, 1024]) for i in range(2)]
    wdt = S.sb("wdt", [128, 8, 2])
    k.load(wdt[:], wt_d.rearrange("(c p) n -> p c n", p=128)[:, :, 1024:1026], wdt)
    k.cp("pool", wT[:, :, 1024:1026], wdt[:], [wdt], [wT])
    for c in range(8):
        k.load(xt[0][:, 0:NF], wf_d[c * 128:(c + 1) * 128, :], xt[0])
        k.cp("act", wF[:, c, :], xt[0][:, 0:NF], [xt[0]], [wF])
        k.load(xt[1][:, :], wt_d[c * 128:(c + 1) * 128, 0:1024], xt[1])
        k.cp("dve", wT[:, c, 0:512], xt[1][:, 0:512], [xt[1]], [wT])
        k.cp("pool", wT[:, c, 512:1024], xt[1][:, 512:1024], [xt[1]], [wT])
    mqk = S.sb("mqk", [128, NR])
    kpart = [S.sb("kpart%d" % h, [64, 8]) for h in range(2)]
    hT = S.sb("hT", [128, 8, 512], BF16)
    st12 = S.sb("st12", [128, 12]); mv = S.sb("mv", [128, 2]); rstd = S.sb("rstd", [128, 1])
    fA = S.sb("fA", [128, 512]); fB = S.sb("fB", [128, 512]); fC = S.sb("fC", [128, 512]); fD = S.sb("fD", [128, 512])
    hq = S.sb("hq", [128, 512])
    cA = S.sb("cA", [128, 512]); cB = S.sb("cB", [128, 512])
    xsT = S.sb("xsT", [128, 512], BF16); bcT = S.sb("bcT", [128, 512], BF16); cTb = S.sb("cTb", [64, 512], BF16)
    hqt = S.sb("hqt", [128, 512], BF16); hkt = S.sb("hkt", [128, 512], BF16); hkh = S.sb("hkh", [128, 512], BF16)
    hdec = S.sb("hdec", [128, 8])
    rqT = S.sb("rqT", [128, 512], BF16); rqTd = S.sb("rqTd", [128, 512], BF16); rkT = S.sb("rkT", [128, 512], BF16)
    mqf = [S.sb("mqf%d" % h, [64, 512]) for h in range(2)]
    mqa = [S.sb("mqa%d" % h, [64, 512], BF16) for h in range(2)]
    Qaug = [S.sb("Qaug%d" % h, [96, 512], BF16) for h in range(2)]
    Pb = [S.sb("Pb%d" % i, [128, 512], BF16) for i in range(3)]
    oTs = [S.sb("oTs%d" % h, [65, 512]) for h in range(2)]
    gates = [S.sb("gates%d" % i, [128, 512], BF16) for i in range(4)]
    gtmp = S.sb("gtmp", [128, 512])
    hi_t = [S.sb("hi%d" % i, [128, 128], BF16) for i in range(4)]
    rkd_t = [S.sb("rkd%d" % i, [128, 128], BF16) for i in range(4)]
    rv_t = [S.sb("rv%d" % i, [128, 128], BF16) for i in range(4)]
    dt4 = S.sb("dt4", [128, 8]); a4 = S.sb("a4", [128, 8])
    dtb4 = S.sb("dtb4", [128, 8]); A4 = S.sb("A4", [128, 8])
    for j_ in range(4):
        k.cp("dve", dtb4[:, 2 * j_:2 * j_ + 2], pcol(P_DTB, 2), [prm], [dtb4])
        k.cp("dve", A4[:, 2 * j_:2 * j_ + 2], Abc[:], [Abc], [A4])

    class _V:
        def __init__(self, buf, j):
            self.buf, self.j = buf, j

        def __getitem__(self, idx):
            return self.buf.t[:, 2 * self.j:2 * self.j + 2][idx]
    dt_t = [_V(dt4, i) for i in range(4)]
    a_t = [_V(a4, i) for i in range(4)]
    ycat = [S.sb("ycat%d" % i, [128, 512], BF16) for i in range(4)]
    sm = S.sb("sm", [128, 16]); sm_d = S.sb("sm_d", [128, 16]); sm_f = S.sb("sm_f", [128, 16]); sm_g = S.sb("sm_g", [128, 16])
    st12r = S.sb("st12r", [128, 12]); mvr = S.sb("mvr", [128, 2]); rstdr = S.sb("rstdr", [128, 1])
    yn_e = S.sb("yn_e", [128, 128]); yn_f = S.sb("yn_f", [128, 128]); yn_g = S.sb("yn_g", [128, 128]); sq_f = S.sb("sq_f", [128, 128])
    xs_tok = S.sb("xs_tok", [128, 128], BF16); xdt = S.sb("xdt", [128, 128], BF16); xdd = S.sb("xdd", [128, 128], BF16)
    b_tok = S.sb("b_tok", [128, 64], BF16)
    eac = S.sb("eac", [128, 4]); ecd = S.sb("ecd", [64, 2])
    segl = S.sb("segl", [128, 2, 128]); LT = S.sb("LT", [128, 2, 128]); CBm = S.sb("CBm", [128, 128])
    MT = S.sb("MT", [128, 2, 128], BF16)
    t1 = S.sb("t1", [128, 128]); yv = S.sb("yv", [128, 128]); gy = S.sb("gy", [128, 128]); sq = t1
    rMT = S.sb("rMT", [128, 2, 128], BF16); hAT = S.sb("hAT", [128, 2, 128], BF16)
    khat_tok = S.sb("khat_tok", [128, 128], BF16)
    m8 = S.sb("m8", [128, 8]); negm = S.sb("negm", [128, 2])

    pA = [S.ps("pA%d" % i, [128, 512]) for i in range(2)]
    pB = [S.ps("pB%d" % i, [128, 512]) for i in range(2)]
    pTb = S.ps("pT", [128, 8, 128], BF16)
    pD_t = S.ps("pD", [128, 512])
    pE_t = S.ps("pE", [128, 512])
    pF_t = S.ps("pF", [128, 512])
    pD_dt = Buf("pD_dt", pD_t[:, 0:8], lock=pD_t.lock); pD_cum = Buf("pD_cum", pD_t[:, 8:16], lock=pD_t.lock); pD_g = Buf("pD_g", pD_t[:, 16:64], lock=pD_t.lock)
    pD_fin = Buf("pD_fin", pD_t[:, 64:196], lock=pD_t.lock)
    pA1v = Buf("pA1v", pA[1][:, :], lock=pA[1].lock)
    pB0_s = Buf("pB0_s", pB[0][:, 0:128], lock=pB[0].lock); pB0_y = Buf("pB0_y", pB[0][:, 128:256], lock=pB[0].lock)
    pB0_u = Buf("pB0_u", pB[0][:, 256:384], lock=pB[0].lock)
    pB1_s = Buf("pB1_s", pB[1][:, 0:128], lock=pB[1].lock); pB1_y = Buf("pB1_y", pB[1][:, 128:256], lock=pB[1].lock)
    pB1_u = Buf("pB1_u", pB[1][:, 256:384], lock=pB[1].lock)
    pE_w = Buf("pE_w", pE_t[:, :], lock=pE_t.lock); pF_w = Buf("pF_w", pF_t[:, :], lock=pF_t.lock)
    pE_seg = Buf("pE_seg", pE_t[:, 0:256], lock=pE_t.lock); pE_sc = Buf("pE_sc", pE_t[:, 256:384], lock=pE_t.lock)
    pF_y = Buf("pF_y", pF_t[:, 0:128], lock=pF_t.lock); pF_yo = Buf("pF_yo", pF_t[:, 128:256], lock=pF_t.lock); pF_st = Buf("pF_st", pF_t[:, 256:384], lock=pF_t.lock)
    pF_u = Buf("pF_u", pF_t[:, 384:512], lock=pF_t.lock)

    src_b = Buf("src_d", src_d)
    ycat_b = Buf("ycat_d", None)

    def rstd_from(var_ap, eps, out_ap, rd, wr):
        k.ts("dve", out_ap, var_ap, eps, None, ALU.add, None, rd, wr)
        k.act(out_ap, out_ap, AF.Ln, wr, wr)
        k.act(out_ap, out_ap, AF.Exp, wr, wr, scale=-0.5)

    def stage_a(J):
        if src_is_x:
            for tl in range(4):
                i = 4 * J + tl
                xb = xt[i % 2]
                k.load(xb[:], src_d[i * 128:(i + 1) * 128, :], xb, reads=[src_b])
                S.op("dve", lambda e, xb=xb: (e.bn_stats(out=st12[:, 0:6], in_=xb[:, 0:512]),
                                              e.bn_stats(out=st12[:, 6:12], in_=xb[:, 512:1024]))[1], [xb], [st12])
                S.op("dve", lambda e: e.bn_aggr(out=mv[:], in_=st12[:]), [st12], [mv])
                rstd_from(mv[:, 1:2], LN_EPS, rstd[:], [mv], [rstd])
                k.ts("dve", xb[:], xb[:], mv[:, 0:1], rstd[:], ALU.subtract, ALU.mult, [xb, mv, rstd], [xb])
                k.tt("pool", xb[:], xb[:], pcol(P_LNG, 1024), ALU.mult, [xb, prm], [xb])
                k.tt("pool", xb[:], xb[:], pcol(P_LNB, 1024), ALU.add, [xb, prm], [xb])
                for hf in range(2):
                    k.trs([(pA[0][:, c * 128:(c + 1) * 128], xb[:, (4 * hf + c) * 128:(4 * hf + c + 1) * 128], identf[:])
                           for c in range(4)], [xb, identf], [pA[0]])
                    k.cp("act", hT[:, 4 * hf:4 * hf + 4, tl * 128:(tl + 1) * 128],
                         pA[0][:, :].rearrange("p (c t) -> p c t", c=4), [pA[0]], [hT])
                    k.cp("dve", hTr[:, 4 * hf:4 * hf + 4, tl * 128:(tl + 1) * 128],
                         pA[0][:, :].rearrange("p (c t) -> p c t", c=4), [pA[0]], [hTr])
        else:
            k.load(hTr[:], src_d[:, :, J * 512:(J + 1) * 512], hTr, reads=[src_b])
            k.cp("act", hT[:, 0:4, :], hTr[:, 0:4, :], [hTr], [hT])
            k.cp("pool", hT[:, 4:8, :], hTr[:, 4:8, :], [hTr], [hT])

    stage_a(0)
    for J in range(NJ):
        def chain_b():
            def fgroup(gi, off, M):
                p = pA[gi % 2]
                k.mms([(p[0:M, :], wF[:, c, off:off + M], hT[:, c, :], c == 0, c == 7) for c in range(8)], [wF, hT], [p])
                return p
            if BLIM > 0:
                p = fgroup(0, 0, 128); k.cp("act", raw_xs[:, 3:515], p[:, :], [p], [raw_xs])
            if BLIM > 1:
                p = fgroup(1, 128, 128); k.cp("act", raw_bc[:, 3:515], p[:, :], [p], [raw_bc])
            if BLIM > 2:
                p = fgroup(2, 256, 128); k.cp("act", hq[:], p[:, :], [p], [hq])
            if BLIM > 3:
                p = fgroup(3, 384, 128); k.act(fA[:], p[:, :], AF.Exp, [p], [fA], scale=-1.0)
            if BLIM > 4:
                p = fgroup(4, 512, 128)
                k.cp("act", rqT[:], p[:, :], [p], [rqT])
                k.tt("dve", rqTd[:], p[:, :], qdec[:], ALU.mult, [p, qdec], [rqTd])
            if BLIM > 5:
                p = fgroup(5, 640, 128); k.act(rkT[:], p[:, :], AF.Copy, [p], [rkT], scale=0.125)
            for tl in range(4 if BLIM > 6 else 0):
                ts_ = slice(tl * 128, (tl + 1) * 128)
                k.mms([(pB[0][:, 0:NR], hTr[:, c, ts_], wR[:, c, :], c == 0, c == 7) for c in range(8)],
                      [hTr, wR], [pB[0]])
                k.cp("act", mqk[:], pB[0][:, 0:NR], [pB[0]], [mqk])
                k.trs([(pB[1][0:64, j * 128:(j + 1) * 128], mqk[:, j * 64:(j + 1) * 64], identf[:]) for j in range(4)],
                      [mqk, identf], [pB[1]])
                for h in range(2):
                    k.act(mqf[h][:, ts_], pB[1][0:64, h * 128:(h + 1) * 128], AF.Copy, [pB[1]], [mqf[h]], scale=0.125)
                    kp = pB[1][0:64, (2 + h) * 128:(3 + h) * 128]
                    k.cp("act", Kaug[h][0:64, J * 512 + tl * 128:J * 512 + (tl + 1) * 128], kp, [pB[1]], [Kaug[h]])
                    S.op("dve", lambda e, kp=kp, h=h, tl=tl: e.tensor_reduce(out=kpart[h][:, tl:tl + 1], in_=kp, axis=AX.X, op=ALU.add),
                         [pB[1]], [kpart[h]])
                    S.op("dve", lambda e, kp=kp, h=h, tl=tl: e.tensor_reduce(out=kpart[h][:, 4 + tl:5 + tl], in_=kp, axis=AX.X, op=ALU.max,
                                                                             apply_absolute_value=True), [pB[1]], [kpart[h]])
            for h in range(2 if BLIM > 6 else 0):
                k.stt(mqa[h][:, :], mqf[h][:, :], -1.0, mqf[h][:, :], ALU.mult, ALU.max, [mqf[h]], [mqa[h]])
                for bb in range(2):
                    k.tt("dve", ksum[h][:, 2 * J + bb:2 * J + bb + 1], kpart[h][:, 2 * bb:2 * bb + 1], kpart[h][:, 2 * bb + 1:2 * bb + 2],
                         ALU.add, [kpart[h]], [ksum[h]])
                S.op("dve", lambda e, h=h: e.tensor_reduce(out=kabs[h][:, 1:2], in_=kpart[h][:, 4:8], axis=AX.X, op=ALU.max),
                     [kpart[h]], [kabs[h]])
                k.tt("dve", kabs[h][:, 0:1], kabs[h][:, 0:1], kabs[h][:, 1:2], ALU.max, [kabs[h]], [kabs[h]])
                k.cp("dve", kabsb[h][:, :], kabs[h][:, 0:1], [kabs[h]], [kabsb[h]])

        def chain_c():
            for tl in range(4):
                i = 4 * J + tl
                lst = []
                pG_, pV_ = (pE_w, pF_w)
                for c in range(8):
                    lst.append((pG_[:, :], hT[:, c, tl * 128:(tl + 1) * 128], wT[:, c, 0:512], c == 0, c == 7))
                for c in range(8):
                    lst.append((pV_[:, :], hT[:, c, tl * 128:(tl + 1) * 128], wT[:, c, 512:1024], c == 0, c == 7))
                for c in range(8):
                    lst.append((pD_dt[:, 2 * tl:2 * tl + 2], hT[:, c, tl * 128:(tl + 1) * 128], wT[:, c, 1024:1026], c == 0, c == 7))
                k.mms(lst, [hT, wT], [pG_, pV_, pD_dt])
                k.act(gtmp[:], pG_[:, :], AF.Exp, [pG_], [gtmp], scale=-1.0)
                k.act(gtmp[:], gtmp[:], AF.Ln, [gtmp], [gtmp], bias=1.0)
                k.act(gtmp[:], gtmp[:], AF.Exp, [gtmp], [gtmp], scale=-1.0)
                k.tt("dve", gates[tl][:], pG_[:, :], gtmp[:], ALU.mult, [pG_, gtmp], [gates[tl]])
                k.cp("act", hi_t[tl][:], pV_[:, 0:128], [pV_], [hi_t[tl]])
                for h in range(2):
                    k.ts("dve", rkd_t[tl][:, h * 64:(h + 1) * 64], pV_[:, 128 + h * 64:128 + (h + 1) * 64], rkdec[:, h:h + 1], None,
                         ALU.mult, None, [pV_, rkdec], [rkd_t[tl]])
                k.cp("act", rv_t[tl][:], pV_[:, 256:384], [pV_], [rv_t[tl]])
                k.cp("dve", Vaug[:, i, :, 0:64], pV_[:, 384:512].rearrange("p (h d) -> p h d", h=2), [pV_], [Vaug])
            k.tt("dve", sm[:, 0:8].rearrange("p (a b) -> p a b", b=2), pD_dt[:, 0:8].rearrange("p (a b) -> p a b", b=2),
                 dtb4[:].rearrange("p (a b) -> p a b", b=2),
                 ALU.add, [pD_dt, dtb4], [sm])
            k.stt(sm[:, 8:16], sm[:, 0:8], -1.0, sm[:, 0:8], ALU.mult, ALU.max, [sm], [sm])
            k.act(sm[:, 8:16], sm[:, 8:16], AF.Exp, [sm], [sm], scale=-1.0)
            k.act(sm[:, 8:16], sm[:, 8:16], AF.Ln, [sm], [sm], bias=1.0)
            k.stt(dt4[:], sm[:, 0:8], 0.0, sm[:, 8:16], ALU.max, ALU.add, [sm], [dt4])
            k.tt("dve", a4[:], dt4[:], A4[:], ALU.mult, [dt4, A4], [a4])

        S.il.run([fn for fn, c_ in ((chain_b, 'b'), (chain_c, 'c')) if c_ in STAGES])

        def chain_d():
            def conv_silu(raw, cw_off, cb_off, outT, outbuf):
                k.ts("dve", cA[:], raw[:, 0:512], pcol(cw_off), pcol(cb_off), ALU.mult, ALU.add, [raw, prm], [cA])
                for j in range(1, 4):
                    k.stt(cA[:], raw[:, j:j + 512], pcol(cw_off + j), cA[:], ALU.mult, ALU.add, [raw, prm, cA], [cA])
                k.cp("pool", raw[:, 0:3], raw[:, 512:515], [raw], [raw])
                k.act(cB[:], cA[:], AF.Exp, [cA], [cB], scale=-1.0)
                k.act(cB[:], cB[:], AF.Ln, [cB], [cB], bias=1.0)
                k.act(cB[:], cB[:], AF.Exp, [cB], [cB], scale=-1.0)
                k.tt("dve", outT[:], cA[:], cB[:], ALU.mult, [cA, cB], [outbuf])
            conv_silu(raw_xs, P_CWX, P_CBX, xsT, xsT)
            conv_silu(raw_bc, P_CWB, P_CBB, bcT, bcT)
            k.mm(pE_w[0:64, :], shsel[:], bcT[:], [shsel, bcT], [pE_w])
            k.cp("act", cTb[:], pE_w[0:64, :], [pE_w], [cTb])

            for tl in range(4):
                ts_ = slice(tl * 128, (tl + 1) * 128)
                k.trs([(pTb[:, 0, :], xsT[:, ts_], identb[:]), (pTb[:, 1, 0:64], bcT[0:64, ts_], identb[0:64, 0:64])],
                      [xsT, bcT, identb], [pTb])
                k.cp("act", xs_tok[:], pTb[:, 0, :], [pTb], [xs_tok])
                for h in range(2):
                    k.ts("dve", xdt[:, h * 64:(h + 1) * 64], pTb[:, 0, h * 64:(h + 1) * 64], dt_t[tl][:, h:h + 1], None, ALU.mult, None,
                         [pTb, dt4], [xdt])
                k.cp("act", b_tok[:], pTb[:, 1, 0:64], [pTb], [b_tok])
                k.mms([(pD_cum[:, 0:2], triU[:], a_t[tl][:], True, True),
                       (pD_cum[:, 2:4], triLs[:], a_t[tl][:], True, True),
                       (pD_cum[0:64, 4:6], ones[:, 0:64], a_t[tl][:], True, True)], [triU, triLs, ones, a_t[tl]], [pD_cum])
                k.act(eac[:], pD_cum[:, 0:4], AF.Exp, [pD_cum], [eac])
                k.act(ecd[:], pD_cum[0:64, 4:6], AF.Exp, [pD_cum], [ecd])
                for h in range(2):
                    k.ts("dve", xdd[:, h * 64:(h + 1) * 64], xdt[:, h * 64:(h + 1) * 64], eac[:, 2 + h:3 + h], None, ALU.mult, None,
                         [xdt, eac], [xdd])
                for h in range(2):
                    k.ts("dve", segl[:, h, :], triLs[:], a_t[tl][:, h:h + 1], None, ALU.mult, None, [triLs, a_t[tl]], [segl])
                k.mms([(pE_seg[:, h * 128:(h + 1) * 128], segl[:, h, :], triU[:], True, True) for h in range(2)], [segl, triU], [pE_seg])
                k.act(LT[:].rearrange("p a b -> p (a b)"), pE_seg[:, :], AF.Exp, [pE_seg], [LT])
                k.mm(pE_sc[:, :], bcT[0:64, ts_], cTb[:, ts_], [bcT, cTb], [pE_sc])
                k.tt("dve", CBm[:], pE_sc[:, :], triU[:], ALU.mult, [pE_sc, triU], [CBm])
                for h in range(2):
                    k.tt("dve", MT[:, h, :], LT[:, h, :], CBm[:], ALU.mult, [LT, CBm], [MT])
                sb_old = STb[st_i["ssd"] % 2]; sb_new = STb[(st_i["ssd"] + 1) % 2]; st_i["ssd"] += 1
                k.mms([(pF_y[:, h * 64:(h + 1) * 64], MT[:, h, :], xdt[:, h * 64:(h + 1) * 64], True, True) for h in range(2)] +
                      [(pF_yo[:, h * 64:(h + 1) * 64], cTb[:, ts_], sb_old[:, h * 64:(h + 1) * 64], True, True) for h in range(2)] +
                      [(pF_st[0:64, h * 64:(h + 1) * 64], b_tok[:], xdd[:, h * 64:(h + 1) * 64], True, True) for h in range(2)],
                      [MT, xdt, cTb, sb_old, b_tok, xdd], [pF_y, pF_yo, pF_st])
                for h in range(2):
                    k.stt(ST[:, h * 64:(h + 1) * 64], ST[:, h * 64:(h + 1) * 64], ecd[:, h:h + 1], pF_st[0:64, h * 64:(h + 1) * 64],
                          ALU.mult, ALU.add, [ST, ecd, pF_st], [ST])
                k.cp("act", sb_new[:], ST[:], [ST], [sb_new])
                for h in range(2):
                    k.act(t1[:, h * 64:(h + 1) * 64], pF_yo[:, h * 64:(h + 1) * 64], AF.Identity, [pF_yo, eac], [t1], scale=eac[:, h:h + 1])
                    k.stt(t1[:, h * 64:(h + 1) * 64], xs_tok[:, h * 64:(h + 1) * 64], pcol(P_DSK + h), t1[:, h * 64:(h + 1) * 64],
                          ALU.mult, ALU.add, [xs_tok, prm, t1], [t1])
                k.tt("dve", yv[:], pF_y[:, :], t1[:], ALU.add, [pF_y, t1], [yv])
                k.tt("dve", gy[:], yv[:], gates[tl][:, 0:128], ALU.mult, [yv, gates[tl]], [gy])
                k.act(sq[:], gy[:], AF.Square, [gy], [sq, sm_d], accum=sm_d[:, 4:5])
                k.ts("dve", sm_d[:, 4:5], sm_d[:, 4:5], 1.0 / 128.0, RMS_EPS, ALU.mult, ALU.add, [sm_d], [sm_d])
                k.act(sm_d[:, 4:5], sm_d[:, 4:5], AF.Ln, [sm_d], [sm_d])
                k.act(sm_d[:, 4:5], sm_d[:, 4:5], AF.Exp, [sm_d], [sm_d], scale=-0.5)
                k.stt(ycat[tl][:, 0:128], gy[:], sm_d[:, 4:5], pcol(P_SNW, 128), ALU.mult, ALU.mult, [gy, sm_d, prm], [ycat[tl]])

        def chain_e():
            for tl in range(4):
                ts_ = slice(tl * 128, (tl + 1) * 128)
                k.mms([(pB0_s[:, 0:128], rkT[0:64, ts_], rqT[0:64, ts_], True, True),
                       (pA1v[:, 0:128], rkT[64:128, ts_], rqT[64:128, ts_], True, True)], [rkT, rqT], [pB0_s, pA1v])
                k.tt("dve", rMT[:, 0, :], pB0_s[:, 0:128], rdec[:, 0, :], ALU.mult, [pB0_s, rdec], [rMT])
                k.tt("dve", rMT[:, 1, :], pA1v[:, 0:128], rdec[:, 1, :], ALU.mult, [pA1v, rdec], [rMT])
                rb_old = Rb[st_i["ret"] % 2]; rb_new = Rb[(st_i["ret"] + 1) % 2]; st_i["ret"] += 1
                lst = []
                for h in range(2):
                    hs = slice(h * 64, (h + 1) * 64)
                    lst.append((pB0_y[:, hs], rMT[:, h, :], rv_t[tl][:, hs], True, False))
                    lst.append((pB0_y[:, hs], rqTd[hs, ts_], rb_old[hs, :], False, True))
                for h in range(2):
                    hs = slice(h * 64, (h + 1) * 64)
                    lst.append((pB0_u[hs, 0:64], rkd_t[tl][:, hs], rv_t[tl][:, hs], True, True))
                k.mms(lst, [rMT, rv_t[tl], rqTd, rb_old, rkd_t[tl]], [pB0_y, pB0_u])
                k.stt(Rr[:], Rr[:], rcg[:], pB0_u[:, 0:64], ALU.mult, ALU.add, [Rr, rcg, pB0_u], [Rr])
                k.cp("act", rb_new[:], Rr[:], [Rr], [rb_new])
                for h in range(2):
                    hs = slice(h * 64, (h + 1) * 64)
                    S.op("dve", lambda e, hs=hs: e.bn_stats(out=st12r[:, 0:6], in_=pB0_y[:, hs]), [pB0_y], [st12r])
                    S.op("dve", lambda e: e.bn_aggr(out=mvr[:], in_=st12r[:, 0:6]), [st12r], [mvr])
                    rstd_from(mvr[:, 1:2], LN_EPS, rstdr[:], [mvr], [rstdr])
                    k.ts("dve", yn_e[:, hs], pB0_y[:, hs], mvr[:, 0:1], rstdr[:], ALU.subtract, ALU.mult, [pB0_y, mvr, rstdr], [yn_e])
                k.tt("dve", ycat[tl][:, 256:384], yn_e[:], gates[tl][:, 256:384], ALU.mult, [yn_e, gates[tl]], [ycat[tl]])

        def chain_f():
            k.act(fA[:], fA[:], AF.Ln, [fA], [fA], bias=1.0)
            k.act(fA[:], fA[:], AF.Exp, [fA], [fA], scale=-1.0)
            k.ts("dve", fA[:], fA[:], oml[:], lb[:], ALU.mult, ALU.add, [fA, oml, lb], [fA])
            k.act(fB[:], fA[:], AF.Ln, [fA], [fB])
            k.ts("dve", fC[:], fA[:], -1.0, 1.0, ALU.mult, ALU.add, [fA], [fC])
            S.op("dve", lambda e: e.tensor_tensor_scan(out=fD[:], data0=rst[:].rearrange("p a b -> p (a b)"), data1=fB[:], initial=0.0,
                                                       op0=ALU.mult, op1=ALU.add), [rst, fB], [fD])
            k.act(fA[:], fD[:], AF.Exp, [fD], [fA])
            k.cp("dve", hdec[:], fA[:].rearrange("p (a b) -> p a b", b=64)[:, :, 63], [fA], [hdec])
            k.tt("dve", hqt[:], hq[:], fA[:], ALU.mult, [hq, fA], [hqt])
            k.act(fB[:], fD[:], AF.Exp, [fD], [fB], scale=-1.0)
            k.tt("dve", hkt[:], fC[:], fB[:], ALU.mult, [fC, fB], [hkt])
            for c in range(8):
                k.act(fA[:, c * 64:(c + 1) * 64], fD[:, c * 64:(c + 1) * 64], AF.Exp, [fD], [fA], scale=-1.0,
                      bias=fD[:, c * 64 + 63:c * 64 + 64])
            k.tt("dve", hkh[:], fC[:], fA[:], ALU.mult, [fC, fA], [hkh])
            for tl in range(4):
                ts_ = slice(tl * 128, (tl + 1) * 128)
                k.trs([(pTb[:, 2, :], hkh[:, ts_], identb[:])], [hkh, identb], [pTb])
                k.cp("act", khat_tok[:], pTb[:, 2, :], [pTb], [khat_tok])
                k.mms([(pB1_s[:, 0:128], hkt[0:64, ts_], hqt[0:64, ts_], True, True),
                       (pA1v[:, 128:256], hkt[64:128, ts_], hqt[64:128, ts_], True, True)], [hkt, hqt], [pB1_s, pA1v])
                k.tt("dve", hAT[:, 0, :], pB1_s[:, 0:128], hmask[:, 0, :], ALU.mult, [pB1_s, hmask], [hAT])
                k.tt("dve", hAT[:, 1, :], pA1v[:, 128:256], hmask[:, 1, :], ALU.mult, [pA1v, hmask], [hAT])
                s0 = Hb[st_i["hg"] % 3]; s1 = Hb[(st_i["hg"] + 1) % 3]; s2 = Hb[(st_i["hg"] + 2) % 3]; st_i["hg"] += 2
                k.mms([(pB1_u[h * 64:(h + 1) * 64, 0:64], khat_tok[0:64, h * 64:(h + 1) * 64], hi_t[tl][0:64, h * 64:(h + 1) * 64], True, True)
                       for h in range(2)] +
                      [(pA1v[h * 64:(h + 1) * 64, 256:320], khat_tok[64:128, h * 64:(h + 1) * 64], hi_t[tl][64:128, h * 64:(h + 1) * 64], True, True)
                       for h in range(2)], [khat_tok, hi_t[tl]], [pB1_u, pA1v])
                k.stt(Hs[:], Hs[:], hdec[:, 2 * tl:2 * tl + 1], pB1_u[:, 0:64], ALU.mult, ALU.add, [Hs, hdec, pB1_u], [Hs])
                k.cp("act", s1[:], Hs[:], [Hs], [s1])
                k.stt(Hs[:], Hs[:], hdec[:, 2 * tl + 1:2 * tl + 2], pA1v[:, 256:320], ALU.mult, ALU.add, [Hs, hdec, pA1v], [Hs])
                k.cp("act", s2[:], Hs[:], [Hs], [s2])
                lst = []
                for h in range(2):
                    hs = slice(h * 64, (h + 1) * 64)
                    lst.append((pB1_y[:, hs], hAT[:, h, :], hi_t[tl][:, hs], True, False))
                    lst.append((pB1_y[0:64, hs], hqt[hs, tl * 128:tl * 128 + 64], s0[hs, :], False, False))
                    lst.append((pB1_y[64:128, hs], hqt[hs, tl * 128 + 64:tl * 128 + 128], s1[hs, :], False, True))
                k.mms(lst, [hAT, hi_t[tl], hqt, s0, s1], [pB1_y])
                for h in range(2):
                    hs = slice(h * 64, (h + 1) * 64)
                    k.act(sq_f[:, hs], pB1_y[:, hs], AF.Square, [pB1_y], [sq_f, sm_f], accum=sm_f[:, 6 + h:7 + h])
                k.ts("dve", sm_f[:, 6:8], sm_f[:, 6:8], 1.0 / 64.0, RMS_EPS, ALU.mult, ALU.add, [sm_f], [sm_f])
                k.act(sm_f[:, 6:8], sm_f[:, 6:8], AF.Ln, [sm_f], [sm_f])
                k.act(sm_f[:, 6:8], sm_f[:, 6:8], AF.Exp, [sm_f], [sm_f], scale=-0.5)
                for h in range(2):
                    hs = slice(h * 64, (h + 1) * 64)
                    k.stt(yn_f[:, hs], pB1_y[:, hs], sm_f[:, 6 + h:7 + h], prm[:, P_HNW + h * 64:P_HNW + (h + 1) * 64], ALU.mult, ALU.mult,
                          [pB1_y, sm_f, prm], [yn_f])
                k.tt("dve", ycat[tl][:, 128:256], yn_f[:], gates[tl][:, 128:256], ALU.mult, [yn_f, gates[tl]], [ycat[tl]])

        def chain_g():
            for h in range(2):
                k.cp("pool", Qaug[h][0:64, :], mqf[h][:, :], [mqf[h]], [Qaug[h]])
            for tl in range(4):
                own = 2 * J + tl // 2
                ts_ = slice(tl * 128, (tl + 1) * 128)
                for h in range(2):
                    lst = [(pD_g[:, 32:33], mqa[h][:, ts_], kabsb[h][:, :], True, True)]
                    if own > 0:
                        lst.append((pD_g[:, 0:own], mqf[h][:, ts_], ksum[h][:, 0:own], True, True))
                    k.mms(lst, [mqa[h], kabsb[h], mqf[h], ksum[h]], [pD_g])
                    k.ts("dve", negm[:, h:h + 1], pD_g[:, 32:33], stl[:, h, tl:tl + 1], -1.0, ALU.add, ALU.mult, [pD_g, stl], [negm])
                    k.ts("dve", negm[:, h:h + 1], negm[:, h:h + 1], BIG, None, ALU.add, None, [negm], [negm])
                    if own > 0:
                        k.cp("dve", gsc[h][:, 0:own], pD_g[:, 0:own], [pD_g], [gsc[h]])
                        S.op("dve", lambda e, h=h: e.max(out=m8[:], in_=gsc[h][:]), [gsc[h]], [m8])
                        k.ts("dve", selb[h][:, 0:own], gsc[h][:, 0:own], m8[:, 2:3], None, ALU.is_ge, None, [gsc[h], m8], [selb[h]])
                    k.memset("dve", selb[h][:, own:own + 1], 1.0, [selb[h]])
                    k.ts("dve", gtok[:, 64:96], selb[h][:], negm[:, h:h + 1], -BIG, ALU.mult, ALU.add, [selb[h], negm], [gtok])
                    k.trs([(pTb[0:96, 3, :], gtok[:], identb[:])], [gtok, identb], [pTb])
                    k.cp("act", Qaug[h][64:96, ts_], pTb[64:96, 3, :], [pTb], [Qaug[h]])
        def moba_attn():
            nkt = 4 * J + 4
            steps = [(kt, h) for kt in range(nkt) for h in range(2)]
            pqs = [pA[0], pA[1], pE_w]
            LA = 2

            def qk(si):
                kt, h = steps[si]
                dl = kt - 4 * J
                c0 = 0 if dl < 0 else 128 * dl
                pq = pqs[si % 3]
                k.mm(pq[:, c0:512], Kaug[h][:, kt * 128:(kt + 1) * 128], Qaug[h][:, c0:512], [Kaug[h], Qaug[h]], [pq])
                if dl >= 0:
                    k.tt("dve", pq[:, c0:c0 + 128], pq[:, c0:c0 + 128], maskneg[:], ALU.add, [pq, maskneg], [pq])

            def pv(si):
                kt, h = steps[si]
                dl = kt - 4 * J
                c0 = 0 if dl < 0 else 128 * dl
                pq = pqs[si % 3]; Pt = Pb[si % 3]
                k.act(Pt[:, c0:512], pq[:, c0:512], AF.Exp, [pq, bcol], [Pt], bias=bcol[:, h, dl + 60:dl + 61])
                k.mm(pB[h][0:65, c0:512], Vaug[:, kt, h, :], Pt[:, c0:512], [Vaug, Pt], [pB[h]], start=(kt == 0), stop=(kt == nkt - 1))
            n = len(steps)
            for si in range(n + LA):
                if si < n:
                    qk(si)
                if si - LA >= 0:
                    pv(si - LA)
            for h in range(2):
                k.cp("act", oTs[h][:], pB[h][0:65, :], [pB[h]], [oTs[h]])
            for tl in range(4):
                ts_ = slice(tl * 128, (tl + 1) * 128)
                k.trs([(pD_fin[:, h * 65:h * 65 + 65], oTs[h][0:65, ts_], identf[0:65, 0:65]) for h in range(2)],
                      [oTs[0], oTs[1], identf], [pD_fin])
                for h in range(2):
                    S.op("dve", lambda e, h=h: e.reciprocal(out=sm_g[:, 8 + h:9 + h], in_=pD_fin[:, h * 65 + 64:h * 65 + 65]), [pD_fin], [sm_g])
                for h in range(2):
                    hs = slice(h * 64, (h + 1) * 64)
                    k.ts("dve", yn_g[:, hs], pD_fin[:, h * 65:h * 65 + 64], sm_g[:, 8 + h:9 + h], None, ALU.mult, None, [pD_fin, sm_g], [yn_g])
                k.tt("dve", ycat[tl][:, 384:512], yn_g[:], gates[tl][:, 384:512], ALU.mult, [yn_g, gates[tl]], [ycat[tl]])

        chains_ = [fn for fn, c_ in ((chain_d, 'd'), (chain_e, 'e'), (chain_f, 'f'), (chain_g, 'g')) if c_ in STAGES]
        if J + 1 < NJ:
            chains_.append(lambda: stage_a(J + 1))
        S.il.run(chains_)
        if 'g' in STAGES:
            moba_attn()

        for _stage in ([0] if 'h' in STAGES else []):
            for tl in range(4):
                i = 4 * J + tl
                if callable(ycat_d):
                    k.store(ycat_d(i)[0], ycat[tl][:], ycat[tl], dstbuf=ycat_d(i)[1])
                else:
                    k.store(ycat_d[i * 128:(i + 1) * 128, :], ycat[tl][:], ycat[tl], dstbuf=ycat_b)
            if after_store is not None:
                after_store(J)


def build_M(T, layer, src_is_x):
    nc = bass.Bass("TRN2", target_bir_lowering=False)
    if src_is_x:
        src = nc.dram_tensor("src", [T, D_MODEL], F32, kind="ExternalInput").ap()
    else:
        src = nc.dram_tensor("src", [128, 8, T], F32, kind="ExternalInput").ap()
    prm = nc.dram_tensor("prm", [128, NPRM], F32, kind="ExternalInput").ap()
    wf = nc.dram_tensor("wf", [D_MODEL, NF], F32, kind="ExternalInput").ap()
    wt = nc.dram_tensor("wt", [D_MODEL, NT], F32, kind="ExternalInput").ap()
    wr = nc.dram_tensor("wr", [D_MODEL, NR], F32, kind="ExternalInput").ap()
    ycat = nc.dram_tensor("ycat", [T, 512], BF16, kind="ExternalOutput").ap()
    with contextlib.ExitStack() as st:
        S = Sched(nc, st)
        k = K(S)
        C = build_consts(S, k)
        m_phase(nc, S, k, C, T, layer, src, prm, wf, wt, wr, ycat, src_is_x)
        S.final_wait()
        S.emit()
    return nc


OFFS = np.cumsum((0, 256, 512, 4, 256, 256, 256, 256, 256, 256, 256, 256, 256, 256, 256, 256))


def core_weights(w_in_l, g):
    o = OFFS
    hp = slice(g * 128, (g + 1) * 128)

    def seg(i, sl):
        return w_in_l[:, o[i] + sl.start:o[i] + sl.stop]
    xbc = w_in_l[:, o[1]:o[2]]
    xs = xbc[:, g * 128:(g + 1) * 128]
    Bm = xbc[:, 256 + g * 64:256 + (g + 1) * 64]
    Cm = xbc[:, 384 + g * 64:384 + (g + 1) * 64]
    mq = seg(11, hp); mk = seg(12, hp)
    wf = np.concatenate([xs, Bm, Cm, seg(3, hp), seg(4, hp), seg(7, hp), seg(8, hp)], axis=1)
    wr = np.concatenate([mq, mk], axis=1)
    dt = w_in_l[:, o[2] + 2 * g:o[2] + 2 * g + 2]
    wt = np.concatenate([seg(0, hp), seg(6, hp), seg(10, hp), seg(14, hp),
                         seg(5, hp), seg(8, hp), seg(9, hp), seg(13, hp), dt], axis=1)
    return np.ascontiguousarray(wf, np.float32), np.ascontiguousarray(wt, np.float32), np.ascontiguousarray(wr, np.float32)


def core_params(inp, l, g, ln_g, ln_b):
    prm = np.zeros((128, NPRM), np.float32)
    bc = lambda v: np.broadcast_to(np.asarray(v, np.float32)[None, :], (128, len(v)))
    prm[:, P_LNG:P_LNG + 1024] = bc(ln_g)
    prm[:, P_LNB:P_LNB + 1024] = bc(ln_b)
    cw = inp["ssd_conv_w"][l]
    cb = inp["ssd_conv_b"][l]
    ch_xs = np.arange(g * 128, (g + 1) * 128)
    ch_bc = np.concatenate([256 + g * 64 + np.arange(64), 384 + g * 64 + np.arange(64)])
    prm[:, P_CWX:P_CWX + 4] = cw[:, ch_xs].T
    prm[:, P_CWB:P_CWB + 4] = cw[:, ch_bc].T
    prm[:, P_CBX] = cb[ch_xs]
    prm[:, P_CBB] = cb[ch_bc]
    hh = slice(2 * g, 2 * g + 2)
    prm[:, P_DTB:P_DTB + 2] = bc(inp["ssd_dt_bias"][l][hh])
    prm[:, P_ALOG:P_ALOG + 2] = bc(inp["ssd_a_log"][l][hh])
    prm[:, P_DSK:P_DSK + 2] = bc(inp["ssd_d"][l][hh])
    prm[:, P_SNW:P_SNW + 128] = bc(inp["ssd_norm_w"][l][g * 128:(g + 1) * 128])
    prm[:, P_HNW:P_HNW + 128] = bc(inp["hgrn_norm_w"][l][g * 128:(g + 1) * 128])
    prm[:, P_LBL:P_LBL + 2] = inp["hgrn_lb_logits"][:, g * 128:(g + 1) * 128].T
    heads = np.arange(2 * g, 2 * g + 2, dtype=np.float64)
    logg = np.log(1.0 - 2.0 ** (-5.0 - heads)).astype(np.float32)
    prm[0:64, P_RLGP] = logg[0]
    prm[64:128, P_RLGP] = logg[1]
    prm[:, P_RLGB:P_RLGB + 2] = bc(logg)
    slopes = (2.0 ** (-8.0 * (heads + 1.0) / 4.0)).astype(np.float32)
    prm[:, P_SLOPE:P_SLOPE + 2] = bc(slopes)
    return prm


def o_phase(nc, S, k, C, NTOK, layer, res_d, y0_d, y1_d, prmo_d, wo_d, out_d, hT_d, res_is_x):
    ident = C["identb"]
    ntile = NTOK // 128
    po = S.sb("prmo", [128, 4096])
    k.load(po[:], prmo_d, po)
    wo = S.sb("wo", [128, 8, 1024], BF16)
    NS = 4
    yt = [S.sb("o_yt%d" % i, [128, 1024], BF16) for i in range(NS)]
    yT = [S.sb("o_yT%d" % i, [128, 8, 128], BF16) for i in range(NS)]
    rt = [S.sb("o_rt%d" % i, [128, 1024]) for i in range(NS)]
    for c in range(8):
        k.load(rt[c % 2][:], wo_d[c * 128:(c + 1) * 128, :], rt[c % 2])
        k.cp("act" if c % 2 == 0 else "dve", wo[:, c, :], rt[c % 2][:], [rt[c % 2]], [wo])
    zt = [S.sb("o_zt%d" % i, [128, 1024]) for i in range(NS)]
    ot = [S.sb("o_ot%d" % i, [128, 1024]) for i in range(NS)]
    oT = [S.sb("o_oT%d" % i, [128, 8, 128]) for i in range(NS)]
    identf = C["identf"]
    st12 = [S.sb("o_st12%d" % i, [128, 12]) for i in range(NS)]
    mv = [S.sb("o_mv%d" % i, [128, 2]) for i in range(NS)]
    rstd = [S.sb("o_rstd%d" % i, [128, 1]) for i in range(NS)]
    pT = [S.ps("o_pT%d" % i, [128, 8, 128], BF16) for i in range(2)]
    pY = [[S.ps("o_pY%d_%d" % (i, hf), [128, 512]) for hf in range(2)] for i in range(2)]
    res_b = Buf("res_d", res_d); y_b = Buf("y_d", None); out_b = Buf("out_d", out_d)

    def ln(q, src, dst, dst_buf, goff, boff):
        S.op("dve", lambda e: (e.bn_stats(out=st12[q][:, 0:6], in_=src[:, 0:512]),
                               e.bn_stats(out=st12[q][:, 6:12], in_=src[:, 512:1024]))[1], [src], [st12[q]])
        S.op("dve", lambda e: e.bn_aggr(out=mv[q][:], in_=st12[q][:]), [st12[q]], [mv[q]])
        k.ts("dve", rstd[q][:], mv[q][:, 1:2], LN_EPS, None, ALU.add, None, [mv[q]], [rstd[q]])
        k.act(rstd[q][:], rstd[q][:], AF.Ln, [rstd[q]], [rstd[q]])
        k.act(rstd[q][:], rstd[q][:], AF.Exp, [rstd[q]], [rstd[q]], scale=-0.5)
        k.ts("dve", src[:], src[:], mv[q][:, 0:1], rstd[q][:], ALU.subtract, ALU.mult, [src, mv[q], rstd[q]], [src])
        k.tt("pool", src[:], src[:], po[:, goff:goff + 1024], ALU.mult, [src, po], [src])
        k.tt("pool", dst[:], src[:], po[:, boff:boff + 1024], ALU.add, [src, po], [dst_buf])

    def loads(i):
        q = i % NS
        tsl = slice(i * 128, (i + 1) * 128)
        y = yt[q]; r = rt[q]
        ya, yb_ = (y0_d(i) if callable(y0_d) else (y0_d[tsl, :], y1_d[tsl, :]))
        S.dma(lambda e, s, y=y, ya=ya, yb_=yb_: (e.dma_start(out=y[:, 0:512], in_=ya).then_inc(s, 16),
                                                 e.dma_start(out=y[:, 512:1024], in_=yb_).then_inc(s, 16)),
              reads=[y_b], writes=[y], key=y, n=2)
        k.load(r[:], res_d[tsl, :], r, reads=[res_b])

    def tile(i):
        q = i % NS
        pq_ = i % 2
        tsl = slice(i * 128, (i + 1) * 128)
        y = yt[q]; r = rt[q]; o = ot[q]
        k.trs([(pT[pq_][:, c, :], y[:, c * 128:(c + 1) * 128], ident[:]) for c in range(8)], [y, ident], [pT[pq_]])
        k.cp("act", yT[q][:], pT[pq_][:], [pT[pq_]], [yT[q]])
        if res_is_x:
            ln(q, r, r, r, 0, 1024)
        for hf in range(2):
            k.mms([(pY[pq_][hf][:, :], yT[q][:, c, :], wo[:, c, hf * 512:(hf + 1) * 512], c == 0, c == 7) for c in range(8)],
                  [yT[q], wo], [pY[pq_][hf]])
            k.stt(zt[q][:, hf * 512:(hf + 1) * 512], r[:, hf * 512:(hf + 1) * 512], ALPHA, pY[pq_][hf][:, :], ALU.mult, ALU.add,
                  [r, pY[pq_][hf]], [zt[q]])
        ln(q, zt[q], o, o, 2048, 3072)
        k.store(out_d[tsl, :], o[:], o, dstbuf=out_b)
        if hT_d is not None:
            t_ = oT[q]
            for hf in range(2):
                k.trs([(pY[pq_][hf][:, c * 128:(c + 1) * 128], o[:, (4 * hf + c) * 128:(4 * hf + c + 1) * 128], identf[:]) for c in range(4)],
                      [o, identf], [pY[pq_][hf]])
                k.cp("act", t_[:, 4 * hf:4 * hf + 4, :], pY[pq_][hf][:, :].rearrange("p (c t) -> p c t", c=4), [pY[pq_][hf]], [t_])
            k.store(hT_d[:, :, tsl], t_[:], t_, dstbuf=out_b)

    for ii in range(0, min(2, ntile)):
        loads(ii)
    for i in range(0, ntile, 2):
        for ii in range(i + 2, min(i + 4, ntile)):
            loads(ii)
        S.il.run([(lambda ii=ii: tile(ii)) for ii in range(i, min(i + 2, ntile))])


def build_O(NTOK, layer, res_is_x, want_hT):
    nc = bass.Bass("TRN2", target_bir_lowering=False)
    res = nc.dram_tensor("res", [NTOK, D_MODEL], F32, kind="ExternalInput").ap()
    y0 = nc.dram_tensor("y0", [NTOK, 512], BF16, kind="ExternalInput").ap()
    y1 = nc.dram_tensor("y1", [NTOK, 512], BF16, kind="ExternalInput").ap()
    prmo = nc.dram_tensor("prmo", [128, 4096], F32, kind="ExternalInput").ap()
    wo = nc.dram_tensor("wo", [D_MODEL, D_MODEL], F32, kind="ExternalInput").ap()
    out = nc.dram_tensor("out", [NTOK, D_MODEL], F32, kind="ExternalOutput").ap()
    hT = nc.dram_tensor("hT", [128, 8, NTOK], F32, kind="ExternalOutput").ap() if want_hT else None
    with contextlib.ExitStack() as st:
        S = Sched(nc, st)
        k = K(S)
        C = build_consts(S, k)
        o_phase(nc, S, k, C, NTOK, layer, res, y0, y1, prmo, wo, out, hT, res_is_x)
        S.final_wait()
        S.emit()
    return nc


def wout_perm():
    rows = []
    for g in range(2):
        for m in range(4):
            rows.append(np.arange(m * 256 + g * 128, m * 256 + (g + 1) * 128))
    return np.concatenate(rows)


def o_params(emb_g, emb_b, ln_g, ln_b):
    bc = lambda v: np.broadcast_to(np.asarray(v, np.float32)[None, :], (128, 1024))
    return np.ascontiguousarray(np.concatenate([bc(emb_g), bc(emb_b), bc(ln_g), bc(ln_b)], axis=1))


_NC_CACHE = {}
PAIRS = [[0, 1], [2, 3], [4, 5], [6, 7]]


def build_fused(T):
    nc = bass.Bass("TRN2", target_bir_lowering=False)
    src = nc.dram_tensor("src", [T, D_MODEL], F32, kind="ExternalInput").ap()
    ins = []
    for l in range(DEPTH):
        ins.append(dict(
            prm=nc.dram_tensor("prm%d" % l, [128, NPRM], F32, kind="ExternalInput").ap(),
            wf=nc.dram_tensor("wf%d" % l, [D_MODEL, NF], F32, kind="ExternalInput").ap(),
            wt=nc.dram_tensor("wt%d" % l, [D_MODEL, NT], F32, kind="ExternalInput").ap(),
            wr=nc.dram_tensor("wr%d" % l, [D_MODEL, NR], F32, kind="ExternalInput").ap(),
            prmo=nc.dram_tensor("prmo%d" % l, [128, 4096], F32, kind="ExternalInput").ap(),
            wo=nc.dram_tensor("wo%d" % l, [D_MODEL, D_MODEL], F32, kind="ExternalInput").ap()))
    out = nc.dram_tensor("out", [T, D_MODEL], F32, kind="ExternalOutput").ap()
    CH = 1024
    NCH = max(T // CH, 1)
    CH = T // NCH
    yc_mine = [[nc.dram_tensor("yc_mine%d_%d" % (l, c), [CH, 512], BF16) for c in range(NCH)] for l in range(DEPTH)]
    yc_pair = [[nc.dram_tensor("yc_pair%d_%d" % (l, c), [2 * CH, 512], BF16) for c in range(NCH)] for l in range(DEPTH)]
    TPC = CH // 128
    h1 = nc.dram_tensor("h1", [T, D_MODEL], F32)
    h1T = nc.dram_tensor("h1T", [128, 8, T], F32)
    with contextlib.ExitStack() as top:
        S = Sched(nc, top)
        k = K(S)
        for l in range(DEPTH):
            with contextlib.ExitStack() as st:
                S.stack = st
                S.prefix = "M%d_" % l
                C = build_consts(S, k)
                tile_bufs = [Buf("ycd%d_%d" % (l, i), None) for i in range(T // 128)]

                def ycat_dst(i, l=l, tile_bufs=tile_bufs):
                    return yc_mine[l][i // TPC].ap()[(i % TPC) * 128:(i % TPC + 1) * 128, :], tile_bufs[i]

                def after_store(J, l=l, tile_bufs=tile_bufs):
                    last = 4 * J + 3
                    if (last + 1) % TPC != 0:
                        return
                    c = last // TPC

                    def cc(e, s):
                        e.collective_compute("AllGather", ALU.bypass, replica_groups=PAIRS,
                                             ins=[yc_mine[l][c].ap()], outs=[yc_pair[l][c].ap()]).then_inc(s, 1)
                    S.dma(cc, reads=tile_bufs[c * TPC:(c + 1) * TPC], key="cc", queue="pool", inc=1)
                m_phase(nc, S, k, C, T, l, src if l == 0 else h1T.ap(), ins[l]["prm"], ins[l]["wf"], ins[l]["wt"], ins[l]["wr"],
                        ycat_dst, l == 0, after_store=after_store)
            S.barrier()
            with contextlib.ExitStack() as st:
                S.stack = st
                S.prefix = "O%d_" % l
                C = build_consts(S, k)
                def ysrc(i, l=l):
                    yp = yc_pair[l][i // TPC].ap()
                    r0 = (i % TPC) * 128
                    return yp[r0:r0 + 128, :], yp[CH + r0:CH + r0 + 128, :]
                o_phase(nc, S, k, C, T, l, src if l == 0 else h1.ap(), ysrc, None, ins[l]["prmo"], ins[l]["wo"],
                        h1.ap() if l < DEPTH - 1 else out, h1T.ap() if l < DEPTH - 1 else None, l == 0)
            S.barrier()
        S.final_wait()
        S.emit()
    return nc


def kernel(x, emb_ln_g, emb_ln_b, w_in, ssd_conv_w, ssd_conv_b, ssd_dt_bias, ssd_a_log, ssd_d, ssd_norm_w,
           hgrn_lb_logits, hgrn_norm_w, w_out, ln_g, ln_b):
    inp = dict(x=x, emb_ln_g=emb_ln_g, emb_ln_b=emb_ln_b, w_in=w_in, ssd_conv_w=ssd_conv_w, ssd_conv_b=ssd_conv_b,
               ssd_dt_bias=ssd_dt_bias, ssd_a_log=ssd_a_log, ssd_d=ssd_d, ssd_norm_w=ssd_norm_w,
               hgrn_lb_logits=hgrn_lb_logits, hgrn_norm_w=hgrn_norm_w, w_out=w_out, ln_g=ln_g, ln_b=ln_b)
    inp = {k_: np.asarray(v, np.float32) for k_, v in inp.items()}
    B, T, D = inp["x"].shape
    assert B == 4 and D == D_MODEL
    cores = list(range(8))
    perm = wout_perm()
    if ("F", T) not in _NC_CACHE:
        _NC_CACHE[("F", T)] = build_fused(T)
    nc = _NC_CACHE[("F", T)]
    shared = {}
    for l in range(DEPTH):
        shared["prmo%d" % l] = o_params(inp["emb_ln_g"], inp["emb_ln_b"], inp["ln_g"][l], inp["ln_b"][l])
        shared["wo%d" % l] = np.ascontiguousarray(inp["w_out"][l][perm])
    percore_g = []
    for g in range(2):
        d = {}
        for l in range(DEPTH):
            wf, wt, wr = core_weights(inp["w_in"][l], g)
            d["wf%d" % l], d["wt%d" % l], d["wr%d" % l] = wf, wt, wr
            d["prm%d" % l] = core_params(inp, l, g, inp["emb_ln_g"], inp["emb_ln_b"])
        percore_g.append(d)
    maps = []
    for c in cores:
        m = {"src": np.ascontiguousarray(inp["x"][c // 2])}
        m.update(shared)
        m.update(percore_g[c % 2])
        maps.append(m)
    r = run_bass_kernel_spmd(nc, maps, core_ids=cores)
    out = np.stack([r.results[2 * b]["out"] for b in range(B)])
    return np.ascontiguousarray(out.astype(np.float32))
```
